# Optimizing a Trainium2 kernel written in Bass

```python
import jax, jax.numpy as jnp
from jax import lax
import numpy as np

D_MODEL = 1024
BATCH = 8
SEQ = 2048
DEPTH = 4

ATTN_HEADS = 8
ATTN_KV_HEADS = 2
HEAD_DIM = 64
WINDOW = 128
BLOCK = 128
DN_HEADS = 8
DN_DK = 64
DN_DV = 64
DN_CONV = 4
CHUNK = 64
CONV_DIM = D_MODEL
CONV_WIDTH = 31
D_FF = 2816
EPS = 1e-6

Q_A = ATTN_HEADS * HEAD_DIM
KV_A = ATTN_KV_HEADS * HEAD_DIM
QK_B = DN_HEADS * DN_DK
V_B = DN_HEADS * DN_DV
QKV_B = 2 * QK_B + V_B
IN_SPLIT_SIZES = (Q_A, KV_A, KV_A, QKV_B, V_B, DN_HEADS, DN_HEADS)
IN_COLS = sum(IN_SPLIT_SIZES)
MIX_WIDTH = Q_A + V_B
N_EVEN = (DEPTH + 1) // 2
N_ODD = DEPTH // 2

kernel_name = "hybrid_swa_deltanet_conformer_macaron"


def rmsnorm(x, w):
    xf = x.astype(jnp.float32)
    y = xf * lax.rsqrt(jnp.mean(xf * xf, axis=-1, keepdims=True) + EPS)
    return (y * w.astype(jnp.float32)).astype(x.dtype)


def layernorm(x, w, b):
    xf = x.astype(jnp.float32)
    mu = jnp.mean(xf, axis=-1, keepdims=True)
    xc = xf - mu
    y = xc * lax.rsqrt(jnp.mean(xc * xc, axis=-1, keepdims=True) + EPS)
    return (y * w.astype(jnp.float32) + b.astype(jnp.float32)).astype(x.dtype)


def l2norm(x):
    xf = x.astype(jnp.float32)
    return xf * lax.rsqrt(jnp.sum(xf * xf, axis=-1, keepdims=True) + EPS)


def causal_depthwise_conv(x, w):
    k_width = w.shape[0]
    return lax.conv_general_dilated(
        x, w[:, None, :].astype(x.dtype), window_strides=(1,), padding=[(k_width - 1, 0)],
        dimension_numbers=('NWC', 'WIO', 'NWC'), feature_group_count=x.shape[-1])


def swiglu(x, w_gate, w_up, w_down):
    return (jax.nn.silu(x @ w_gate) * (x @ w_up)) @ w_down


def alibi_slopes(n_heads):
    return jnp.asarray(2.0 ** (-8.0 * np.arange(1, n_heads + 1) / n_heads), dtype=jnp.float32)


def sliding_window_attention(q, k, v, sinks):
    B, T, Hq, d = q.shape
    Hkv = k.shape[2]
    G = Hq // Hkv
    N = T // BLOCK
    qb = q.reshape(B, N, BLOCK, Hkv, G, d)

    def with_prev(t):
        tb = t.reshape(B, N, BLOCK, Hkv, d)
        prev = jnp.pad(tb, ((0, 0), (1, 0), (0, 0), (0, 0), (0, 0)))[:, :-1]
        return jnp.concatenate([prev, tb], axis=2)

    kb, vb = with_prev(k), with_prev(v)
    s = jnp.einsum('bnikgd,bnjkd->bkgnij', qb, kb).astype(jnp.float32) * (d ** -0.5)
    i = jnp.arange(BLOCK)[:, None]
    j = jnp.arange(2 * BLOCK)[None, :]
    dist = i + BLOCK - j
    blk = jnp.arange(N)[:, None, None]
    valid = (dist >= 0) & (dist < WINDOW) & ((blk > 0) | (j >= BLOCK))
    slopes = alibi_slopes(Hq).reshape(Hkv, G)[:, :, None, None, None]
    s = s - slopes * dist.astype(jnp.float32)
    s = jnp.where(valid, s, -1e30)
    sink = sinks.astype(jnp.float32).reshape(Hkv, G)[:, :, None, None, None]
    m = jnp.maximum(jnp.max(s, axis=-1, keepdims=True), sink)
    e = jnp.exp(s - m)
    p = e / (jnp.sum(e, axis=-1, keepdims=True) + jnp.exp(sink - m))
    o = jnp.einsum('bkgnij,bnjkd->bnikgd', p.astype(v.dtype), vb)
    return o.reshape(B, T, Hq * d)


def gated_delta_rule_chunked(q, k, v, g, beta):
    B, T, H, dk = q.shape
    dv = v.shape[-1]
    N = T // CHUNK
    f32 = jnp.float32

    def chunks(t):
        return t.astype(f32).reshape(B, N, CHUNK, H, -1).transpose(0, 3, 1, 2, 4)

    def chunks_s(t):
        return t.astype(f32).reshape(B, N, CHUNK, H).transpose(0, 3, 1, 2)

    q = chunks(q) * (dk ** -0.5)
    k, v = chunks(k), chunks(v)
    g, beta = chunks_s(g), chunks_s(beta)
    gc = jnp.cumsum(g, axis=-1)
    causal = jnp.tril(jnp.ones((CHUNK, CHUNK), dtype=bool))
    strict = jnp.tril(jnp.ones((CHUNK, CHUNK), dtype=bool), -1)
    diff = gc[..., :, None] - gc[..., None, :]
    decay = jnp.where(causal, jnp.exp(jnp.where(causal, diff, 0.0)), 0.0)
    kb = k * beta[..., None]
    low = jnp.where(strict, jnp.einsum('bhnid,bhnjd->bhnij', kb, k) * decay, 0.0)
    rhs = jnp.concatenate([v * beta[..., None], kb * jnp.exp(gc)[..., None]], axis=-1)
    sol = lax.linalg.triangular_solve(low, rhs, left_side=True, lower=True, unit_diagonal=True)
    u, w = sol[..., :dv], sol[..., dv:]
    attn = jnp.einsum('bhnid,bhnjd->bhnij', q, k) * decay
    q_dec = q * jnp.exp(gc)[..., None]
    k_dec = k * jnp.exp(gc[..., -1:] - gc)[..., None]
    g_last = jnp.exp(gc[..., -1])

    def step(S, xs):
        u_n, w_n, attn_n, qd_n, kd_n, gl_n = xs
        v_new = u_n - jnp.einsum('bhcd,bhde->bhce', w_n, S)
        o_n = jnp.einsum('bhcd,bhde->bhce', qd_n, S) + jnp.einsum('bhij,bhje->bhie', attn_n, v_new)
        S = S * gl_n[..., None, None] + jnp.einsum('bhcd,bhce->bhde', kd_n, v_new)
        return S, o_n

    xs = tuple(jnp.moveaxis(t, 2, 0) for t in (u, w, attn, q_dec, k_dec, g_last))
    S0 = jnp.zeros((B, H, dk, dv), f32)
    _, o = lax.scan(step, S0, xs)
    return o.transpose(1, 0, 3, 2, 4).reshape(B, T, H, dv)


def attn_deltanet_mixer(h, w_in, dn_conv_w, attn_sinks, dn_a_log, dn_dt_bias, dn_norm_w, w_out):
    B, T, _ = h.shape
    proj = h @ w_in
    split_idx = list(np.cumsum(IN_SPLIT_SIZES)[:-1])
    qa, ka, va, qkv_b, z, b_raw, a_raw = jnp.split(proj, split_idx, axis=-1)
    att = sliding_window_attention(qa.reshape(B, T, ATTN_HEADS, HEAD_DIM),
                                   ka.reshape(B, T, ATTN_KV_HEADS, HEAD_DIM),
                                   va.reshape(B, T, ATTN_KV_HEADS, HEAD_DIM), attn_sinks)
    qkv_b = jax.nn.silu(causal_depthwise_conv(qkv_b, dn_conv_w))
    qb, kb, vb = jnp.split(qkv_b, [QK_B, 2 * QK_B], axis=-1)
    qb = l2norm(qb.reshape(B, T, DN_HEADS, DN_DK))
    kb = l2norm(kb.reshape(B, T, DN_HEADS, DN_DK))
    vb = vb.reshape(B, T, DN_HEADS, DN_DV)
    beta = jax.nn.sigmoid(b_raw.astype(jnp.float32))
    g = -jnp.exp(dn_a_log.astype(jnp.float32)) * jax.nn.softplus(
        a_raw.astype(jnp.float32) + dn_dt_bias.astype(jnp.float32))
    o = gated_delta_rule_chunked(qb, kb, vb, g, beta)
    o = rmsnorm(o, dn_norm_w) * jax.nn.silu(z.reshape(B, T, DN_HEADS, DN_DV).astype(jnp.float32))
    mix = jnp.concatenate([att, o.reshape(B, T, V_B).astype(h.dtype)], axis=-1)
    return mix @ w_out


def conformer_conv_module(h, w_pw1, b_pw1, w_dw, b_dw, ln_w, ln_b, w_pw2, b_pw2):
    u = h @ w_pw1 + b_pw1
    u = u[..., :CONV_DIM] * jax.nn.sigmoid(u[..., CONV_DIM:])
    u = causal_depthwise_conv(u, w_dw) + b_dw
    u = jax.nn.silu(layernorm(u, ln_w, ln_b))
    return u @ w_pw2 + b_pw2


def setup_inputs(seed: int = 0) -> dict:
    key = jax.random.key(seed)
    ks = jax.random.split(key, 24)
    f32 = jnp.float32

    def nrm(k, shape, scale):
        return jax.random.normal(k, shape, f32) * scale

    dt = jnp.exp(jax.random.uniform(ks[10], (N_EVEN, DN_HEADS), f32,
                                    np.log(1e-3), np.log(1e-1)))
    return {
        "x": nrm(ks[0], (BATCH, SEQ, D_MODEL), 1.0),
        "norm_w": 1.0 + nrm(ks[1], (DEPTH, 3, D_MODEL), 0.02),
        "ffn_w_gate": nrm(ks[2], (DEPTH, 2, D_MODEL, D_FF), D_MODEL ** -0.5),
        "ffn_w_up": nrm(ks[3], (DEPTH, 2, D_MODEL, D_FF), D_MODEL ** -0.5),
        "ffn_w_down": nrm(ks[4], (DEPTH, 2, D_FF, D_MODEL), D_FF ** -0.5),
        "mix_w_in": nrm(ks[5], (N_EVEN, D_MODEL, IN_COLS), D_MODEL ** -0.5),
        "dn_conv_w": nrm(ks[6], (N_EVEN, DN_CONV, QKV_B), DN_CONV ** -0.5),
        "attn_sinks": nrm(ks[7], (N_EVEN, ATTN_HEADS), 0.5),
        "dn_a_log": jnp.log(jax.random.uniform(ks[8], (N_EVEN, DN_HEADS), f32, 1.0, 16.0)),
        "dn_dt_bias": dt + jnp.log(-jnp.expm1(-dt)),
        "dn_norm_w": 1.0 + nrm(ks[9], (N_EVEN, DN_DV), 0.02),
        "mix_w_out": nrm(ks[11], (N_EVEN, MIX_WIDTH, D_MODEL), MIX_WIDTH ** -0.5),
        "conv_w_pw1": nrm(ks[12], (N_ODD, D_MODEL, 2 * CONV_DIM), D_MODEL ** -0.5),
        "conv_b_pw1": nrm(ks[13], (N_ODD, 2 * CONV_DIM), 0.02),
        "conv_w_dw": nrm(ks[14], (N_ODD, CONV_WIDTH, CONV_DIM), CONV_WIDTH ** -0.5),
        "conv_b_dw": nrm(ks[15], (N_ODD, CONV_DIM), 0.02),
        "conv_ln_w": 1.0 + nrm(ks[16], (N_ODD, CONV_DIM), 0.02),
        "conv_ln_b": nrm(ks[17], (N_ODD, CONV_DIM), 0.02),
        "conv_w_pw2": nrm(ks[18], (N_ODD, CONV_DIM, D_MODEL), CONV_DIM ** -0.5),
        "conv_b_pw2": nrm(ks[19], (N_ODD, D_MODEL), 0.02),
        "final_norm_w": 1.0 + nrm(ks[20], (D_MODEL,), 0.02),
    }


def reference(x, norm_w, ffn_w_gate, ffn_w_up, ffn_w_down, mix_w_in, dn_conv_w, attn_sinks,
              dn_a_log, dn_dt_bias, dn_norm_w, mix_w_out, conv_w_pw1, conv_b_pw1, conv_w_dw,
              conv_b_dw, conv_ln_w, conv_ln_b, conv_w_pw2, conv_b_pw2, final_norm_w):
    for layer in range(DEPTH):
        x = x + 0.5 * swiglu(rmsnorm(x, norm_w[layer, 0]),
                             ffn_w_gate[layer, 0], ffn_w_up[layer, 0], ffn_w_down[layer, 0])
        h = rmsnorm(x, norm_w[layer, 1])
        if layer % 2 == 0:
            e = layer // 2
            x = x + attn_deltanet_mixer(h, mix_w_in[e], dn_conv_w[e], attn_sinks[e], dn_a_log[e],
                                        dn_dt_bias[e], dn_norm_w[e], mix_w_out[e])
        else:
            c = layer // 2
            x = x + conformer_conv_module(h, conv_w_pw1[c], conv_b_pw1[c], conv_w_dw[c], conv_b_dw[c],
                                          conv_ln_w[c], conv_ln_b[c], conv_w_pw2[c], conv_b_pw2[c])
        x = x + 0.5 * swiglu(rmsnorm(x, norm_w[layer, 2]),
                             ffn_w_gate[layer, 1], ffn_w_up[layer, 1], ffn_w_down[layer, 1])
    return rmsnorm(x, final_norm_w)
```

```python
import os
import numpy as np
from contextlib import ExitStack
import concourse.bass as bass
import concourse.mybir as mybir
from concourse.bass_utils import run_bass_kernel_spmd

F32 = mybir.dt.float32
BF16 = mybir.dt.bfloat16
AF = mybir.ActivationFunctionType
ALU = mybir.AluOpType

D = 1024
T = 2048
DFF = 2816
NCH = D // 128
NTB = T // 512
NMC = DFF // 128
GROUPS = [(0, 8), (8, 8), (16, 6)]
EPS = 1e-6
RING = 8
PAD = 32
CPW = 296
ARENA = NCH * T + 8 * (PAD + T)
EPW = 133
CSTW = 1344
GT = 256
NCK = GT // 64
SLOPES = [2.0 ** (-8.0 * (h + 1) / 8) for h in range(8)]
EVEN_PARTS = ("att", "dn")


class Sem:
    def __init__(self, h):
        self.h = h
        self.count = 0


class Buf:
    __slots__ = ("w", "r")

    def __init__(self):
        self.w = None
        self.r = {}


class Eng:
    def __init__(self, name):
        self.name = name
        self.ops = []
        self.sem = None
        self.seen = {}
        self.fence = None


class KB:
    def __init__(self, nc, st):
        self.nc = nc
        self.st = st
        self.nsem = 0
        self.E = {}
        for n in ("pe", "act", "dve", "pool", "sp"):
            e = Eng(n)
            e.sem = self.newsem(n)
            self.E[n] = e
        self.dsems = []

    def newsem(self, name):
        self.nsem += 1
        return Sem(self.st.enter_context(self.nc.semaphore(f"s_{name}_{self.nsem}")))

    def sb(self, name, shape, dt):
        return self.st.enter_context(self.nc.sbuf_tensor(name, shape, dt))

    def ps(self, name, shape, dt=F32):
        return self.st.enter_context(self.nc.psum_tensor(name, shape, dt))

    def op(self, en, reads, writes, emit, dsem=None):
        eng = self.E[en]
        need = {}

        def add(tok):
            if tok is None:
                return
            s, v = tok
            if need.get(s, 0) < v:
                need[s] = v

        if eng.fence is not None:
            for tok in eng.fence:
                add(tok)
            eng.fence = None
        for b in reads:
            add(b.w)
        for b in writes:
            add(b.w)
            for s, v in b.r.items():
                add((s, v))
        waits = []
        for s, v in need.items():
            if en == "pe" and s is eng.sem:
                continue
            if eng.seen.get(s, 0) < v:
                eng.seen[s] = v
                waits.append((s, v))
        if dsem is None:
            sem = eng.sem
            sem.count += 1
            inc = 1
        else:
            sem = dsem
            sem.count += 16
            inc = 16
        tok = (sem, sem.count)
        eng.ops.append((waits, emit, sem, inc))
        for b in reads:
            if b.r.get(sem, 0) < sem.count:
                b.r[sem] = sem.count
        for b in writes:
            b.w = tok
            b.r = {}
        return tok

    def fence(self, new_epoch=False):
        toks = [(e.sem, e.sem.count) for e in self.E.values() if e.sem.count > 0]
        toks += [(s, s.count) for s in self.dsems if s.count > 0]
        for e in self.E.values():
            e.fence = list(toks)
        if new_epoch:
            for n in ("pe", "act", "dve"):
                self.E[n].sem = self.newsem(n)

    def replay(self, en, e):
        for waits, emit, sem, inc in self.E[en].ops:
            for s, v in waits:
                e.wait_ge(s.h, v)
            ins = emit(e)
            if ins is not None:
                ins.then_inc(sem.h, inc)


def _tile(a):
    t = np.zeros((128, 1024), np.float32)
    t[:, : a.shape[1]] = a
    return t


def pack_ffn(wg, wu, wd):
    tiles = []
    for (m0, nm) in GROUPS:
        for mi in range(nm):
            m = m0 + mi
            for w in (wg, wu):
                blk = w[:, m * 128:(m + 1) * 128].reshape(NCH, 128, 128)
                tiles.append(_tile(blk.transpose(1, 0, 2).reshape(128, NCH * 128)))
        for c in range(NCH):
            blk = wd[m0 * 128:(m0 + nm) * 128, c * 128:(c + 1) * 128].reshape(nm, 128, 128)
            tiles.append(_tile(blk.transpose(1, 0, 2).reshape(128, nm * 128)))
    return tiles


class Prog:
    def __init__(self, stages, n_wtiles):
        self.stages = stages
        nc = bass.Bass("TRN2", target_bir_lowering=False)
        self.nc = nc
        self.st = ExitStack()
        st = self.st
        self.xin = nc.dram_tensor("xin", [D, T], F32, kind="ExternalInput").ap()
        self.wst = nc.dram_tensor("wst", [max(n_wtiles, 1), 128, 1024], F32, kind="ExternalInput").ap()
        self.nw_d = nc.dram_tensor("nw", [128, 13 * NCH], F32, kind="ExternalInput").ap()
        self.cp_d = nc.dram_tensor("cp", [128, 2 * CPW], F32, kind="ExternalInput").ap()
        self.ident_d = nc.dram_tensor("ident", [128, 128], F32, kind="ExternalInput").ap()
        self.xout = nc.dram_tensor("xout", [D, T], F32, kind="ExternalOutput").ap()
        kb = KB(nc, st)
        self.kb = kb
        self.x = kb.sb("x_sb", [128, NCH, T], F32)
        self.xb = [[Buf() for _ in range(NTB)] for _ in range(NCH)]
        self.arena = kb.sb("arena_sb", [128, ARENA], BF16)
        self.xn = self.arena[:, 0:NCH * T].rearrange("p (c t) -> p c t", c=NCH)
        self.xnb = [[Buf() for _ in range(NTB)] for _ in range(NCH)]
        self.nw = kb.sb("nw_sb", [128, 13 * NCH], F32)
        self.nwb = Buf()
        self.ones = kb.sb("ones_sb", [128, 128], BF16)
        self.onesb = Buf()
        self.ring = kb.sb("wring", [128, RING, 1024], BF16)
        self.ringb = [Buf() for _ in range(RING)]
        self.ringsem = [kb.newsem("ring") for _ in range(RING)]
        kb.dsems += self.ringsem
        self.wi = 0
        self.act = self.arena[:, NCH * T:NCH * T + 8 * (PAD + T)].rearrange("p (c t) -> p c t", c=8)
        self.s2 = kb.sb("s2_sb", [128, NCH * T], BF16)
        self.mix = self.s2[:, :].rearrange("p (c t) -> p c t", c=NCH)
        self.mixb = [[Buf() for _ in range(16)] for _ in range(NCH)]
        self.ep_d = nc.dram_tensor("ep", [128, 2 * EPW], F32, kind="ExternalInput").ap()
        self.cst_d = nc.dram_tensor("cst", [128, CSTW], F32, kind="ExternalInput").ap()
        self.ep = kb.sb("ep_sb", [128, 2 * EPW], F32)
        self.cst = kb.sb("cst_sb", [128, CSTW], F32)
        self.epb = Buf()
        self.cstb = Buf()
        self.actb = [[Buf() for _ in range(NTB)] for _ in range(8)]
        self.sq = kb.sb("sq_sb", [128, 4, 512], BF16)
        self.sqb = [Buf() for _ in range(4)]
        self.sqi = 0
        self.rr = kb.sb("r_sb", [128, 512], F32)
        self.rrb = Buf()
        self.rt = kb.sb("rt_sb", [128, 512], F32)
        self.rtb = Buf()
        self.epsc = kb.sb("eps_sb", [128, 1], F32)
        self.epsb = Buf()
        self.sg = kb.sb("sg_sb", [128, 2, 512], F32)
        self.sgb = [Buf() for _ in range(2)]
        self.cp = kb.sb("cp_sb", [128, 2 * CPW], F32)
        self.cpb = Buf()
        self.ident = kb.sb("ident_sb", [128, 128], BF16)
        self.identf = kb.sb("identf_sb", [128, 128], F32)
        self.identb = Buf()
        self.diag = self.s2[:, 0:2 * 31 * 128].rearrange("p (a k j) -> p a k j", a=2, k=31)
        self.diagb = [Buf(), Buf()]
        self.dgi = 0
        self.tmpa = kb.sb("tmpa_sb", [128, 512], F32)
        self.tmpab = Buf()
        self.tmpb = kb.sb("tmpb_sb", [128, 2, 512], F32)
        self.tmpbb = [Buf(), Buf()]
        self.tbi = 0
        self.ones64 = kb.sb("ones64_sb", [128, 128], F32)
        self.sk = kb.sb("sk_sb", [128, 4], F32)
        self.skb = Buf()
        self.den = kb.sb("den_sb", [128, 128], F32)
        self.denb = Buf()
        self.sgi2 = 0
        self.pb = [kb.ps(f"psb{i}", [128, 512], F32) for i in range(8)]
        self.pbb = [Buf() for _ in range(8)]
        self.gi = 0
        self.yi = 0

    def wtile(self, ncols=1024):
        i = self.wi
        self.wi += 1
        slot = i % RING
        buf = self.ringb[slot]
        dst = self.ring[:, slot, 0:ncols]
        src = self.wst[i, :, 0:ncols]
        self.kb.op("pool", [], [buf], lambda e: e.dma_start(out=dst, in_=src), dsem=self.ringsem[slot])
        return self.ring[:, slot, :], buf

    def load(self):
        kb = self.kb
        s1 = kb.newsem("ldx")
        kb.dsems.append(s1)
        for t in range(NTB):
            st_ = kb.newsem("ldx%d" % t)
            kb.dsems.append(st_)
            for c in range(NCH):
                dst = self.x[:, c, t * 512:(t + 1) * 512]
                src = self.xin[c * 128:(c + 1) * 128, t * 512:(t + 1) * 512]
                kb.op("sp", [], [self.xb[c][t]], lambda e, dst=dst, src=src: e.dma_start(out=dst, in_=src), dsem=st_)
            for c in range(NCH):
                self.xb[c][t].w = (st_, st_.count)
        kb.op("sp", [], [self.nwb], lambda e: e.dma_start(out=self.nw[:, :], in_=self.nw_d), dsem=s1)
        kb.op("sp", [], [self.cpb], lambda e: e.dma_start(out=self.cp[:, :], in_=self.cp_d), dsem=s1)
        kb.op("sp", [], [self.epb], lambda e: e.dma_start(out=self.ep[:, :], in_=self.ep_d), dsem=s1)
        kb.op("sp", [], [self.cstb], lambda e: e.dma_start(out=self.cst[:, :], in_=self.cst_d), dsem=s1)
        kb.op("sp", [], [self.identb], lambda e: e.dma_start(out=self.identf[:, :], in_=self.ident_d), dsem=s1)
        fin = (s1, s1.count)
        self.nwb.w = fin
        self.cpb.w = fin
        self.epb.w = fin
        self.cstb.w = fin
        self.identb.w = fin
        kb.op("dve", [self.identb], [self.identb], lambda e: e.tensor_copy(out=self.ident[:, :], in_=self.identf[:, :]))
        kb.op("dve", [], [self.onesb], lambda e: e.memset(self.ones[:, :], 1.0 / D))
        kb.op("dve", [], [self.epsb], lambda e: e.memset(self.epsc[:, :], EPS))

    def store(self):
        kb = self.kb
        s1 = kb.newsem("stx")
        kb.dsems.append(s1)
        for t in range(NTB):
            for c in range(NCH):
                src = self.x[:, c, t * 512:(t + 1) * 512]
                dst = self.xout[c * 128:(c + 1) * 128, t * 512:(t + 1) * 512]
                kb.op("sp", [self.xb[c][t]], [], lambda e, dst=dst, src=src: e.dma_start(out=dst, in_=src), dsem=s1)
        fin = Buf()
        fin.w = (s1, s1.count)
        kb.op("sp", [fin], [], lambda e: None)

    def rmsnorm(self, nidx, final=False):
        kb = self.kb
        stat = self.pb[6]
        statb = self.pbb[6]
        for t in range(NTB):
            ts = slice(t * 512, (t + 1) * 512)
            for c in range(NCH):
                k = self.sqi % 4
                self.sqi += 1
                sqv = self.sq[:, k, :]
                xv = self.x[:, c, ts]
                kb.op("act", [self.xb[c][t]], [self.sqb[k]],
                      lambda e, sqv=sqv, xv=xv: e.activation(out=sqv, in_=xv, func=AF.Square))
                kb.op("pe", [self.sqb[k], self.onesb], [statb],
                      lambda e, sqv=sqv, c=c: e.matmul(out=stat[:, :], lhsT=self.ones[:, :], rhs=sqv,
                                                       start=(c == 0), stop=(c == NCH - 1)))
            kb.op("act", [statb, self.epsb], [self.rtb],
                  lambda e: e.activation(out=self.rt[:, :], in_=stat[:, :], func=AF.Ln, bias=self.epsc[:, 0:1]))
            kb.op("act", [self.rtb], [self.rrb],
                  lambda e: e.activation(out=self.rr[:, :], in_=self.rt[:, :], func=AF.Exp, scale=-0.5))
            for c in range(NCH):
                xv = self.x[:, c, ts]
                wv = self.nw[:, nidx * NCH + c: nidx * NCH + c + 1]
                eng_x = "dve"
                if final:
                    kb.op("dve", [self.rrb, self.nwb], [self.xb[c][t]],
                          lambda e, xv=xv, wv=wv: e.scalar_tensor_tensor(
                              out=xv, in0=xv, scalar=wv, in1=self.rr[:, :], op0=ALU.mult, op1=ALU.mult))
                else:
                    ov = self.xn[:, c, ts]
                    kb.op(eng_x, [self.xb[c][t], self.rrb, self.nwb], [self.xnb[c][t]],
                          lambda e, xv=xv, wv=wv, ov=ov: e.scalar_tensor_tensor(
                              out=ov, in0=xv, scalar=wv, in1=self.rr[:, :], op0=ALU.mult, op1=ALU.mult))

    def ffn(self, nidx):
        kb = self.kb
        self.rmsnorm(nidx)
        for (m0, nm) in GROUPS:
            for mi in range(nm):
                wg, wgb = self.wtile()
                wu, wub = self.wtile()
                for t in range(NTB):
                    ts = slice(t * 512, (t + 1) * 512)
                    j = self.gi % 2
                    self.gi += 1
                    pg, pgb = self.pb[2 * j], self.pbb[2 * j]
                    pu, pub = self.pb[2 * j + 1], self.pbb[2 * j + 1]
                    xr = [self.xnb[c][t] for c in range(NCH)]

                    def mm(e, w=wg, p=pg, ts=ts):
                        for k in range(NCH):
                            ins = e.matmul(out=p[:, :], lhsT=w[:, k * 128:(k + 1) * 128], rhs=self.xn[:, k, ts],
                                           start=(k == 0), stop=(k == NCH - 1))
                        return ins
                    kb.op("pe", [wgb] + xr, [pgb], mm)
                    kb.op("pe", [wub] + xr, [pub], lambda e, mm=mm, wu=wu, pu=pu, ts=ts: mm(e, wu, pu, ts))
                    sgv = self.sg[:, j, :]
                    kb.op("act", [pgb], [self.sgb[j]],
                          lambda e, sgv=sgv, pg=pg: e.activation(out=sgv, in_=pg[:, :], func=AF.Silu))
                    av = self.act[:, mi, PAD + t * 512: PAD + (t + 1) * 512]
                    kb.op("dve", [self.sgb[j], pub], [self.actb[mi][t]],
                          lambda e, av=av, sgv=sgv, pu=pu: e.tensor_tensor(out=av, in0=pu[:, :], in1=sgv, op=ALU.mult))
            for c in range(NCH):
                wd, wdb = self.wtile(nm * 128)
                for t in range(NTB):
                    ts = slice(t * 512, (t + 1) * 512)
                    j = self.yi % 2
                    self.yi += 1
                    py, pyb = self.pb[4 + j], self.pbb[4 + j]

                    def mmd(e, wd=wd, py=py, ts=ts, nm=nm):
                        for mi in range(nm):
                            ins = e.matmul(out=py[:, :], lhsT=wd[:, mi * 128:(mi + 1) * 128], rhs=self.act[:, mi, PAD + ts.start: PAD + ts.stop],
                                           start=(mi == 0), stop=(mi == nm - 1))
                        return ins
                    kb.op("pe", [wdb] + [self.actb[mi][t] for mi in range(nm)], [pyb], mmd)
                    xv = self.x[:, c, ts]
                    kb.op("dve", [pyb], [self.xb[c][t]],
                          lambda e, xv=xv, py=py: e.scalar_tensor_tensor(
                              out=xv, in0=py[:, :], scalar=0.5, in1=xv, op0=ALU.mult, op1=ALU.add))

    def convmix(self, layer):
        kb = self.kb
        ci = layer // 2
        self.rmsnorm(layer * 3 + 1)
        o = ci * CPW
        b1a = lambda j: self.cp[:, o + j: o + j + 1]
        b1g = lambda j: self.cp[:, o + 8 + j: o + 8 + j + 1]
        wdw = self.cp[:, o + 16: o + 16 + 248]
        bdw = lambda c: self.cp[:, o + 264 + c: o + 264 + c + 1]
        lnw = lambda c: self.cp[:, o + 272 + c: o + 272 + c + 1]
        lnb = lambda c: self.cp[:, o + 280 + c: o + 280 + c + 1]
        b2 = lambda c: self.cp[:, o + 288 + c: o + 288 + c + 1]
        padb = [self.actb[c][0] for c in range(8)]
        kb.op("dve", [], padb, lambda e: e.memset(self.act[:, :, 0:PAD], 0.0))
        for j in range(8):
            wa, wab = self.wtile()
            wg, wgb = self.wtile()
            for t in range(NTB):
                ts = slice(t * 512, (t + 1) * 512)
                q = self.gi % 2
                self.gi += 1
                pa, pab = self.pb[2 * q], self.pbb[2 * q]
                pg, pgb = self.pb[2 * q + 1], self.pbb[2 * q + 1]
                xr = [self.xnb[c][t] for c in range(NCH)]

                def mm(e, w=wa, p=pa, ts=ts):
                    for k in range(NCH):
                        ins = e.matmul(out=p[:, :], lhsT=w[:, k * 128:(k + 1) * 128], rhs=self.xn[:, k, ts],
                                       start=(k == 0), stop=(k == NCH - 1))
                    return ins
                kb.op("pe", [wab] + xr, [pab], mm)
                kb.op("pe", [wgb] + xr, [pgb], lambda e, mm=mm, wg=wg, pg=pg, ts=ts: mm(e, wg, pg, ts))
                sgv = self.sg[:, q, :]
                kb.op("act", [pgb, self.cpb], [self.sgb[q]],
                      lambda e, sgv=sgv, pg=pg, j=j: e.activation(out=sgv, in_=pg[:, :], func=AF.Sigmoid, bias=b1g(j)))
                av = self.act[:, j, PAD + t * 512: PAD + (t + 1) * 512]
                kb.op("dve", [self.sgb[q], pab, self.cpb], [self.actb[j][t]],
                      lambda e, av=av, sgv=sgv, pa=pa, j=j: e.scalar_tensor_tensor(
                          out=av, in0=pa[:, :], scalar=b1a(j), in1=sgv, op0=ALU.add, op1=ALU.mult))
        for c in range(NCH):
            dsl = self.dgi % 2
            self.dgi += 1
            dg = self.diag[:, dsl, :, :]
            idb = self.ident[:, :].unsqueeze(1).broadcast_to([128, 31, 128])
            wb = wdw[:, c * 31:(c + 1) * 31].unsqueeze(2).broadcast_to([128, 31, 128])
            kb.op("dve", [self.identb, self.cpb], [self.diagb[dsl]],
                  lambda e, dg=dg, idb=idb, wb=wb: e.tensor_tensor(out=dg, in0=idb, in1=wb, op=ALU.mult))
            for t in range(NTB):
                ts = slice(t * 512, (t + 1) * 512)
                q = self.yi % 2
                self.yi += 1
                py, pyb = self.pb[4 + q], self.pbb[4 + q]

                def mmc(e, py=py, c=c, t=t, dsl=dsl):
                    for k in range(31):
                        st0 = PAD + t * 512 - 30 + k
                        ins = e.matmul(out=py[:, :], lhsT=self.diag[:, dsl, k, :], rhs=self.act[:, c, st0: st0 + 512],
                                       start=(k == 0), stop=(k == 30))
                    return ins
                rd = [self.diagb[dsl], self.actb[c][t]] + ([self.actb[c][t - 1]] if t > 0 else [])
                kb.op("pe", rd, [pyb], mmc)
                ov = self.xn[:, c, ts]
                kb.op("act", [pyb, self.cpb], [self.xnb[c][t]],
                      lambda e, ov=ov, py=py, c=c: e.activation(out=ov, in_=py[:, :], func=AF.Identity, bias=bdw(c)))
        msq, msqb = self.pb[6], self.pbb[6]
        mean, meanb = self.pb[7], self.pbb[7]
        for t in range(NTB):
            ts = slice(t * 512, (t + 1) * 512)
            for c in range(NCH):
                k = self.sqi % 4
                self.sqi += 1
                sqv = self.sq[:, k, :]
                cv = self.xn[:, c, ts]
                kb.op("act", [self.xnb[c][t]], [self.sqb[k]],
                      lambda e, sqv=sqv, cv=cv: e.activation(out=sqv, in_=cv, func=AF.Square))
                kb.op("pe", [self.sqb[k], self.onesb], [msqb],
                      lambda e, sqv=sqv, c=c: e.matmul(out=msq[:, :], lhsT=self.ones[:, :], rhs=sqv,
                                                       start=(c == 0), stop=(c == NCH - 1)))
                kb.op("pe", [self.xnb[c][t], self.onesb], [meanb],
                      lambda e, cv=cv, c=c: e.matmul(out=mean[:, :], lhsT=self.ones[:, :], rhs=cv,
                                                     start=(c == 0), stop=(c == NCH - 1)))
            kb.op("act", [meanb], [self.tmpab],
                  lambda e: e.activation(out=self.tmpa[:, :], in_=mean[:, :], func=AF.Square))
            kb.op("dve", [msqb, self.tmpab], [self.tmpab],
                  lambda e: e.tensor_tensor(out=self.tmpa[:, :], in0=msq[:, :], in1=self.tmpa[:, :], op=ALU.subtract))
            kb.op("act", [self.tmpab, self.epsb], [self.rtb],
                  lambda e: e.activation(out=self.rt[:, :], in_=self.tmpa[:, :], func=AF.Ln, bias=self.epsc[:, 0:1]))
            kb.op("act", [self.rtb], [self.rrb],
                  lambda e: e.activation(out=self.rr[:, :], in_=self.rt[:, :], func=AF.Exp, scale=-0.5))
            for c in range(NCH):
                k = self.tbi % 2
                self.tbi += 1
                tv = self.tmpb[:, k, :]
                cv = self.xn[:, c, ts]
                kb.op("dve", [self.xnb[c][t], meanb], [self.tmpbb[k]],
                      lambda e, tv=tv, cv=cv: e.tensor_tensor(out=tv, in0=cv, in1=mean[:, :], op=ALU.subtract))
                kb.op("dve", [self.tmpbb[k], self.rrb], [self.tmpbb[k]],
                      lambda e, tv=tv: e.tensor_tensor(out=tv, in0=tv, in1=self.rr[:, :], op=ALU.mult))
                av = self.act[:, c, PAD + t * 512: PAD + (t + 1) * 512]
                kb.op("act", [self.tmpbb[k], self.cpb], [self.actb[c][t]],
                      lambda e, av=av, tv=tv, c=c: e.activation(out=av, in_=tv, func=AF.Silu, bias=lnb(c), scale=lnw(c)))
        for c in range(NCH):
            w2, w2b = self.wtile()
            for t in range(NTB):
                ts = slice(t * 512, (t + 1) * 512)
                q = self.yi % 2
                self.yi += 1
                py, pyb = self.pb[4 + q], self.pbb[4 + q]

                def mm2(e, w2=w2, py=py, t=t):
                    for k in range(NCH):
                        ins = e.matmul(out=py[:, :], lhsT=w2[:, k * 128:(k + 1) * 128],
                                       rhs=self.act[:, k, PAD + t * 512: PAD + (t + 1) * 512],
                                       start=(k == 0), stop=(k == NCH - 1))
                    return ins
                kb.op("pe", [w2b] + [self.actb[k][t] for k in range(NCH)], [pyb], mm2)
                xv = self.x[:, c, ts]
                kb.op("dve", [pyb, self.cpb], [self.xb[c][t]],
                      lambda e, xv=xv, py=py, c=c: e.scalar_tensor_tensor(
                          out=xv, in0=py[:, :], scalar=b2(c), in1=xv, op0=ALU.add, op1=ALU.add))
        kb.fence()

    def evenmix(self, layer):
        kb = self.kb
        e_i = layer // 2
        self.rmsnorm(layer * 3 + 1)
        kb.fence()
        A = self.arena
        o0 = NCH * T
        qa = A[:, o0:o0 + 4 * T].rearrange("p (c t) -> p c t", c=4)
        ka = A[:, o0 + 4 * T:o0 + 6 * T].rearrange("p (h t) -> p h t", h=2)
        vax = A[:, o0 + 6 * T:o0 + 6 * T + 16 * 256].rearrange("p (n k j) -> p n k j", n=16, k=2)
        onx = A[:, o0 + 6 * T + 4096:o0 + 6 * T + 4096 + 256].rearrange("p (k j) -> p k j", k=2)
        qab = [[Buf() for _ in range(NTB)] for _ in range(4)]
        kab = [Buf() for _ in range(NTB)]
        vab = [Buf() for _ in range(4)]
        onxb = Buf()
        eo = e_i * EPW
        if "att" in EVEN_PARTS:
            self.attention(qa, ka, vax, onx, qab, kab, vab, onxb, eo)
        else:
            for c in range(4):
                for n in range(16):
                    kb.op("dve", [], [self.mixb[c][n]],
                          lambda e, c=c, n=n: e.memset(self.mix[:, c, n * 128:(n + 1) * 128], 0.0))
        kb.fence()
        if "dn" in EVEN_PARTS:
            self.deltanet(layer, eo)
        else:
            for c in range(4, 8):
                for n in range(16):
                    kb.op("dve", [], [self.mixb[c][n]],
                          lambda e, c=c, n=n: e.memset(self.mix[:, c, n * 128:(n + 1) * 128], 0.0))
        kb.fence()
        for c in range(NCH):
            w2, w2b = self.wtile()
            for t in range(NTB):
                ts = slice(t * 512, (t + 1) * 512)
                q = self.yi % 2
                self.yi += 1
                py, pyb = self.pb[4 + q], self.pbb[4 + q]

                def mm2(e, w2=w2, py=py, ts=ts):
                    for k in range(NCH):
                        ins = e.matmul(out=py[:, :], lhsT=w2[:, k * 128:(k + 1) * 128], rhs=self.mix[:, k, ts],
                                       start=(k == 0), stop=(k == NCH - 1))
                    return ins
                rd = [w2b] + [self.mixb[k][4 * t + i] for k in range(NCH) for i in range(4)]
                kb.op("pe", rd, [pyb], mm2)
                xv = self.x[:, c, ts]
                kb.op("dve", [pyb], [self.xb[c][t]],
                      lambda e, xv=xv, py=py: e.tensor_tensor(out=xv, in0=py[:, :], in1=xv, op=ALU.add))
        kb.fence()

    def attention(self, qa, ka, vax, onx, qab, kab, vab, onxb, eo):
        kb = self.kb
        kb.op("dve", [], vab, lambda e: e.memset(vax, 0.0))
        kb.op("dve", [], kab, lambda e: e.memset(ka, 0.0))
        kb.op("dve", [], [onxb], lambda e: e.memset(onx, 0.0))
        kb.op("dve", [onxb], [onxb], lambda e: e.memset(onx[:, 0, 0:64], 1.0))
        kb.op("dve", [onxb], [onxb], lambda e: e.memset(onx[:, 1, 64:128], 1.0))
        sk = self.sk
        kb.op("act", [self.epb], [self.skb],
              lambda e: e.activation(out=sk[:, :], in_=self.ep[:, eo:eo + 4], func=AF.Exp))
        for c in range(5):
            w, wb = self.wtile()
            for t in range(NTB):
                ts = slice(t * 512, (t + 1) * 512)
                q = self.gi % 2
                self.gi += 1
                p, pbf = self.pb[2 * q], self.pbb[2 * q]

                def mm(e, w=w, p=p, ts=ts):
                    for k in range(NCH):
                        ins = e.matmul(out=p[:, :], lhsT=w[:, k * 128:(k + 1) * 128], rhs=self.xn[:, k, ts],
                                       start=(k == 0), stop=(k == NCH - 1))
                    return ins
                kb.op("pe", [wb] + [self.xnb[k][t] for k in range(NCH)], [pbf], mm)
                if c < 4:
                    ov, ob = qa[:, c, ts], qab[c][t]
                    kb.op("act", [pbf], [ob], lambda e, ov=ov, p=p: e.activation(out=ov, in_=p[:, :], func=AF.Copy))
                else:
                    for h in range(2):
                        kb.op("act", [pbf], [kab[t]], lambda e, h=h, p=p, ts=ts: e.activation(
                            out=ka[h * 64:(h + 1) * 64, h, ts], in_=p[h * 64:(h + 1) * 64, :], func=AF.Copy))
        wv, wvb = self.wtile()
        for t in range(NTB):
            q = self.gi % 2
            self.gi += 1
            p, pbf = self.pb[2 * q], self.pbb[2 * q]

            def mmv(e, p=p, t=t):
                for i in range(4):
                    n = 4 * t + i
                    for k in range(NCH):
                        ins = e.matmul(out=p[:, i * 128:(i + 1) * 128], lhsT=self.xn[:, k, n * 128:(n + 1) * 128],
                                       rhs=wv[:, k * 128:(k + 1) * 128], start=(k == 0), stop=(k == NCH - 1))
                return ins
            kb.op("pe", [wvb] + [self.xnb[k][t] for k in range(NCH)], [pbf], mmv)
            pv = p[:, :].rearrange("p (n j) -> p n j", n=4)
            kb.op("act", [pbf], [vab[t]],
                  lambda e, pv=pv, t=t: e.activation(out=vax[:, 4 * t:4 * t + 4, 0, 0:64], in_=pv[:, :, 0:64], func=AF.Copy))
            kb.op("act", [pbf], [vab[t]],
                  lambda e, pv=pv, t=t: e.activation(out=vax[:, 4 * t:4 * t + 4, 1, 64:128], in_=pv[:, :, 64:128], func=AF.Copy))
        import os
        if os.environ.get("ATT_STOP") == "3":
            for c in range(4):
                for n in range(16):
                    kb.op("dve", [], [self.mixb[c][n]],
                          lambda e, c=c, n=n: e.memset(self.mix[:, c, n * 128:(n + 1) * 128], 0.0))
            return
        d0 = self.cst[:, 0:256]
        hb = (lambda h: 0) if os.environ.get("ATT_NOB64") else (lambda h: h)
        def att_iter(n, c, t, qs, ks, nk):
            q = self.gi % 2
            self.gi += 1
            p, pbf = self.pb[2 * q], self.pbb[2 * q]
            po, pob = self.pb[2 * q + 1], self.pbb[2 * q + 1]

            def mms(e, p=p, c=c, qs=qs, ks=ks, nk=nk):
                for h in range(2):
                    for kk in range(nk):
                        ins = e.matmul(out=p[:, h * 256 + kk * 128:h * 256 + (kk + 1) * 128],
                                       lhsT=ka[:, h, ks[kk]], rhs=qa[:, c, qs],
                                       start=True, stop=True)
                return ins
            rd = [qab[c][t], kab[t]] + ([kab[(n - 1) // 4]] if n > 0 else [])
            kb.op("pe", rd, [pbf], mms)
            w = 128 * nk
            si = self.sgi2 % 2
            self.sgi2 += 1
            tm = self.sg[:, si, :]
            tmb = self.sgb[si]
            k = self.sqi % 4
            self.sqi += 1
            pt = self.sq[:, k, :]
            ptb = self.sqb[k]
            for h in range(2):
                sl = 8.0 * SLOPES[c + 4 * h]
                kb.op("dve", [pbf, self.cstb], [tmb],
                      lambda e, h=h, sl=sl, p=p, tm=tm, w=w: e.scalar_tensor_tensor(
                          out=tm[:, h * 256:h * 256 + w], in0=p[:, h * 256:h * 256 + w], scalar=1.0 / sl, in1=d0[:, 0:w],
                          op0=ALU.mult, op1=ALU.add))
                kb.op("act", [tmb], [ptb], lambda e, h=h, sl=sl, pt=pt, tm=tm, w=w: e.activation(
                    out=pt[:, h * 256:h * 256 + w], in_=tm[:, h * 256:h * 256 + w], func=AF.Exp, scale=0.125 * sl))

            yield "A"
            if os.environ.get("ATT_STOP") == "4":
                kb.op("dve", [ptb], [self.mixb[c][n]],
                      lambda e, c=c, qs=qs: e.memset(self.mix[:, c, qs], 0.0))
                return

            def mmo(e, po=po, pt=pt, n=n, nk=nk):
                for part, (lh, off) in enumerate(((vax, 0), (onx, 128))):
                    cnt = 0
                    for h in range(2):
                        for kk in range(nk):
                            lhsT = vax[:, n - kk, h, :] if part == 0 else onx[:, h, :]
                            ins = e.matmul(out=po[:, off:off + 128], lhsT=lhsT,
                                           rhs=pt[:, h * 256 + kk * 128:h * 256 + (kk + 1) * 128],
                                           start=(cnt == 0), stop=(cnt == 2 * nk - 1))
                            cnt += 1
                return ins
            rd = [ptb, onxb, vab[t]] + ([vab[(n - 1) // 4]] if n > 0 else [])
            kb.op("pe", rd, [pob], mmo)
            if os.environ.get("ATT_STOP") == "5":
                kb.op("dve", [pob], [self.mixb[c][n]],
                      lambda e, c=c, qs=qs: e.memset(self.mix[:, c, qs], 0.0))
                return
            dn = self.den
            kb.op("dve", [pob, self.skb], [self.denb],
                  lambda e, po=po, c=c: e.tensor_scalar_add(out=dn[:, :], in0=po[:, 128:256], scalar1=sk[:, c:c + 1]))
            kb.op("dve", [self.denb], [self.denb], lambda e: e.reciprocal(out=dn[:, :], in_=dn[:, :]))
            kb.op("dve", [pob, self.denb], [self.mixb[c][n]],
                  lambda e, po=po, c=c, qs=qs: e.tensor_tensor(out=self.mix[:, c, qs], in0=po[:, 0:128], in1=dn[:, :],
                                                               op=ALU.mult))

        prev = None
        for n in range(16):
            t = n // 4
            qs = slice(n * 128, (n + 1) * 128)
            ks = [slice(n * 128, (n + 1) * 128), slice((n - 1) * 128, n * 128)]
            nk = 1 if n == 0 else 2
            for c in range(4):
                g = att_iter(n, c, t, qs, ks, nk)
                next(g, None)
                if prev is not None:
                    for _ in prev:
                        pass
                prev = g
        for _ in prev:
            pass

    def deltanet(self, layer, eo):
        kb = self.kb
        self.dbg_list = []
        A = self.arena
        pos = [0]

        def alloc(n_bf16, dt=BF16):
            a0 = pos[0]
            self.dbg_list.append((a0, n_bf16, dt == F32))
            pos[0] += (n_bf16 + 15) // 16 * 16
            assert pos[0] <= ARENA, ("arena overflow", pos[0], ARENA)
            self.arena_used = pos[0]
            v = A[:, a0:a0 + n_bf16]
            return v.bitcast(F32) if dt == F32 else v

        G = GT
        xg = alloc(8 * G).rearrange("p (c t) -> p c t", c=8)
        cin = alloc(12 * (G + 4)).rearrange("p (c t) -> p c t", c=12)
        ctmp = alloc(2 * 2 * G, F32).rearrange("p (a t) -> p a t", a=2)
        qT = alloc(4 * G).rearrange("p (c t) -> p c t", c=4)
        kT = alloc(4 * G).rearrange("p (c t) -> p c t", c=4)
        vT = alloc(4 * G).rearrange("p (c t) -> p c t", c=4)
        kTz = alloc(2 * 4 * G).rearrange("p (b c t) -> p b c t", b=2, c=4)
        k_tm = alloc(NCK * 512).rearrange("p (c f) -> p c f", c=NCK)
        v_tm = alloc(NCK * 512).rearrange("p (c f) -> p c f", c=NCK)
        zs = alloc(NCK * 512).rearrange("p (c f) -> p c f", c=NCK)
        ba = alloc(2 * NCK * 16, F32).rearrange("p (c f) -> p c f", c=NCK)
        beta = alloc(2 * NCK * 8, F32).rearrange("p (c f) -> p c f", c=NCK)
        gt = alloc(2 * NCK * 8, F32).rearrange("p (c f) -> p c f", c=NCK)
        gtmp = alloc(2 * NCK * 8, F32).rearrange("p (c f) -> p c f", c=NCK)
        nA = alloc(2 * 8, F32)
        bd16 = alloc(128)
        ones64 = self.ones64[:, :]
        onec = alloc(2 * 1, F32)
        rhsG = alloc(2 * 512, F32).rearrange("p (h j) -> p h j", h=8)
        rhsB = alloc(2 * 512, F32).rearrange("p (h j) -> p h j", h=8)
        gcs = alloc(2 * 8, F32)
        sc = alloc(2 * 8, F32)
        kdsc = alloc(2 * 8, F32)
        dT = alloc(2 * 512, F32).rearrange("p (h j) -> p h j", h=8)
        t1 = alloc(2 * 512, F32).rearrange("p (h j) -> p h j", h=8)
        eg = alloc(2 * 512, F32).rearrange("p (h j) -> p h j", h=8)
        u_sb = alloc(2 * 512, F32).rearrange("p (h j) -> p h j", h=8)
        o_sb = alloc(2 * 512, F32).rearrange("p (h j) -> p h j", h=8)
        S32 = alloc(2 * 512, F32).rearrange("p (c j) -> p c j", c=4)
        tmpS = t1.rearrange("p h j -> p (h j)").rearrange("p (c j) -> p c j", c=4)
        gl = alloc(2 * 4, F32)
        ss8 = alloc(2 * 8, F32)
        Lt = alloc(512).rearrange("p (h j) -> p h j", h=8)
        Tm = alloc(512).rearrange("p (h j) -> p h j", h=8)
        Tt = alloc(512).rearrange("p (h j) -> p h j", h=8)
        Ym = alloc(512).rearrange("p (h j) -> p h j", h=8)
        attnT = alloc(512).rearrange("p (h j) -> p h j", h=8)
        vb = alloc(512).rearrange("p (h j) -> p h j", h=8)
        kdec = alloc(512).rearrange("p (h j) -> p h j", h=8)
        vnew = alloc(512).rearrange("p (h j) -> p h j", h=8)
        ogb = alloc(512).rearrange("p (h j) -> p h j", h=8)
        kbgx = alloc(1024).rearrange("p (h j) -> p h j", h=8)
        wT = alloc(256).rearrange("p (c j) -> p c j", c=4)
        qd = alloc(256).rearrange("p (c j) -> p c j", c=4)
        Sbf = alloc(512).rearrange("p (c j) -> p c j", c=4)
        B = {n: Buf() for n in ("xg", "ctmp0", "ctmp1", "qT", "kT", "vT", "k_tm", "v_tm", "zs", "ba", "beta", "gt", "gtmp",
                                "nA", "bd16", "ones64", "onec", "rhsG", "rhsB", "gcs", "sc", "kdsc", "dT", "t1", "eg", "u",
                                "o", "S32", "tmpS", "gl", "ss8", "Lt", "T", "Tt", "Y", "attnT", "vb", "kdec", "vnew", "ogb",
                                "kbgx", "wT", "qd", "Sbf", "sqd", "rtd", "rsd")}
        cinb = [Buf() for _ in range(12)]
        bank = [0]

        def nb():
            i = bank[0] % 6
            bank[0] += 1
            return self.pb[i], self.pbb[i]

        cstb, epb = self.cstb, self.epb
        U64 = self.cst[0:64, 256:320]
        NEGT = self.cst[0:64, 320:384]
        LM = lambda bi: self.cst[0:64, 448 + bi * 64:448 + (bi + 1) * 64]
        NLM = lambda bi: self.cst[0:64, 960 + bi * 64:960 + (bi + 1) * 64]
        BDM = self.cst[:, 832:960]
        I64f = self.identf[0:64, 0:64]
        I64b = self.ident[0:64, 0:64]
        cw = lambda fc, k: self.ep[:, eo + 4 + fc * 4 + k:eo + 4 + fc * 4 + k + 1]
        alog = self.ep[0:64, eo + 52:eo + 60]
        dtb = self.ep[0:64, eo + 60:eo + 68]
        nwd = self.ep[0:64, eo + 68:eo + 132]
        V, Aop = "dve", "act"
        PL = "dve" if os.environ.get("DN_NOPOOL") else "pool"

        def bc(ap, shape):
            return ap.broadcast_to(shape)

        kb.op(V, [], cinb, lambda e: e.memset(cin, 0.0))
        kb.op(V, [], [B["kT"]], lambda e: e.memset(kTz, 0.0))
        kb.op(V, [], [B["S32"]], lambda e: e.memset(S32, 0.0))
        kb.op(V, [], [B["Sbf"]], lambda e: e.memset(Sbf, 0.0))
        kb.op(V, [], [B["kbgx"]], lambda e: e.memset(kbgx, 0.0))
        kb.op(V, [], [B["ones64"]], lambda e: e.memset(ones64, 1.0))
        kb.op(V, [], [B["onec"]], lambda e: e.memset(onec, 1.0))
        kb.op(V, [cstb], [B["bd16"]], lambda e: e.tensor_copy(out=bd16, in_=BDM))
        kb.op(Aop, [epb], [B["nA"]], lambda e: e.activation(out=nA[0:64, :], in_=alog, func=AF.Exp))
        kb.op(V, [B["nA"]], [B["nA"]], lambda e: e.tensor_scalar_mul(out=nA[0:64, :], in0=nA[0:64, :], scalar1=-1.0))
        sqd = self.sq[:, 0, 0:G]
        rtd = self.rt[:, 0:G]
        rsd = self.rr[:, 0:G]
        nidx = layer * 3 + 1
        tpb = self.pb[7][:, :].bitcast(BF16)
        tpbb = self.pbb[7]
        stat, statb = self.pb[6], self.pbb[6]


        for _nm in ("rhsG", "rhsB", "gcs", "sc", "kdsc", "dT", "t1", "eg", "Lt", "T", "Tt", "Y"):
            B[_nm + "0"] = Buf()
            B[_nm + "1"] = Buf()

        def BB(nm):
            return [B[nm + "0"], B[nm + "1"]]
        SH = [64, 4, 64]

        def v3h(ap):
            return ap.rearrange("p (h j) -> p h j", h=4)
        bkH = [[0], [0]]

        def mk_nb(hh):
            def f():
                i = 2 * hh + bkH[hh][0] % 2
                bkH[hh][0] += 1
                return self.pb[i], self.pbb[i]
            return f
        nbPh = [mk_nb(0), mk_nb(1)]
        B["t1r"] = Buf()
        t1r = self.sg[:, 0, :].rearrange("p (h j) -> p h j", h=8)
        tmpSr = self.sg[:, 0, :].rearrange("p (c j) -> p c j", c=4)
        bkP, bkR = [0], [0]

        def nbP():
            i = bkP[0] % 4
            bkP[0] += 1
            return self.pb[i], self.pbb[i]

        def nbR():
            i = 4 + bkR[0] % 2
            bkR[0] += 1
            return self.pb[i], self.pbb[i]
        B["rtd"] = self.rtb
        B["rsd"] = self.rrb
        B["sqd"] = self.sqb[0]
        AX = mybir.AxisListType.X
        S8 = [64, 8, 64]

        def v3(ap):
            return ap.rearrange("p (h j) -> p h j", h=8)

        def chunk_pa(n, ch, cs, hh):
            nb = nbPh[hh]
            h0 = 4 * hh
            hs = slice(h0, h0 + 4)
            gch = gt[0:64, ch, hs]
            bch = beta[0:64, ch, hs]
            kb.op(PL, [cstb, B["gt"]], [B["rhsG%d" % hh]], lambda e: e.tensor_tensor(
                out=rhsG[0:64, hs], in0=bc(U64.unsqueeze(1), SH), in1=bc(gch.unsqueeze(2), SH), op=ALU.mult))
            kb.op(PL, [self.identb, B["beta"]], [B["rhsB%d" % hh]], lambda e: e.tensor_tensor(
                out=rhsB[0:64, hs], in0=bc(I64f.unsqueeze(1), SH), in1=bc(bch.unsqueeze(2), SH), op=ALU.mult))
            pG, pGb = nb()
            kb.op("pe", [B["ones64"], B["rhsG%d" % hh]], [pGb], lambda e: e.matmul(
                out=pG[:, 0:256], lhsT=ones64[0:64, :], rhs=rhsG[0:64, hs].rearrange("p h j -> p (h j)"), start=True, stop=True))
            pC, pCb = nb()
            kb.op("pe", [cstb, B["gt"]], [pCb], lambda e: e.matmul(out=pC[0:64, 0:4], lhsT=U64, rhs=gch, start=True, stop=True))
            kb.op(V, [pCb], [B["gcs%d" % hh]], lambda e: e.tensor_copy(out=gcs[0:64, hs], in_=pC[0:64, 0:4]))
            pG3 = v3h(pG[0:64, 0:256])
            kb.op(V, [pGb, B["gcs%d" % hh]], [B["t1%d" % hh]], lambda e: e.tensor_tensor(
                out=t1[0:64, hs], in0=pG3, in1=bc(gcs[0:64, hs].unsqueeze(2), SH), op=ALU.subtract))
            kb.op(V, [B["t1%d" % hh], cstb], [B["t1%d" % hh]], lambda e: e.tensor_tensor(
                out=t1[0:64, hs], in0=t1[0:64, hs], in1=bc(NEGT.unsqueeze(1), SH), op=ALU.add))
            kb.op(Aop, [B["t1%d" % hh]], [B["dT%d" % hh]], lambda e: e.activation(out=dT[0:64, hs], in_=t1[0:64, hs], func=AF.Exp))
            kb.op(Aop, [pGb], [B["eg%d" % hh]], lambda e: e.activation(
                out=eg[:, hs].rearrange("p h j -> p (h j)"), in_=pG[:, 0:256], func=AF.Exp))
            kb.op(V, [pGb, B["gcs%d" % hh]], [B["kdsc%d" % hh]], lambda e: e.tensor_tensor(
                out=kdsc[0:64, hs], in0=pG3[:, :, 63], in1=gcs[0:64, hs], op=ALU.subtract))
            kb.op(Aop, [B["kdsc%d" % hh]], [B["kdsc%d" % hh]], lambda e: e.activation(out=kdsc[0:64, hs], in_=kdsc[0:64, hs], func=AF.Exp))
            kb.op(Aop, [B["gcs%d" % hh]], [B["sc%d" % hh]], lambda e: e.activation(out=sc[0:64, hs], in_=gcs[0:64, hs], func=AF.Exp))
            kb.op(V, [B["sc%d" % hh], B["beta"]], [B["sc%d" % hh]], lambda e: e.tensor_tensor(
                out=sc[0:64, hs], in0=sc[0:64, hs], in1=bch, op=ALU.mult))
            yield "PA"
            pB, pBb = nb()
            kb.op("pe", [B["ones64"], B["rhsB%d" % hh]], [pBb], lambda e: e.matmul(
                out=pB[0:64, 0:256], lhsT=ones64[0:64, 0:64], rhs=rhsB[0:64, hs].rearrange("p h j -> p (h j)"), start=True, stop=True))
            pK, pKb = nb()

            def mmk(e, p=pK, other=kT):
                for h in range(h0, h0 + 4):
                    hb, pr = h % 2, h // 2
                    rhs = kTz[:, hb, pr, cs] if other is kT else other[:, pr, cs]
                    ins = e.matmul(out=p[0:64, (h - h0) * 64:(h - h0 + 1) * 64], lhsT=kTz[:, hb, pr, cs], rhs=rhs, start=True, stop=True)
                return ins
            kb.op("pe", [B["kT"]], [pKb], mmk)
            kb.op(V, [B["dT%d" % hh], self.identb], [B["t1%d" % hh]], lambda e: e.tensor_tensor(
                out=t1[0:64, hs], in0=dT[0:64, hs], in1=bc(I64f.unsqueeze(1), SH), op=ALU.subtract))
            kb.op(V, [pKb, B["t1%d" % hh]], [B["t1%d" % hh]], lambda e: e.tensor_tensor(
                out=t1[0:64, hs], in0=v3h(pK[0:64, 0:256]), in1=t1[0:64, hs], op=ALU.mult))
            kb.op(V, [pBb, B["t1%d" % hh]], [B["Lt%d" % hh]], lambda e: e.tensor_tensor(
                out=Lt[0:64, hs], in0=v3h(pB[0:64, 0:256]), in1=t1[0:64, hs], op=ALU.mult))
            yield "PA"
            for bi in range(6):
                first, last = bi == 0, bi == 5
                bsz = 1 << bi
                nk = 64 // (2 * bsz)

                def half(ap3, t, bsz=bsz):
                    return ap3.rearrange("p h (k t r) -> p h k t r", t=2, r=bsz)[:, :, :, t, :]

                def half2(ap2, t, bsz=bsz):
                    return ap2.rearrange("p (k t r) -> p k t r", t=2, r=bsz)[:, :, t, :]

                def cmp3(p, bsz=bsz, nk=nk):
                    return p[0:64, 0:128].rearrange("p (h k r) -> p h k r", h=4, r=bsz)
                pY, pYb = nb()

                def mmy(e, pY=pY, first=first):
                    for h in range(h0, h0 + 4):
                        ins = e.matmul(out=pY[0:64, (h - h0) * 64:(h - h0 + 1) * 64], lhsT=Lt[0:64, h, :],
                                       rhs=(I64b if first else Tm[0:64, h, :]), start=True, stop=True)
                    return ins
                kb.op("pe", [B["Lt%d" % hh], self.identb] + ([] if first else [B["T%d" % hh]]), [pYb], mmy)
                kb.op(V, [pYb, cstb], [B["Y%d" % hh]], lambda e, pY=pY, bi=bi: e.tensor_tensor(
                    out=Ym[0:64, hs], in0=v3h(pY[0:64, 0:256]), in1=bc(NLM(bi).unsqueeze(1), SH), op=ALU.mult))
                yield "PA"
                if not last:
                    pZ, pZb = nb()

                    def mmz1(e, pZ=pZ, first=first, half2=half2, bsz=bsz, nk=nk):
                        for h in range(h0, h0 + 4):
                            o_ = pZ[0:64, (h - h0) * 32:(h - h0 + 1) * 32].rearrange("p (k r) -> p k r", r=bsz)
                            e.matmul(out=o_, lhsT=(I64b if first else Tt[0:64, h, :]), rhs=half2(Ym[0:64, h, :], 0),
                                     start=True, stop=False)
                            ins = e.matmul(out=o_, lhsT=I64b, rhs=half2((I64b if first else Tm[0:64, h, :]), 0),
                                           start=False, stop=True)
                        return ins
                    kb.op("pe", [B["Y%d" % hh], self.identb] + ([] if first else [B["Tt%d" % hh], B["T%d" % hh]]), [pZb], mmz1)
                pZ2, pZ2b = nb()

                def mmz2(e, pZ2=pZ2, first=first, half2=half2, bsz=bsz):
                    for h in range(h0, h0 + 4):
                        o_ = pZ2[0:64, (h - h0) * 32:(h - h0 + 1) * 32].rearrange("p (k r) -> p k r", r=bsz)
                        ins = e.matmul(out=o_, lhsT=Ym[0:64, h, :], rhs=half2((I64b if first else Tt[0:64, h, :]), 1),
                                       start=True, stop=True)
                    return ins
                kb.op("pe", [B["Y%d" % hh], self.identb] + ([] if first else [B["Tt%d" % hh]]), [pZ2b], mmz2)
                if first:
                    kb.op(V, [self.identb], [B["T%d" % hh]], lambda e: e.tensor_copy(out=Tm[0:64, hs], in_=bc(I64b.unsqueeze(1), SH)))
                    kb.op(V, [self.identb], [B["Tt%d" % hh]], lambda e: e.tensor_copy(out=Tt[0:64, hs], in_=bc(I64b.unsqueeze(1), SH)))
                if not last:
                    kb.op(Aop, [pZb], [B["T%d" % hh]], lambda e, pZ=pZ, half=half, cmp3=cmp3: e.activation(
                        out=half(Tm[0:64, hs], 0), in_=cmp3(pZ), func=AF.Copy))
                kb.op(V, [pZ2b, B["Tt%d" % hh]], [B["Tt%d" % hh]], lambda e, pZ2=pZ2, half=half, cmp3=cmp3: e.tensor_tensor(
                    out=half(Tt[0:64, hs], 1), in0=half(Tt[0:64, hs], 1), in1=cmp3(pZ2), op=ALU.add))
                yield "PA"
            yield "PAend"

        def chunk_rest(n, ch, cs):
            nb = nbPh[0]
            gch = gt[0:64, ch, :]
            bch = beta[0:64, ch, :]

            def mmk(e, p, other):
                for h in range(8):
                    hb, pr = h % 2, h // 2
                    ins = e.matmul(out=p[0:64, h * 64:(h + 1) * 64], lhsT=kTz[:, hb, pr, cs], rhs=other[:, pr, cs], start=True, stop=True)
                return ins
            pQ, pQb = nb()
            kb.op("pe", [B["kT"], B["qT"]], [pQb], lambda e: mmk(e, pQ, qT))
            kb.op(V, [pQb, *BB("dT")], [B["attnT"]], lambda e: e.tensor_tensor(
                out=attnT[0:64], in0=v3(pQ[0:64, :]), in1=dT[0:64], op=ALU.mult))
            k8 = v3(k_tm[0:64, ch, :])
            kb.op(PL, [B["v_tm"], B["beta"]], [B["vb"]], lambda e: e.tensor_tensor(
                out=vb[0:64], in0=v3(v_tm[0:64, ch, :]), in1=bc(bch.unsqueeze(2), S8), op=ALU.mult))
            kx5 = kbgx[0:64].rearrange("p (pr hb) (hf d) -> p pr hb hf d", hb=2, hf=2)
            k4 = k_tm[0:64, ch, :].rearrange("p (pr hb d) -> p pr hb d", pr=4, hb=2)
            sc4 = sc[0:64, :].rearrange("p (pr hb) -> p pr hb", hb=2)
            for hb in range(2):
                kb.op(PL, [B["k_tm"], *BB("sc")], [B["kbgx"]], lambda e, hb=hb: e.tensor_tensor(
                    out=kx5[:, :, hb, hb, :], in0=k4[:, :, hb, :], in1=bc(sc4[:, :, hb].unsqueeze(2), [64, 4, 64]), op=ALU.mult))
            kb.op(PL, [B["k_tm"], *BB("kdsc")], [B["kdec"]], lambda e: e.tensor_tensor(
                out=kdec[0:64], in0=k8, in1=bc(kdsc[0:64, :].unsqueeze(2), S8), op=ALU.mult))
            pU, pUb = nb()

            def mmu(e):
                for h in range(8):
                    ins = e.matmul(out=pU[0:64, h * 64:(h + 1) * 64], lhsT=Tt[0:64, h, :], rhs=vb[0:64, h, :], start=True, stop=True)
                return ins
            kb.op("pe", [*BB("Tt"), B["vb"]], [pUb], mmu)
            kb.op(Aop, [pUb], [B["u"]], lambda e: e.activation(
                out=u_sb[0:64].rearrange("p h j -> p (h j)"), in_=pU[0:64, :], func=AF.Copy))
            pW, pWb = nb()

            def mmw(e):
                for h in range(8):
                    pr, hb = h // 2, h % 2
                    ins = e.matmul(out=pW[:, pr * 64:(pr + 1) * 64], lhsT=kbgx[0:64, h, :], rhs=Tt[0:64, h, :],
                                   start=(hb == 0), stop=(hb == 1))
                return ins
            kb.op("pe", [*BB("Tt"), B["kbgx"]], [pWb], mmw)
            kb.op(Aop, [pWb], [B["wT"]], lambda e: e.activation(
                out=wT.rearrange("p c j -> p (c j)"), in_=pW[:, 0:256], func=AF.Copy))
            eg5 = eg.rearrange("p (pr hb) j -> p pr hb j", hb=2)
            for hb in range(2):
                ps_ = slice(hb * 64, (hb + 1) * 64)
                kb.op(PL, [B["qT"], *BB("eg")], [B["qd"]], lambda e, hb=hb, ps_=ps_: e.tensor_tensor(
                    out=qd[ps_, :, :], in0=qT[ps_, :, cs], in1=eg5[ps_, :, hb, :], op=ALU.mult))
                kb.op(PL, [*BB("eg")], [B["gl"]], lambda e, hb=hb, ps_=ps_: e.tensor_copy(
                    out=gl[ps_, :], in_=eg5[ps_, :, hb, 63]))
            yield "Rstart"
            nb = nbR
            pA, pAb = nb()

            def mma(e):
                for pr in range(4):
                    ins = e.matmul(out=pA[0:64, pr * 128:(pr + 1) * 128], lhsT=wT[:, pr, :], rhs=Sbf[:, pr, :], start=True, stop=True)
                return ins
            kb.op("pe", [B["wT"], B["Sbf"]], [pAb], mma)
            kb.op(V, [B["u"], pAb], [B["vnew"]], lambda e: e.tensor_tensor(
                out=vnew[0:64], in0=u_sb[0:64], in1=v3(pA[0:64, :]), op=ALU.subtract))
            yield "R"
            pO, pOb = nb()

            def mmo(e):
                for pr in range(4):
                    for hb in range(2):
                        h = 2 * pr + hb
                        e.matmul(out=pO[0:64, h * 64:(h + 1) * 64], lhsT=qd[:, pr, :], rhs=Sbf[:, pr, hb * 64:(hb + 1) * 64],
                                 start=True, stop=False)
                        ins = e.matmul(out=pO[0:64, h * 64:(h + 1) * 64], lhsT=attnT[0:64, h, :], rhs=vnew[0:64, h, :],
                                       start=False, stop=True)
                return ins
            kb.op("pe", [B["qd"], B["Sbf"], B["attnT"], B["vnew"]], [pOb], mmo)
            kb.op(Aop, [pOb], [B["o"]], lambda e: e.activation(
                out=o_sb[0:64].rearrange("p h j -> p (h j)"), in_=pO[0:64, :], func=AF.Copy))
            yield "R"
            pS, pSb = nb()

            def mms(e):
                for pr in range(4):
                    for hb in range(2):
                        h = 2 * pr + hb
                        ins = e.matmul(out=pS[hb * 64:(hb + 1) * 64, pr * 128 + hb * 64:pr * 128 + (hb + 1) * 64],
                                       lhsT=kdec[0:64, h, :], rhs=vnew[0:64, h, :], start=True, stop=True)
                return ins
            kb.op("pe", [B["kdec"], B["vnew"]], [pSb], mms)
            kb.op(PL, [B["gl"], B["S32"]], [B["S32"]], lambda e: e.tensor_tensor(
                out=S32, in0=S32, in1=bc(gl.unsqueeze(2), [128, 4, 128]), op=ALU.mult))
            pS4 = pS[:, :].rearrange("p (c j) -> p c j", c=4)
            for hb in range(2):
                ps_ = slice(hb * 64, (hb + 1) * 64)
                kb.op(V, [pSb, B["S32"]], [B["S32"]], lambda e, ps_=ps_: e.tensor_tensor(
                    out=S32[ps_, :, ps_], in0=S32[ps_, :, ps_], in1=pS4[ps_, :, ps_], op=ALU.add))
            kb.op(Aop, [B["S32"]], [B["Sbf"]], lambda e: e.activation(
                out=Sbf.rearrange("p c j -> p (c j)"), in_=S32.rearrange("p c j -> p (c j)"), func=AF.Copy))
            yield "R"
            kb.op(Aop, [pOb], [B["t1r"]], lambda e: e.activation(
                out=t1r[0:64].rearrange("p h j -> p (h j)"), in_=pO[0:64, :], func=AF.Square))
            kb.op(V, [B["t1r"]], [B["ss8"]], lambda e: e.tensor_reduce(out=ss8[0:64, :], in_=t1r[0:64], axis=AX, op=ALU.add))
            kb.op(Aop, [B["ss8"], self.epsb], [B["ss8"]], lambda e: e.activation(
                out=ss8[0:64, :], in_=ss8[0:64, :], func=AF.Ln, bias=self.epsc[0:64, 0:1], scale=1.0 / 64))
            kb.op(Aop, [B["ss8"]], [B["ss8"]], lambda e: e.activation(
                out=ss8[0:64, :], in_=ss8[0:64, :], func=AF.Exp, scale=-0.5))
            kb.op(V, [B["o"], B["ss8"]], [B["t1r"]], lambda e: e.tensor_tensor(
                out=t1r[0:64], in0=o_sb[0:64], in1=bc(ss8[0:64, :].unsqueeze(2), S8), op=ALU.mult))
            kb.op(V, [B["t1r"], B["zs"]], [B["ogb"]], lambda e: e.tensor_tensor(
                out=ogb[0:64], in0=t1r[0:64], in1=v3(zs[0:64, ch, :]), op=ALU.mult))

            yield "R"

            def tro(e):
                for pr in range(4):
                    ins = e.transpose(out=tpb[:, pr * 64:(pr + 1) * 64],
                                      in_=ogb[0:64, 2 * pr:2 * pr + 2, :].rearrange("p h j -> p (h j)"),
                                      identity=self.ident[0:64, 0:64])
                return ins
            kb.op("pe", [B["ogb"], self.identb], [tpbb], tro)
            mb = [self.mixb[c][n // 2] for c in range(4, 8)]
            kb.op(Aop, [tpbb, epb], mb, lambda e: e.activation(
                out=self.mix[:, 4:8, n * 64:(n + 1) * 64], in_=tpb[:, 0:256].rearrange("p (c j) -> p c j", c=4), func=AF.Copy,
                scale=self.ep[:, eo + 132:eo + 133]))


        rt2 = [self.rt[:, 0:G], self.rt[:, G:2 * G]]
        rs2 = [self.rr[:, 0:G], self.rr[:, G:2 * G]]
        rt2b = [Buf(), Buf()]
        rs2b = [Buf(), Buf()]

        dgc = ctmp.rearrange("p a t -> p (a t)").bitcast(BF16).rearrange("p (a k j) -> p a k j", a=2, k=4)
        cwv = self.ep[:, eo + 4:eo + 52].rearrange("p (c k) -> p c k", k=4)

        def stage1a(fc):
            w, wb = self.wtile()
            p, pbf = nb()

            def mm(e, w=w, p=p):
                for k in range(NCH):
                    ins = e.matmul(out=p[:, 0:G], lhsT=w[:, k * 128:(k + 1) * 128], rhs=xg[:, k, :],
                                   start=(k == 0), stop=(k == NCH - 1))
                return ins
            kb.op("pe", [wb, B["xg"]], [pbf], mm)
            kb.op(Aop, [pbf], [cinb[fc]],
                  lambda e, p=p, fc=fc: e.activation(out=cin[:, fc, 3:3 + G], in_=p[:, 0:G], func=AF.Copy))

        def stage1b(fc):
            ci = fc % 2
            dg = dgc[:, ci]
            dgb = B["ctmp%d" % ci]
            kb.op(V, [self.identb, epb], [dgb], lambda e, dg=dg, fc=fc: e.tensor_tensor(
                out=dg, in0=bc(self.ident[:, :].unsqueeze(1), [128, 4, 128]),
                in1=bc(cwv[:, fc, :].unsqueeze(2), [128, 4, 128]), op=ALU.mult))
            pc, pcb = nb()

            def mmc(e, pc=pc, ci=ci, fc=fc):
                for k in range(4):
                    ins = e.matmul(out=pc[:, 0:G], lhsT=dgc[:, ci, k, :], rhs=cin[:, fc, k:k + G],
                                   start=(k == 0), stop=(k == 3))
                return ins
            kb.op("pe", [dgb, cinb[fc]], [pcb], mmc)
            kb.op(V, [cinb[fc]], [cinb[fc]],
                  lambda e, fc=fc: e.tensor_copy(out=cin[:, fc, 0:3], in_=cin[:, fc, G:G + 3]))
            if fc >= 8:
                dst, dstb = vT[:, fc - 8, :], B["vT"]
            elif fc >= 4:
                dst, dstb = kT[:, fc - 4, :], B["kT"]
            else:
                dst, dstb = qT[:, fc, :], B["qT"]
            kb.op(Aop, [pcb], [dstb], lambda e, pc=pc, dst=dst: e.activation(out=dst, in_=pc[:, 0:G], func=AF.Silu))

        def stage2a(fc):
            ci = fc % 2
            dst, dstb = (qT[:, fc, :], B["qT"]) if fc < 4 else (kT[:, fc - 4, :], B["kT"])
            sqv = self.sq[:, ci, 0:G]
            kb.op(Aop, [dstb], [self.sqb[ci]], lambda e, dst=dst, sqv=sqv: e.activation(out=sqv, in_=dst, func=AF.Square))
            ps_, psb_ = nb()
            kb.op("pe", [self.sqb[ci], B["bd16"]], [psb_],
                  lambda e, ps_=ps_, sqv=sqv: e.matmul(out=ps_[:, 0:G], lhsT=bd16, rhs=sqv, start=True, stop=True))
            return ps_, psb_

        def stage2b(fc, ps_, psb_):
            ci = fc % 2
            dst, dstb, scl = (qT[:, fc, :], B["qT"], 0.125) if fc < 4 else (kT[:, fc - 4, :], B["kT"], 1.0)
            kb.op(Aop, [psb_, self.epsb], [rt2b[ci]],
                  lambda e, ps_=ps_, ci=ci: e.activation(out=rt2[ci], in_=ps_[:, 0:G], func=AF.Ln, bias=self.epsc[:, 0:1]))
            kb.op(Aop, [rt2b[ci]], [rs2b[ci]],
                  lambda e, ci=ci: e.activation(out=rs2[ci], in_=rt2[ci], func=AF.Exp, scale=-0.5))
            kb.op(V, [dstb, rs2b[ci]], [dstb],
                  lambda e, dst=dst, scl=scl, ci=ci: e.scalar_tensor_tensor(
                      out=dst, in0=dst, scalar=scl, in1=rs2[ci], op0=ALU.mult, op1=ALU.mult))
            if fc >= 4:
                for hb in range(2):
                    kb.op(Aop, [dstb], [dstb], lambda e, hb=hb, fc=fc: e.activation(
                        out=kTz[hb * 64:(hb + 1) * 64, hb, fc - 4, :], in_=kT[hb * 64:(hb + 1) * 64, fc - 4, :], func=AF.Copy))

        for gi in range(T // G):
            t0 = gi * G
            tb = t0 // 512
            gs = slice(t0, t0 + G)
            for c in range(NCH):
                k = self.sqi % 4
                self.sqi += 1
                sqv = self.sq[:, k, 0:G]
                xv = self.x[:, c, gs]
                kb.op(Aop, [self.xb[c][tb]], [self.sqb[k]],
                      lambda e, sqv=sqv, xv=xv: e.activation(out=sqv, in_=xv, func=AF.Square))
                kb.op("pe", [self.sqb[k], self.onesb], [statb],
                      lambda e, sqv=sqv, c=c: e.matmul(out=stat[:, 0:G], lhsT=self.ones[:, :], rhs=sqv,
                                                       start=(c == 0), stop=(c == NCH - 1)))
            kb.op(Aop, [statb, self.epsb], [rt2b[0]],
                  lambda e: e.activation(out=rtd, in_=stat[:, 0:G], func=AF.Ln, bias=self.epsc[:, 0:1]))
            kb.op(Aop, [rt2b[0]], [rs2b[0]], lambda e: e.activation(out=rsd, in_=rtd, func=AF.Exp, scale=-0.5))
            for c in range(NCH):
                xv = self.x[:, c, gs]
                wv = self.nw[:, nidx * NCH + c: nidx * NCH + c + 1]
                kb.op(V, [self.xb[c][tb], rs2b[0], self.nwb], [B["xg"]],
                      lambda e, xv=xv, wv=wv, c=c: e.scalar_tensor_tensor(
                          out=xg[:, c, :], in0=xv, scalar=wv, in1=rsd, op0=ALU.mult, op1=ALU.mult))
            stage1a(0)
            for fc in range(12):
                if fc + 1 < 12:
                    stage1a(fc + 1)
                stage1b(fc)
            wz = [self.wtile() for _ in range(4)]
            for ch in range(NCK):
                p, pbf = nb()

                def mmz(e, p=p, ch=ch, wz=wz):
                    for k in range(NCH):
                        ins = e.matmul(out=p[0:64, :], lhsT=xg[:, k, ch * 64:(ch + 1) * 64],
                                       rhs=wz[k // 2][0][:, (k % 2) * 512:(k % 2 + 1) * 512],
                                       start=(k == 0), stop=(k == NCH - 1))
                    return ins
                kb.op("pe", [B["xg"]] + [w_[1] for w_ in wz], [pbf], mmz)
                kb.op(Aop, [pbf], [B["zs"]],
                      lambda e, p=p, ch=ch: e.activation(out=zs[0:64, ch, :], in_=p[0:64, :], func=AF.Silu))
            pend = stage2a(0)
            for fc in range(8):
                nxt = stage2a(fc + 1) if fc + 1 < 8 else None
                stage2b(fc, *pend)
                pend = nxt
            wba, wbab = self.wtile(128)
            p, pbf = nb()

            def mmb(e, p=p, wba=wba):
                for ch in range(NCK):
                    for k in range(NCH):
                        ins = e.matmul(out=p[0:64, ch * 16:(ch + 1) * 16], lhsT=xg[:, k, ch * 64:(ch + 1) * 64],
                                       rhs=wba[:, k * 16:(k + 1) * 16], start=(k == 0), stop=(k == NCH - 1))
                return ins
            kb.op("pe", [B["xg"], wbab], [pbf], mmb)
            kb.op(V, [pbf], [B["ba"]],
                  lambda e, p=p: e.tensor_copy(out=ba[0:64], in_=p[0:64, 0:NCK * 16].rearrange("p (c f) -> p c f", c=NCK)))
            kb.op(Aop, [B["ba"]], [B["beta"]],
                  lambda e: e.activation(out=beta[0:64], in_=ba[0:64, :, 0:8], func=AF.Exp, scale=-1.0))
            kb.op(V, [B["beta"]], [B["beta"]], lambda e: e.tensor_scalar_add(out=beta[0:64], in0=beta[0:64], scalar1=1.0))
            kb.op(V, [B["beta"]], [B["beta"]], lambda e: e.reciprocal(out=beta[0:64], in_=beta[0:64]))
            kb.op(V, [B["ba"], epb], [B["gtmp"]],
                  lambda e: e.tensor_tensor(out=gtmp[0:64], in0=ba[0:64, :, 8:16],
                                            in1=bc(dtb.unsqueeze(1), [64, NCK, 8]), op=ALU.add))
            kb.op(Aop, [B["gtmp"]], [B["gtmp"]], lambda e: e.activation(out=gtmp[0:64], in_=gtmp[0:64], func=AF.Exp))
            kb.op(Aop, [B["gtmp"], B["onec"]], [B["gtmp"]],
                  lambda e: e.activation(out=gtmp[0:64], in_=gtmp[0:64], func=AF.Ln, bias=onec[0:64, 0:1]))
            kb.op(V, [B["gtmp"], B["nA"]], [B["gt"]],
                  lambda e: e.tensor_tensor(out=gt[0:64], in0=gtmp[0:64], in1=bc(nA[0:64, :].unsqueeze(1), [64, NCK, 8]),
                                            op=ALU.mult))
            for ch in range(NCK):
                for (src, srcb, dst, dstb) in ((kT, B["kT"], k_tm, B["k_tm"]), (vT, B["vT"], v_tm, B["v_tm"])):
                    def tr(e, src=src, ch=ch):
                        for pr in range(4):
                            ins = e.transpose(out=tpb[0:64, pr * 128:(pr + 1) * 128], in_=src[:, pr, ch * 64:(ch + 1) * 64],
                                              identity=self.ident[:, :])
                        return ins
                    kb.op("pe", [srcb, self.identb], [tpbb], tr)
                    kb.op(Aop, [tpbb], [dstb],
                          lambda e, dst=dst, ch=ch: e.activation(out=dst[0:64, ch, :], in_=tpb[0:64, 0:512], func=AF.Copy))
            if os.environ.get("DN_NOCHUNK"):
                if gi == 0:
                    for c in range(4, 8):
                        for nn in range(16):
                            kb.op(V, [], [self.mixb[c][nn]],
                                  lambda e, c=c, nn=nn: e.memset(self.mix[:, c, nn * 128:(nn + 1) * 128], 0.0))
                continue
            def args(ch):
                return (gi * NCK + ch, ch, slice(ch * 64, (ch + 1) * 64))

            def run_pa(ch, rgen):
                ga, gb_ = chunk_pa(*args(ch), 0), chunk_pa(*args(ch), 1)
                da = db = False
                dr = rgen is None
                k = 0
                while not (da and db and dr):
                    if not da and next(ga, "PAend") == "PAend":
                        da = True
                    if not db and next(gb_, "PAend") == "PAend":
                        db = True
                    k += 1
                    if not dr and (k % 2 == 0 or (da and db)):
                        if next(rgen, None) is None:
                            dr = True

            def run_until(g, marks):
                for m in g:
                    if m in marks:
                        return m
                return None
            run_pa(0, None)
            for ch in range(NCK):
                gr = chunk_rest(*args(ch))
                run_until(gr, ("Rstart",))
                if ch + 1 < NCK:
                    run_pa(ch + 1, gr)
                else:
                    for _ in gr:
                        pass

    def build(self):
        kb = self.kb
        self.load()
        for stg in self.stages:
            if stg[0] == "ffn":
                self.ffn(stg[1])
            elif stg[0] == "conv":
                self.convmix(stg[1])
            elif stg[0] == "even":
                self.evenmix(stg[1])
            elif stg[0] == "final":
                self.rmsnorm(12, final=True)
            else:
                raise ValueError(stg)
        self.store()
        nc = self.nc
        with nc.Block() as block:
            @block.tensor
            def _(e):
                kb.replay("pe", e)

            @block.scalar
            def _(e):
                kb.replay("act", e)

            @block.vector
            def _(e):
                kb.replay("dve", e)

            @block.gpsimd
            def _(e):
                kb.replay("pool", e)

            @block.sync
            def _(e):
                kb.replay("sp", e)
        self.st.close()
        return nc


def pack_cols(w, m):
    nk = w.shape[0] // 128
    blk = w[:, m * 128:(m + 1) * 128].reshape(nk, 128, 128)
    return _tile(blk.transpose(1, 0, 2).reshape(128, nk * 128))


def pack_conv(w1, w2):
    tiles = []
    for j in range(8):
        tiles.append(pack_cols(w1, j))
        tiles.append(pack_cols(w1, 8 + j))
    for c in range(8):
        tiles.append(pack_cols(w2, c))
    return tiles


def pack_cp(inp):
    cp = np.zeros((128, 2 * CPW), np.float32)
    for ci in range(2):
        o = ci * CPW
        cp[:, o:o + 16] = inp["conv_b_pw1"][ci].reshape(16, 128).T
        cp[:, o + 16:o + 264] = inp["conv_w_dw"][ci].reshape(31, 8, 128).transpose(2, 1, 0).reshape(128, 248)
        cp[:, o + 264:o + 272] = inp["conv_b_dw"][ci].reshape(8, 128).T
        cp[:, o + 272:o + 280] = inp["conv_ln_w"][ci].reshape(8, 128).T
        cp[:, o + 280:o + 288] = inp["conv_ln_b"][ci].reshape(8, 128).T
        cp[:, o + 288:o + 296] = inp["conv_b_pw2"][ci].reshape(8, 128).T
    return cp


QPERM = np.concatenate([np.concatenate([np.arange(c * 64, (c + 1) * 64), np.arange((4 + c) * 64, (5 + c) * 64)])
                        for c in range(4)])


def pack_dn(w_in):
    tiles = []
    for gi in range(T // GT):
        for fc in range(12):
            tiles.append(pack_cols(w_in[:, 768:2304], fc))
        wz = w_in[:, 2304:2816]
        for kp in range(4):
            tiles.append(_tile(np.concatenate([wz[(2 * kp) * 128:(2 * kp + 1) * 128], wz[(2 * kp + 1) * 128:(2 * kp + 2) * 128]], axis=1)))
        wba = w_in[:, 2816:2832]
        tiles.append(_tile(wba.reshape(8, 128, 16).transpose(1, 0, 2).reshape(128, 128)))
    return tiles


def pack_even(w_in, w_out):
    tiles = []
    if "att" in EVEN_PARTS:
        wq = w_in[:, 0:512][:, QPERM]
        for c in range(4):
            tiles.append(pack_cols(wq, c))
        tiles.append(pack_cols(w_in[:, 512:640], 0))
        tiles.append(pack_cols(w_in[:, 640:768], 0))
    if "dn" in EVEN_PARTS:
        tiles += pack_dn(w_in)
    wo = np.concatenate([w_out[0:512][QPERM], w_out[512:1024]], axis=0)
    for c in range(8):
        tiles.append(pack_cols(wo, c))
    return tiles


def pack_ep(inp):
    ep = np.zeros((128, 2 * EPW), np.float32)
    for e in range(2):
        o = e * EPW
        sk = inp["attn_sinks"][e]
        ep[0:64, o:o + 4] = sk[0:4][None, :]
        ep[64:128, o:o + 4] = sk[4:8][None, :]
        ep[:, o + 4:o + 52] = inp["dn_conv_w"][e].reshape(4, 12, 128).transpose(2, 1, 0).reshape(128, 48)
        ep[:, o + 52:o + 60] = inp["dn_a_log"][e][None, :]
        ep[:, o + 60:o + 68] = inp["dn_dt_bias"][e][None, :]
        ep[:, o + 68:o + 132] = inp["dn_norm_w"][e][None, :]
        ep[:, o + 132] = np.concatenate([inp["dn_norm_w"][e], inp["dn_norm_w"][e]])
    return ep


def make_cst():
    c = np.zeros((128, CSTW), np.float32)
    j = np.arange(128)[:, None]
    i = np.arange(128)[None, :]
    c[:, 0:128] = np.where(j <= i, (j - i).astype(np.float32), -1e9)
    c[:, 128:256] = np.where(j > i, (j - i - 128).astype(np.float32), -1e9)
    t = np.arange(64)[:, None]
    u = np.arange(64)[None, :]
    c[0:64, 256:320] = (t <= u).astype(np.float32)
    c[0:64, 320:384] = np.where(u >= t, 0.0, -30000.0)
    for bi, b in enumerate((1, 2, 4, 8, 16, 32)):
        same = (t // (2 * b)) == (u // (2 * b))
        lm = same & ((t % (2 * b)) >= b) & ((u % (2 * b)) < b)
        c[0:64, 448 + bi * 64:448 + (bi + 1) * 64] = lm.astype(np.float32)
        c[0:64, 960 + bi * 64:960 + (bi + 1) * 64] = -lm.astype(np.float32)
    c[:, 832:960] = ((j // 64) == (i // 64)).astype(np.float32)
    return c


def stage_tiles(stages, inp):
    tiles = []
    for stg in stages:
        if stg[0] == "ffn":
            layer, j = divmod(stg[1], 3)
            fi = 0 if j == 0 else 1
            tiles += pack_ffn(inp["ffn_w_gate"][layer, fi], inp["ffn_w_up"][layer, fi], inp["ffn_w_down"][layer, fi])
        elif stg[0] == "even":
            e = stg[1] // 2
            tiles += pack_even(inp["mix_w_in"][e], inp["mix_w_out"][e])
        elif stg[0] == "conv":
            ci = stg[1] // 2
            tiles += pack_conv(inp["conv_w_pw1"][ci], inp["conv_w_pw2"][ci])
    return tiles


def pack_nw(inp):
    nw = np.concatenate([inp["norm_w"].reshape(12, D), inp["final_norm_w"].reshape(1, D)], axis=0)
    return np.ascontiguousarray(nw.reshape(13, NCH, 128).transpose(2, 0, 1).reshape(128, 13 * NCH))


def run_stages(stages, xT_list, inp, trace=False):
    tiles = stage_tiles(stages, inp)
    wst = np.stack(tiles) if tiles else np.zeros((1, 128, 1024), np.float32)
    nc = Prog(stages, len(tiles)).build()
    nw = pack_nw(inp)
    cp = pack_cp(inp)
    ident = np.eye(128, dtype=np.float32)
    ep = pack_ep(inp)
    cst = make_cst()
    in_maps = [{"xin": np.ascontiguousarray(xT), "wst": wst, "nw": nw, "cp": cp, "ident": ident, "ep": ep, "cst": cst}
               for xT in xT_list]
    res = run_bass_kernel_spmd(nc, in_maps, core_ids=list(range(len(xT_list))), trace=trace)
    return [r["xout"] for r in res.results], res


FULL = []
for _l in range(4):
    FULL += [("ffn", 3 * _l + 0), ("even" if _l % 2 == 0 else "conv", _l), ("ffn", 3 * _l + 2)]
FULL += [("final",)]


def kernel(**inputs):
    inp = {k: np.asarray(v) for k, v in inputs.items()}
    x = inp["x"]
    xT = [np.ascontiguousarray(x[b].T) for b in range(8)]
    outs, _ = run_stages(FULL, xT, inp)
    return np.stack([o.T for o in outs]).astype(np.float32)
```

```python
import os
import numpy as np
from contextlib import ExitStack
import concourse.bass as bass
import concourse.mybir as mybir
from concourse.bass_utils import run_bass_kernel_spmd

F32 = mybir.dt.float32
BF16 = mybir.dt.bfloat16
AF = mybir.ActivationFunctionType
ALU = mybir.AluOpType

D = 1024
T = 2048
DFF = 2816
NCH = D // 128
NTB = T // 512
NMC = DFF // 128
GROUPS = [(0, 8), (8, 8), (16, 6)]
EPS = 1e-6
RING = 8
PAD = 32
CPW = 296
ARENA = NCH * T + 8 * (PAD + T)
EPW = 133
CSTW = 1344
GT = 256
NCK = GT // 64
SLOPES = [2.0 ** (-8.0 * (h + 1) / 8) for h in range(8)]
EVEN_PARTS = ("att", "dn")


class Sem:
    def __init__(self, h):
        self.h = h
        self.count = 0


class Buf:
    __slots__ = ("w", "r")

    def __init__(self):
        self.w = None
        self.r = {}


class Eng:
    def __init__(self, name):
        self.name = name
        self.ops = []
        self.sem = None
        self.seen = {}
        self.fence = None


class KB:
    def __init__(self, nc, st):
        self.nc = nc
        self.st = st
        self.nsem = 0
        self.E = {}
        for n in ("pe", "act", "dve", "pool", "sp"):
            e = Eng(n)
            e.sem = self.newsem(n)
            self.E[n] = e
        self.dsems = []

    def newsem(self, name):
        self.nsem += 1
        return Sem(self.st.enter_context(self.nc.semaphore(f"s_{name}_{self.nsem}")))

    def sb(self, name, shape, dt):
        return self.st.enter_context(self.nc.sbuf_tensor(name, shape, dt))

    def ps(self, name, shape, dt=F32):
        return self.st.enter_context(self.nc.psum_tensor(name, shape, dt))

    def op(self, en, reads, writes, emit, dsem=None):
        eng = self.E[en]
        need = {}

        def add(tok):
            if tok is None:
                return
            s, v = tok
            if need.get(s, 0) < v:
                need[s] = v

        if eng.fence is not None and not (en == "pool" and dsem is not None):
            for tok in eng.fence:
                add(tok)
            eng.fence = None
        for b in reads:
            add(b.w)
        for b in writes:
            add(b.w)
            for s, v in b.r.items():
                add((s, v))
        waits = []
        for s, v in need.items():
            if en == "pe" and s is eng.sem:
                continue
            if eng.seen.get(s, 0) < v:
                eng.seen[s] = v
                waits.append((s, v))
        if dsem is None:
            sem = eng.sem
            sem.count += 1
            inc = 1
        else:
            sem = dsem
            sem.count += 16
            inc = 16
        tok = (sem, sem.count)
        eng.ops.append((waits, emit, sem, inc))
        for b in reads:
            if b.r.get(sem, 0) < sem.count:
                b.r[sem] = sem.count
        for b in writes:
            b.w = tok
            b.r = {}
        return tok

    def fence(self, new_epoch=False):
        toks = [(e.sem, e.sem.count) for e in self.E.values() if e.sem.count > 0]
        toks += [(s, s.count) for s in self.dsems if s.count > 0]
        for e in self.E.values():
            e.fence = list(toks)
        if new_epoch:
            for n in ("pe", "act", "dve"):
                self.E[n].sem = self.newsem(n)

    def replay(self, en, e):
        for waits, emit, sem, inc in self.E[en].ops:
            for s, v in waits:
                e.wait_ge(s.h, v)
            ins = emit(e)
            if ins is not None:
                ins.then_inc(sem.h, inc)


def _tile(a):
    t = np.zeros((128, 1024), np.float32)
    t[:, : a.shape[1]] = a
    return t


def pack_ffn(wg, wu, wd):
    tiles = []
    for (m0, nm) in GROUPS:
        for mi in range(nm):
            m = m0 + mi
            for w in (wg, wu):
                blk = w[:, m * 128:(m + 1) * 128].reshape(NCH, 128, 128)
                tiles.append(_tile(blk.transpose(1, 0, 2).reshape(128, NCH * 128)))
        for c in range(NCH):
            blk = wd[m0 * 128:(m0 + nm) * 128, c * 128:(c + 1) * 128].reshape(nm, 128, 128)
            tiles.append(_tile(blk.transpose(1, 0, 2).reshape(128, nm * 128)))
    return tiles


class Prog:
    def __init__(self, stages, n_wtiles):
        self.stages = stages
        nc = bass.Bass("TRN2", target_bir_lowering=False)
        self.nc = nc
        self.st = ExitStack()
        st = self.st
        self.xin = nc.dram_tensor("xin", [D, T], F32, kind="ExternalInput").ap()
        self.wst = nc.dram_tensor("wst", [max(n_wtiles, 1), 128, 1024], F32, kind="ExternalInput").ap()
        self.nw_d = nc.dram_tensor("nw", [128, 13 * NCH], F32, kind="ExternalInput").ap()
        self.cp_d = nc.dram_tensor("cp", [128, 2 * CPW], F32, kind="ExternalInput").ap()
        self.ident_d = nc.dram_tensor("ident", [128, 128], F32, kind="ExternalInput").ap()
        self.xout = nc.dram_tensor("xout", [D, T], F32, kind="ExternalOutput").ap()
        kb = KB(nc, st)
        self.kb = kb
        self.x = kb.sb("x_sb", [128, NCH, T], F32)
        self.xb = [[Buf() for _ in range(NTB)] for _ in range(NCH)]
        self.arena = kb.sb("arena_sb", [128, ARENA], BF16)
        self.xn = self.arena[:, 0:NCH * T].rearrange("p (c t) -> p c t", c=NCH)
        self.xnb = [[Buf() for _ in range(NTB)] for _ in range(NCH)]
        self.nw = kb.sb("nw_sb", [128, 13 * NCH], F32)
        self.nwb = Buf()
        self.ones = kb.sb("ones_sb", [128, 128], BF16)
        self.onesb = Buf()
        self.ring = kb.sb("wring", [128, RING, 1024], BF16)
        self.ringb = [Buf() for _ in range(RING)]
        self.ringsem = [kb.newsem("ring") for _ in range(RING)]
        kb.dsems += self.ringsem
        self.wi = 0
        self.act = self.arena[:, NCH * T:NCH * T + 8 * (PAD + T)].rearrange("p (c t) -> p c t", c=8)
        self.s2 = kb.sb("s2_sb", [128, NCH * T], BF16)
        self.mix = self.s2[:, :].rearrange("p (c t) -> p c t", c=NCH)
        self.mixb = [[Buf() for _ in range(16)] for _ in range(NCH)]
        self.ep_d = nc.dram_tensor("ep", [128, 2 * EPW], F32, kind="ExternalInput").ap()
        self.cst_d = nc.dram_tensor("cst", [128, CSTW], F32, kind="ExternalInput").ap()
        self.ep = kb.sb("ep_sb", [128, 2 * EPW], F32)
        self.cst = kb.sb("cst_sb", [128, CSTW], F32)
        self.epb = Buf()
        self.cstb = Buf()
        self.actb = [[Buf() for _ in range(NTB)] for _ in range(8)]
        self.sq = kb.sb("sq_sb", [128, 4, 512], BF16)
        self.sqb = [Buf() for _ in range(4)]
        self.sqi = 0
        self.rr = kb.sb("r_sb", [128, 512], F32)
        self.rrb = Buf()
        self.rt = kb.sb("rt_sb", [128, 512], F32)
        self.rtb = Buf()
        self.epsc = kb.sb("eps_sb", [128, 1], F32)
        self.epsb = Buf()
        self.sg = kb.sb("sg_sb", [128, 2, 512], F32)
        self.sgb = [Buf() for _ in range(2)]
        self.cp = kb.sb("cp_sb", [128, 2 * CPW], F32)
        self.cpb = Buf()
        self.ident = kb.sb("ident_sb", [128, 128], BF16)
        self.identf = kb.sb("identf_sb", [128, 128], F32)
        self.identb = Buf()
        self.diag = self.s2[:, 0:2 * 31 * 128].rearrange("p (a k j) -> p a k j", a=2, k=31)
        self.diagb = [Buf(), Buf()]
        self.dgi = 0
        self.tmpa = kb.sb("tmpa_sb", [128, 512], F32)
        self.tmpab = Buf()
        self.tmpb = kb.sb("tmpb_sb", [128, 2, 512], F32)
        self.tmpbb = [Buf(), Buf()]
        self.tbi = 0
        self.ones64 = kb.sb("ones64_sb", [128, 128], F32)
        self.sk = kb.sb("sk_sb", [128, 4], F32)
        self.skb = Buf()
        self.den = kb.sb("den_sb", [128, 128], F32)
        self.denb = Buf()
        self.sgi2 = 0
        self.pb = [kb.ps(f"psb{i}", [128, 512], F32) for i in range(8)]
        self.pbb = [Buf() for _ in range(8)]
        self.gi = 0
        self.yi = 0

    def wtile(self, ncols=1024):
        i = self.wi
        self.wi += 1
        slot = i % RING
        buf = self.ringb[slot]
        dst = self.ring[:, slot, 0:ncols]
        src = self.wst[i, :, 0:ncols]
        self.kb.op("pool", [], [buf], lambda e: e.dma_start(out=dst, in_=src), dsem=self.ringsem[slot])
        return self.ring[:, slot, :], buf

    def load(self):
        kb = self.kb
        s1 = kb.newsem("ldx")
        kb.dsems.append(s1)
        for t in range(NTB):
            st_ = kb.newsem("ldx%d" % t)
            kb.dsems.append(st_)
            for c in range(NCH):
                dst = self.x[:, c, t * 512:(t + 1) * 512]
                src = self.xin[c * 128:(c + 1) * 128, t * 512:(t + 1) * 512]
                kb.op("sp", [], [self.xb[c][t]], lambda e, dst=dst, src=src: e.dma_start(out=dst, in_=src), dsem=st_)
            for c in range(NCH):
                self.xb[c][t].w = (st_, st_.count)
        kb.op("sp", [], [self.nwb], lambda e: e.dma_start(out=self.nw[:, :], in_=self.nw_d), dsem=s1)
        kb.op("sp", [], [self.cpb], lambda e: e.dma_start(out=self.cp[:, :], in_=self.cp_d), dsem=s1)
        kb.op("sp", [], [self.epb], lambda e: e.dma_start(out=self.ep[:, :], in_=self.ep_d), dsem=s1)
        kb.op("sp", [], [self.cstb], lambda e: e.dma_start(out=self.cst[:, :], in_=self.cst_d), dsem=s1)
        kb.op("sp", [], [self.identb], lambda e: e.dma_start(out=self.identf[:, :], in_=self.ident_d), dsem=s1)
        fin = (s1, s1.count)
        self.nwb.w = fin
        self.cpb.w = fin
        self.epb.w = fin
        self.cstb.w = fin
        self.identb.w = fin
        kb.op("dve", [self.identb], [self.identb], lambda e: e.tensor_copy(out=self.ident[:, :], in_=self.identf[:, :]))
        kb.op("dve", [], [self.onesb], lambda e: e.memset(self.ones[:, :], 1.0 / D))
        kb.op("dve", [], [self.epsb], lambda e: e.memset(self.epsc[:, :], EPS))

    def store(self):
        kb = self.kb
        s1 = kb.newsem("stx")
        kb.dsems.append(s1)
        for t in range(NTB):
            for c in range(NCH):
                src = self.x[:, c, t * 512:(t + 1) * 512]
                dst = self.xout[c * 128:(c + 1) * 128, t * 512:(t + 1) * 512]
                kb.op("sp", [self.xb[c][t]], [], lambda e, dst=dst, src=src: e.dma_start(out=dst, in_=src), dsem=s1)
        fin = Buf()
        fin.w = (s1, s1.count)
        kb.op("sp", [fin], [], lambda e: None)

    def rmsnorm(self, nidx, final=False):
        kb = self.kb
        stat = self.pb[6]
        statb = self.pbb[6]
        for t in range(NTB):
            ts = slice(t * 512, (t + 1) * 512)
            for c in range(NCH):
                k = self.sqi % 4
                self.sqi += 1
                sqv = self.sq[:, k, :]
                xv = self.x[:, c, ts]
                kb.op("act", [self.xb[c][t]], [self.sqb[k]],
                      lambda e, sqv=sqv, xv=xv: e.activation(out=sqv, in_=xv, func=AF.Square))
                kb.op("pe", [self.sqb[k], self.onesb], [statb],
                      lambda e, sqv=sqv, c=c: e.matmul(out=stat[:, :], lhsT=self.ones[:, :], rhs=sqv,
                                                       start=(c == 0), stop=(c == NCH - 1)))
            kb.op("act", [statb, self.epsb], [self.rtb],
                  lambda e: e.activation(out=self.rt[:, :], in_=stat[:, :], func=AF.Ln, bias=self.epsc[:, 0:1]))
            kb.op("act", [self.rtb], [self.rrb],
                  lambda e: e.activation(out=self.rr[:, :], in_=self.rt[:, :], func=AF.Exp, scale=-0.5))
            for c in range(NCH):
                xv = self.x[:, c, ts]
                wv = self.nw[:, nidx * NCH + c: nidx * NCH + c + 1]
                eng_x = "dve"
                if final:
                    kb.op("dve", [self.rrb, self.nwb], [self.xb[c][t]],
                          lambda e, xv=xv, wv=wv: e.scalar_tensor_tensor(
                              out=xv, in0=xv, scalar=wv, in1=self.rr[:, :], op0=ALU.mult, op1=ALU.mult))
                else:
                    ov = self.xn[:, c, ts]
                    kb.op(eng_x, [self.xb[c][t], self.rrb, self.nwb], [self.xnb[c][t]],
                          lambda e, xv=xv, wv=wv, ov=ov: e.scalar_tensor_tensor(
                              out=ov, in0=xv, scalar=wv, in1=self.rr[:, :], op0=ALU.mult, op1=ALU.mult))

    def ffn(self, nidx):
        kb = self.kb
        self.rmsnorm(nidx)
        for (m0, nm) in GROUPS:
            for mi in range(nm):
                wg, wgb = self.wtile()
                wu, wub = self.wtile()
                for t in range(NTB):
                    ts = slice(t * 512, (t + 1) * 512)
                    j = self.gi % 2
                    self.gi += 1
                    pg, pgb = self.pb[2 * j], self.pbb[2 * j]
                    pu, pub = self.pb[2 * j + 1], self.pbb[2 * j + 1]
                    xr = [self.xnb[c][t] for c in range(NCH)]

                    def mm(e, w=wg, p=pg, ts=ts):
                        for k in range(NCH):
                            ins = e.matmul(out=p[:, :], lhsT=w[:, k * 128:(k + 1) * 128], rhs=self.xn[:, k, ts],
                                           start=(k == 0), stop=(k == NCH - 1))
                        return ins
                    kb.op("pe", [wgb] + xr, [pgb], mm)
                    kb.op("pe", [wub] + xr, [pub], lambda e, mm=mm, wu=wu, pu=pu, ts=ts: mm(e, wu, pu, ts))
                    sgv = self.sg[:, j, :]
                    kb.op("act", [pgb], [self.sgb[j]],
                          lambda e, sgv=sgv, pg=pg: e.activation(out=sgv, in_=pg[:, :], func=AF.Silu))
                    av = self.act[:, mi, PAD + t * 512: PAD + (t + 1) * 512]
                    kb.op("dve", [self.sgb[j], pub], [self.actb[mi][t]],
                          lambda e, av=av, sgv=sgv, pu=pu: e.tensor_tensor(out=av, in0=pu[:, :], in1=sgv, op=ALU.mult))
            for c in range(NCH):
                wd, wdb = self.wtile(nm * 128)
                for t in range(NTB):
                    ts = slice(t * 512, (t + 1) * 512)
                    j = self.yi % 2
                    self.yi += 1
                    py, pyb = self.pb[4 + j], self.pbb[4 + j]

                    def mmd(e, wd=wd, py=py, ts=ts, nm=nm):
                        for mi in range(nm):
                            ins = e.matmul(out=py[:, :], lhsT=wd[:, mi * 128:(mi + 1) * 128], rhs=self.act[:, mi, PAD + ts.start: PAD + ts.stop],
                                           start=(mi == 0), stop=(mi == nm - 1))
                        return ins
                    kb.op("pe", [wdb] + [self.actb[mi][t] for mi in range(nm)], [pyb], mmd)
                    xv = self.x[:, c, ts]
                    kb.op("dve", [pyb], [self.xb[c][t]],
                          lambda e, xv=xv, py=py: e.scalar_tensor_tensor(
                              out=xv, in0=py[:, :], scalar=0.5, in1=xv, op0=ALU.mult, op1=ALU.add))

    def convmix(self, layer):
        kb = self.kb
        ci = layer // 2
        self.rmsnorm(layer * 3 + 1)
        o = ci * CPW
        b1a = lambda j: self.cp[:, o + j: o + j + 1]
        b1g = lambda j: self.cp[:, o + 8 + j: o + 8 + j + 1]
        wdw = self.cp[:, o + 16: o + 16 + 248]
        bdw = lambda c: self.cp[:, o + 264 + c: o + 264 + c + 1]
        lnw = lambda c: self.cp[:, o + 272 + c: o + 272 + c + 1]
        lnb = lambda c: self.cp[:, o + 280 + c: o + 280 + c + 1]
        b2 = lambda c: self.cp[:, o + 288 + c: o + 288 + c + 1]
        padb = [self.actb[c][0] for c in range(8)]
        kb.op("dve", [], padb, lambda e: e.memset(self.act[:, :, 0:PAD], 0.0))
        for j in range(8):
            wa, wab = self.wtile()
            wg, wgb = self.wtile()
            for t in range(NTB):
                ts = slice(t * 512, (t + 1) * 512)
                q = self.gi % 2
                self.gi += 1
                pa, pab = self.pb[2 * q], self.pbb[2 * q]
                pg, pgb = self.pb[2 * q + 1], self.pbb[2 * q + 1]
                xr = [self.xnb[c][t] for c in range(NCH)]

                def mm(e, w=wa, p=pa, ts=ts):
                    for k in range(NCH):
                        ins = e.matmul(out=p[:, :], lhsT=w[:, k * 128:(k + 1) * 128], rhs=self.xn[:, k, ts],
                                       start=(k == 0), stop=(k == NCH - 1))
                    return ins
                kb.op("pe", [wab] + xr, [pab], mm)
                kb.op("pe", [wgb] + xr, [pgb], lambda e, mm=mm, wg=wg, pg=pg, ts=ts: mm(e, wg, pg, ts))
                sgv = self.sg[:, q, :]
                kb.op("act", [pgb, self.cpb], [self.sgb[q]],
                      lambda e, sgv=sgv, pg=pg, j=j: e.activation(out=sgv, in_=pg[:, :], func=AF.Sigmoid, bias=b1g(j)))
                av = self.act[:, j, PAD + t * 512: PAD + (t + 1) * 512]
                kb.op("dve", [self.sgb[q], pab, self.cpb], [self.actb[j][t]],
                      lambda e, av=av, sgv=sgv, pa=pa, j=j: e.scalar_tensor_tensor(
                          out=av, in0=pa[:, :], scalar=b1a(j), in1=sgv, op0=ALU.add, op1=ALU.mult))
        for c in range(NCH):
            dsl = self.dgi % 2
            self.dgi += 1
            dg = self.diag[:, dsl, :, :]
            idb = self.ident[:, :].unsqueeze(1).broadcast_to([128, 31, 128])
            wb = wdw[:, c * 31:(c + 1) * 31].unsqueeze(2).broadcast_to([128, 31, 128])
            kb.op("dve", [self.identb, self.cpb], [self.diagb[dsl]],
                  lambda e, dg=dg, idb=idb, wb=wb: e.tensor_tensor(out=dg, in0=idb, in1=wb, op=ALU.mult))
            for t in range(NTB):
                ts = slice(t * 512, (t + 1) * 512)
                q = self.yi % 2
                self.yi += 1
                py, pyb = self.pb[4 + q], self.pbb[4 + q]

                def mmc(e, py=py, c=c, t=t, dsl=dsl):
                    for k in range(31):
                        st0 = PAD + t * 512 - 30 + k
                        ins = e.matmul(out=py[:, :], lhsT=self.diag[:, dsl, k, :], rhs=self.act[:, c, st0: st0 + 512],
                                       start=(k == 0), stop=(k == 30))
                    return ins
                rd = [self.diagb[dsl], self.actb[c][t]] + ([self.actb[c][t - 1]] if t > 0 else [])
                kb.op("pe", rd, [pyb], mmc)
                ov = self.xn[:, c, ts]
                kb.op("act", [pyb, self.cpb], [self.xnb[c][t]],
                      lambda e, ov=ov, py=py, c=c: e.activation(out=ov, in_=py[:, :], func=AF.Identity, bias=bdw(c)))
        msq, msqb = self.pb[6], self.pbb[6]
        mean, meanb = self.pb[7], self.pbb[7]
        for t in range(NTB):
            ts = slice(t * 512, (t + 1) * 512)
            for c in range(NCH):
                k = self.sqi % 4
                self.sqi += 1
                sqv = self.sq[:, k, :]
                cv = self.xn[:, c, ts]
                kb.op("act", [self.xnb[c][t]], [self.sqb[k]],
                      lambda e, sqv=sqv, cv=cv: e.activation(out=sqv, in_=cv, func=AF.Square))
                kb.op("pe", [self.sqb[k], self.onesb], [msqb],
                      lambda e, sqv=sqv, c=c: e.matmul(out=msq[:, :], lhsT=self.ones[:, :], rhs=sqv,
                                                       start=(c == 0), stop=(c == NCH - 1)))
                kb.op("pe", [self.xnb[c][t], self.onesb], [meanb],
                      lambda e, cv=cv, c=c: e.matmul(out=mean[:, :], lhsT=self.ones[:, :], rhs=cv,
                                                     start=(c == 0), stop=(c == NCH - 1)))
            kb.op("act", [meanb], [self.tmpab],
                  lambda e: e.activation(out=self.tmpa[:, :], in_=mean[:, :], func=AF.Square))
            kb.op("dve", [msqb, self.tmpab], [self.tmpab],
                  lambda e: e.tensor_tensor(out=self.tmpa[:, :], in0=msq[:, :], in1=self.tmpa[:, :], op=ALU.subtract))
            kb.op("act", [self.tmpab, self.epsb], [self.rtb],
                  lambda e: e.activation(out=self.rt[:, :], in_=self.tmpa[:, :], func=AF.Ln, bias=self.epsc[:, 0:1]))
            kb.op("act", [self.rtb], [self.rrb],
                  lambda e: e.activation(out=self.rr[:, :], in_=self.rt[:, :], func=AF.Exp, scale=-0.5))
            for c in range(NCH):
                k = self.tbi % 2
                self.tbi += 1
                tv = self.tmpb[:, k, :]
                cv = self.xn[:, c, ts]
                kb.op("dve", [self.xnb[c][t], meanb], [self.tmpbb[k]],
                      lambda e, tv=tv, cv=cv: e.tensor_tensor(out=tv, in0=cv, in1=mean[:, :], op=ALU.subtract))
                kb.op("dve", [self.tmpbb[k], self.rrb], [self.tmpbb[k]],
                      lambda e, tv=tv: e.tensor_tensor(out=tv, in0=tv, in1=self.rr[:, :], op=ALU.mult))
                av = self.act[:, c, PAD + t * 512: PAD + (t + 1) * 512]
                kb.op("act", [self.tmpbb[k], self.cpb], [self.actb[c][t]],
                      lambda e, av=av, tv=tv, c=c: e.activation(out=av, in_=tv, func=AF.Silu, bias=lnb(c), scale=lnw(c)))
        for c in range(NCH):
            w2, w2b = self.wtile()
            for t in range(NTB):
                ts = slice(t * 512, (t + 1) * 512)
                q = self.yi % 2
                self.yi += 1
                py, pyb = self.pb[4 + q], self.pbb[4 + q]

                def mm2(e, w2=w2, py=py, t=t):
                    for k in range(NCH):
                        ins = e.matmul(out=py[:, :], lhsT=w2[:, k * 128:(k + 1) * 128],
                                       rhs=self.act[:, k, PAD + t * 512: PAD + (t + 1) * 512],
                                       start=(k == 0), stop=(k == NCH - 1))
                    return ins
                kb.op("pe", [w2b] + [self.actb[k][t] for k in range(NCH)], [pyb], mm2)
                xv = self.x[:, c, ts]
                kb.op("dve", [pyb, self.cpb], [self.xb[c][t]],
                      lambda e, xv=xv, py=py, c=c: e.scalar_tensor_tensor(
                          out=xv, in0=py[:, :], scalar=b2(c), in1=xv, op0=ALU.add, op1=ALU.add))
        kb.fence()

    def evenmix(self, layer):
        kb = self.kb
        e_i = layer // 2
        self.rmsnorm(layer * 3 + 1)
        kb.fence()
        A = self.arena
        o0 = NCH * T
        qa = A[:, o0:o0 + 4 * T].rearrange("p (c t) -> p c t", c=4)
        ka = A[:, o0 + 4 * T:o0 + 6 * T].rearrange("p (h t) -> p h t", h=2)
        vax = A[:, o0 + 6 * T:o0 + 6 * T + 16 * 256].rearrange("p (n k j) -> p n k j", n=16, k=2)
        onx = A[:, o0 + 6 * T + 4096:o0 + 6 * T + 4096 + 256].rearrange("p (k j) -> p k j", k=2)
        qab = [[Buf() for _ in range(NTB)] for _ in range(4)]
        kab = [Buf() for _ in range(NTB)]
        vab = [Buf() for _ in range(4)]
        onxb = Buf()
        eo = e_i * EPW
        if "att" in EVEN_PARTS:
            self.attention(qa, ka, vax, onx, qab, kab, vab, onxb, eo)
        else:
            for c in range(4):
                for n in range(16):
                    kb.op("dve", [], [self.mixb[c][n]],
                          lambda e, c=c, n=n: e.memset(self.mix[:, c, n * 128:(n + 1) * 128], 0.0))
        kb.fence()
        if "dn" in EVEN_PARTS:
            self.deltanet(layer, eo)
        else:
            for c in range(4, 8):
                for n in range(16):
                    kb.op("dve", [], [self.mixb[c][n]],
                          lambda e, c=c, n=n: e.memset(self.mix[:, c, n * 128:(n + 1) * 128], 0.0))
        kb.fence()
        for c in range(NCH):
            w2, w2b = self.wtile()
            for t in range(NTB):
                ts = slice(t * 512, (t + 1) * 512)
                q = self.yi % 2
                self.yi += 1
                py, pyb = self.pb[4 + q], self.pbb[4 + q]

                def mm2(e, w2=w2, py=py, ts=ts):
                    for k in range(NCH):
                        ins = e.matmul(out=py[:, :], lhsT=w2[:, k * 128:(k + 1) * 128], rhs=self.mix[:, k, ts],
                                       start=(k == 0), stop=(k == NCH - 1))
                    return ins
                rd = [w2b] + [self.mixb[k][4 * t + i] for k in range(NCH) for i in range(4)]
                kb.op("pe", rd, [pyb], mm2)
                xv = self.x[:, c, ts]
                kb.op("dve", [pyb], [self.xb[c][t]],
                      lambda e, xv=xv, py=py: e.tensor_tensor(out=xv, in0=py[:, :], in1=xv, op=ALU.add))
        kb.fence()

    def attention(self, qa, ka, vax, onx, qab, kab, vab, onxb, eo):
        kb = self.kb
        kb.op("dve", [], vab, lambda e: e.memset(vax, 0.0))
        kb.op("dve", [], kab, lambda e: e.memset(ka, 0.0))
        kb.op("dve", [], [onxb], lambda e: e.memset(onx, 0.0))
        kb.op("dve", [onxb], [onxb], lambda e: e.memset(onx[:, 0, 0:64], 1.0))
        kb.op("dve", [onxb], [onxb], lambda e: e.memset(onx[:, 1, 64:128], 1.0))
        sk = self.sk
        kb.op("act", [self.epb], [self.skb],
              lambda e: e.activation(out=sk[:, :], in_=self.ep[:, eo:eo + 4], func=AF.Exp))
        for c in range(5):
            w, wb = self.wtile()
            for t in range(NTB):
                ts = slice(t * 512, (t + 1) * 512)
                q = self.gi % 2
                self.gi += 1
                p, pbf = self.pb[2 * q], self.pbb[2 * q]

                def mm(e, w=w, p=p, ts=ts):
                    for k in range(NCH):
                        ins = e.matmul(out=p[:, :], lhsT=w[:, k * 128:(k + 1) * 128], rhs=self.xn[:, k, ts],
                                       start=(k == 0), stop=(k == NCH - 1))
                    return ins
                kb.op("pe", [wb] + [self.xnb[k][t] for k in range(NCH)], [pbf], mm)
                if c < 4:
                    ov, ob = qa[:, c, ts], qab[c][t]
                    kb.op("act", [pbf], [ob], lambda e, ov=ov, p=p: e.activation(out=ov, in_=p[:, :], func=AF.Copy))
                else:
                    for h in range(2):
                        kb.op("act", [pbf], [kab[t]], lambda e, h=h, p=p, ts=ts: e.activation(
                            out=ka[h * 64:(h + 1) * 64, h, ts], in_=p[h * 64:(h + 1) * 64, :], func=AF.Copy))
        wv, wvb = self.wtile()
        for t in range(NTB):
            q = self.gi % 2
            self.gi += 1
            p, pbf = self.pb[2 * q], self.pbb[2 * q]

            def mmv(e, p=p, t=t):
                for i in range(4):
                    n = 4 * t + i
                    for k in range(NCH):
                        ins = e.matmul(out=p[:, i * 128:(i + 1) * 128], lhsT=self.xn[:, k, n * 128:(n + 1) * 128],
                                       rhs=wv[:, k * 128:(k + 1) * 128], start=(k == 0), stop=(k == NCH - 1))
                return ins
            kb.op("pe", [wvb] + [self.xnb[k][t] for k in range(NCH)], [pbf], mmv)
            pv = p[:, :].rearrange("p (n j) -> p n j", n=4)
            kb.op("act", [pbf], [vab[t]],
                  lambda e, pv=pv, t=t: e.activation(out=vax[:, 4 * t:4 * t + 4, 0, 0:64], in_=pv[:, :, 0:64], func=AF.Copy))
            kb.op("act", [pbf], [vab[t]],
                  lambda e, pv=pv, t=t: e.activation(out=vax[:, 4 * t:4 * t + 4, 1, 64:128], in_=pv[:, :, 64:128], func=AF.Copy))
        import os
        if os.environ.get("ATT_STOP") == "3":
            for c in range(4):
                for n in range(16):
                    kb.op("dve", [], [self.mixb[c][n]],
                          lambda e, c=c, n=n: e.memset(self.mix[:, c, n * 128:(n + 1) * 128], 0.0))
            return
        d0 = self.cst[:, 0:256]
        hb = (lambda h: 0) if os.environ.get("ATT_NOB64") else (lambda h: h)
        def att_iter(n, c, t, qs, ks, nk):
            q = self.gi % 2
            self.gi += 1
            p, pbf = self.pb[2 * q], self.pbb[2 * q]
            po, pob = self.pb[2 * q + 1], self.pbb[2 * q + 1]

            def mms(e, p=p, c=c, qs=qs, ks=ks, nk=nk):
                for h in range(2):
                    for kk in range(nk):
                        ins = e.matmul(out=p[:, h * 256 + kk * 128:h * 256 + (kk + 1) * 128],
                                       lhsT=ka[:, h, ks[kk]], rhs=qa[:, c, qs],
                                       start=True, stop=True)
                return ins
            rd = [qab[c][t], kab[t]] + ([kab[(n - 1) // 4]] if n > 0 else [])
            kb.op("pe", rd, [pbf], mms)
            w = 128 * nk
            si = self.sgi2 % 2
            self.sgi2 += 1
            tm = self.sg[:, si, :]
            tmb = self.sgb[si]
            k = self.sqi % 4
            self.sqi += 1
            pt = self.sq[:, k, :]
            ptb = self.sqb[k]
            for h in range(2):
                sl = 8.0 * SLOPES[c + 4 * h]
                kb.op("dve", [pbf, self.cstb], [tmb],
                      lambda e, h=h, sl=sl, p=p, tm=tm, w=w: e.scalar_tensor_tensor(
                          out=tm[:, h * 256:h * 256 + w], in0=p[:, h * 256:h * 256 + w], scalar=1.0 / sl, in1=d0[:, 0:w],
                          op0=ALU.mult, op1=ALU.add))
                kb.op("act", [tmb], [ptb], lambda e, h=h, sl=sl, pt=pt, tm=tm, w=w: e.activation(
                    out=pt[:, h * 256:h * 256 + w], in_=tm[:, h * 256:h * 256 + w], func=AF.Exp, scale=0.125 * sl))

            yield "A"
            if os.environ.get("ATT_STOP") == "4":
                kb.op("dve", [ptb], [self.mixb[c][n]],
                      lambda e, c=c, qs=qs: e.memset(self.mix[:, c, qs], 0.0))
                return

            def mmo(e, po=po, pt=pt, n=n, nk=nk):
                for part, (lh, off) in enumerate(((vax, 0), (onx, 128))):
                    cnt = 0
                    for h in range(2):
                        for kk in range(nk):
                            lhsT = vax[:, n - kk, h, :] if part == 0 else onx[:, h, :]
                            ins = e.matmul(out=po[:, off:off + 128], lhsT=lhsT,
                                           rhs=pt[:, h * 256 + kk * 128:h * 256 + (kk + 1) * 128],
                                           start=(cnt == 0), stop=(cnt == 2 * nk - 1))
                            cnt += 1
                return ins
            rd = [ptb, onxb, vab[t]] + ([vab[(n - 1) // 4]] if n > 0 else [])
            kb.op("pe", rd, [pob], mmo)
            if os.environ.get("ATT_STOP") == "5":
                kb.op("dve", [pob], [self.mixb[c][n]],
                      lambda e, c=c, qs=qs: e.memset(self.mix[:, c, qs], 0.0))
                return
            dn = self.den
            kb.op("dve", [pob, self.skb], [self.denb],
                  lambda e, po=po, c=c: e.tensor_scalar_add(out=dn[:, :], in0=po[:, 128:256], scalar1=sk[:, c:c + 1]))
            kb.op("dve", [self.denb], [self.denb], lambda e: e.reciprocal(out=dn[:, :], in_=dn[:, :]))
            kb.op("dve", [pob, self.denb], [self.mixb[c][n]],
                  lambda e, po=po, c=c, qs=qs: e.tensor_tensor(out=self.mix[:, c, qs], in0=po[:, 0:128], in1=dn[:, :],
                                                               op=ALU.mult))

        prev = None
        for n in range(16):
            t = n // 4
            qs = slice(n * 128, (n + 1) * 128)
            ks = [slice(n * 128, (n + 1) * 128), slice((n - 1) * 128, n * 128)]
            nk = 1 if n == 0 else 2
            for c in range(4):
                g = att_iter(n, c, t, qs, ks, nk)
                next(g, None)
                if prev is not None:
                    for _ in prev:
                        pass
                prev = g
        for _ in prev:
            pass

    def deltanet(self, layer, eo):
        kb = self.kb
        self.dbg_list = []
        A = self.arena
        pos = [0]

        def alloc(n_bf16, dt=BF16):
            a0 = pos[0]
            self.dbg_list.append((a0, n_bf16, dt == F32))
            pos[0] += (n_bf16 + 15) // 16 * 16
            assert pos[0] <= ARENA, ("arena overflow", pos[0], ARENA)
            self.arena_used = pos[0]
            v = A[:, a0:a0 + n_bf16]
            return v.bitcast(F32) if dt == F32 else v

        G = GT
        xg = alloc(8 * G).rearrange("p (c t) -> p c t", c=8)
        cin = alloc(12 * (G + 4)).rearrange("p (c t) -> p c t", c=12)
        ctmp = alloc(2 * 2 * G, F32).rearrange("p (a t) -> p a t", a=2)
        qT = alloc(4 * G).rearrange("p (c t) -> p c t", c=4)
        kT = alloc(4 * G).rearrange("p (c t) -> p c t", c=4)
        vT = alloc(4 * G).rearrange("p (c t) -> p c t", c=4)
        kTz = alloc(2 * 4 * G).rearrange("p (b c t) -> p b c t", b=2, c=4)
        k_tm = alloc(NCK * 512).rearrange("p (c f) -> p c f", c=NCK)
        v_tm = alloc(NCK * 512).rearrange("p (c f) -> p c f", c=NCK)
        zs = alloc(NCK * 512).rearrange("p (c f) -> p c f", c=NCK)
        ba = alloc(2 * NCK * 16, F32).rearrange("p (c f) -> p c f", c=NCK)
        beta = alloc(2 * NCK * 8, F32).rearrange("p (c f) -> p c f", c=NCK)
        gt = alloc(2 * NCK * 8, F32).rearrange("p (c f) -> p c f", c=NCK)
        gtmp = alloc(2 * NCK * 8, F32).rearrange("p (c f) -> p c f", c=NCK)
        nA = alloc(2 * 8, F32)
        bd16 = alloc(128)
        ones64 = self.ones64[:, :]
        onec = alloc(2 * 1, F32)
        rhsG = alloc(2 * 512, F32).rearrange("p (h j) -> p h j", h=8)
        rhsB = alloc(2 * 512, F32).rearrange("p (h j) -> p h j", h=8)
        gcs = alloc(2 * 8, F32)
        sc = alloc(2 * 8, F32)
        kdsc = alloc(2 * 8, F32)
        dT = alloc(2 * 512, F32).rearrange("p (h j) -> p h j", h=8)
        t1 = alloc(2 * 512, F32).rearrange("p (h j) -> p h j", h=8)
        eg = alloc(2 * 512, F32).rearrange("p (h j) -> p h j", h=8)
        u_sb = alloc(2 * 512, F32).rearrange("p (h j) -> p h j", h=8)
        o_sb = alloc(2 * 512, F32).rearrange("p (h j) -> p h j", h=8)
        S32 = alloc(2 * 512, F32).rearrange("p (c j) -> p c j", c=4)
        tmpS = t1.rearrange("p h j -> p (h j)").rearrange("p (c j) -> p c j", c=4)
        gl = alloc(2 * 4, F32)
        ss8 = alloc(2 * 8, F32)
        Lt = alloc(512).rearrange("p (h j) -> p h j", h=8)
        Tm = alloc(512).rearrange("p (h j) -> p h j", h=8)
        Tt = alloc(512).rearrange("p (h j) -> p h j", h=8)
        Ym = alloc(512).rearrange("p (h j) -> p h j", h=8)
        attnT = alloc(512).rearrange("p (h j) -> p h j", h=8)
        vb = alloc(512).rearrange("p (h j) -> p h j", h=8)
        kdec = alloc(512).rearrange("p (h j) -> p h j", h=8)
        vnew = alloc(512).rearrange("p (h j) -> p h j", h=8)
        ogb = alloc(512).rearrange("p (h j) -> p h j", h=8)
        kbgx = alloc(1024).rearrange("p (h j) -> p h j", h=8)
        wT = alloc(256).rearrange("p (c j) -> p c j", c=4)
        qd = alloc(256).rearrange("p (c j) -> p c j", c=4)
        Sbf = alloc(512).rearrange("p (c j) -> p c j", c=4)
        B = {n: Buf() for n in ("xg", "ctmp0", "ctmp1", "qT", "kT", "vT", "k_tm", "v_tm", "zs", "ba", "beta", "gt", "gtmp",
                                "nA", "bd16", "ones64", "onec", "rhsG", "rhsB", "gcs", "sc", "kdsc", "dT", "t1", "eg", "u",
                                "o", "S32", "tmpS", "gl", "ss8", "Lt", "T", "Tt", "Y", "attnT", "vb", "kdec", "vnew", "ogb",
                                "kbgx", "wT", "qd", "Sbf", "sqd", "rtd", "rsd")}
        cinb = [Buf() for _ in range(12)]
        bank = [0]

        def nb():
            i = bank[0] % 6
            bank[0] += 1
            return self.pb[i], self.pbb[i]

        cstb, epb = self.cstb, self.epb
        U64 = self.cst[0:64, 256:320]
        NEGT = self.cst[0:64, 320:384]
        LM = lambda bi: self.cst[0:64, 448 + bi * 64:448 + (bi + 1) * 64]
        NLM = lambda bi: self.cst[0:64, 960 + bi * 64:960 + (bi + 1) * 64]
        BDM = self.cst[:, 832:960]
        I64f = self.identf[0:64, 0:64]
        I64b = self.ident[0:64, 0:64]
        cw = lambda fc, k: self.ep[:, eo + 4 + fc * 4 + k:eo + 4 + fc * 4 + k + 1]
        alog = self.ep[0:64, eo + 52:eo + 60]
        dtb = self.ep[0:64, eo + 60:eo + 68]
        nwd = self.ep[0:64, eo + 68:eo + 132]
        V, Aop = "dve", "act"
        PL = "dve" if os.environ.get("DN_NOPOOL") else "pool"

        def bc(ap, shape):
            return ap.broadcast_to(shape)

        kb.op(V, [], cinb, lambda e: e.memset(cin, 0.0))
        kb.op(V, [], [B["kT"]], lambda e: e.memset(kTz, 0.0))
        kb.op(V, [], [B["S32"]], lambda e: e.memset(S32, 0.0))
        kb.op(V, [], [B["Sbf"]], lambda e: e.memset(Sbf, 0.0))
        kb.op(V, [], [B["kbgx"]], lambda e: e.memset(kbgx, 0.0))
        kb.op(V, [], [B["ones64"]], lambda e: e.memset(ones64, 1.0))
        kb.op(V, [], [B["onec"]], lambda e: e.memset(onec, 1.0))
        kb.op(V, [cstb], [B["bd16"]], lambda e: e.tensor_copy(out=bd16, in_=BDM))
        kb.op(Aop, [epb], [B["nA"]], lambda e: e.activation(out=nA[0:64, :], in_=alog, func=AF.Exp))
        kb.op(V, [B["nA"]], [B["nA"]], lambda e: e.tensor_scalar_mul(out=nA[0:64, :], in0=nA[0:64, :], scalar1=-1.0))
        sqd = self.sq[:, 0, 0:G]
        rtd = self.rt[:, 0:G]
        rsd = self.rr[:, 0:G]
        nidx = layer * 3 + 1
        tpb = self.pb[7][:, :].bitcast(BF16)
        tpbb = self.pbb[7]
        stat, statb = self.pb[6], self.pbb[6]


        for _nm in ("rhsG", "rhsB", "gcs", "sc", "kdsc", "dT", "t1", "eg", "Lt", "T", "Tt", "Y"):
            B[_nm + "0"] = Buf()
            B[_nm + "1"] = Buf()

        def BB(nm):
            return [B[nm + "0"], B[nm + "1"]]
        SH = [64, 4, 64]

        def v3h(ap):
            return ap.rearrange("p (h j) -> p h j", h=4)
        bkH = [[0], [0]]

        def mk_nb(hh):
            def f():
                i = 2 * hh + bkH[hh][0] % 2
                bkH[hh][0] += 1
                return self.pb[i], self.pbb[i]
            return f
        nbPh = [mk_nb(0), mk_nb(1)]
        B["t1r"] = Buf()
        t1r = self.sg[:, 0, :].rearrange("p (h j) -> p h j", h=8)
        tmpSr = self.sg[:, 0, :].rearrange("p (c j) -> p c j", c=4)
        bkP, bkR = [0], [0]

        def nbP():
            i = bkP[0] % 4
            bkP[0] += 1
            return self.pb[i], self.pbb[i]

        def nbR():
            i = 4 + bkR[0] % 2
            bkR[0] += 1
            return self.pb[i], self.pbb[i]
        B["rtd"] = self.rtb
        B["rsd"] = self.rrb
        B["sqd"] = self.sqb[0]
        AX = mybir.AxisListType.X
        S8 = [64, 8, 64]

        def v3(ap):
            return ap.rearrange("p (h j) -> p h j", h=8)

        def chunk_pa(n, ch, cs, hh):
            nb = nbPh[hh]
            h0 = 4 * hh
            hs = slice(h0, h0 + 4)
            gch = gt[0:64, ch, hs]
            bch = beta[0:64, ch, hs]
            kb.op(PL, [cstb, B["gt"]], [B["rhsG%d" % hh]], lambda e: e.tensor_tensor(
                out=rhsG[0:64, hs], in0=bc(U64.unsqueeze(1), SH), in1=bc(gch.unsqueeze(2), SH), op=ALU.mult))
            kb.op(PL, [self.identb, B["beta"]], [B["rhsB%d" % hh]], lambda e: e.tensor_tensor(
                out=rhsB[0:64, hs], in0=bc(I64f.unsqueeze(1), SH), in1=bc(bch.unsqueeze(2), SH), op=ALU.mult))
            pG, pGb = nb()
            kb.op("pe", [B["ones64"], B["rhsG%d" % hh]], [pGb], lambda e: e.matmul(
                out=pG[:, 0:256], lhsT=ones64[0:64, :], rhs=rhsG[0:64, hs].rearrange("p h j -> p (h j)"), start=True, stop=True))
            pC, pCb = nb()
            kb.op("pe", [cstb, B["gt"]], [pCb], lambda e: e.matmul(out=pC[0:64, 0:4], lhsT=U64, rhs=gch, start=True, stop=True))
            kb.op(V, [pCb], [B["gcs%d" % hh]], lambda e: e.tensor_copy(out=gcs[0:64, hs], in_=pC[0:64, 0:4]))
            pG3 = v3h(pG[0:64, 0:256])
            kb.op(V, [pGb, B["gcs%d" % hh]], [B["t1%d" % hh]], lambda e: e.tensor_tensor(
                out=t1[0:64, hs], in0=pG3, in1=bc(gcs[0:64, hs].unsqueeze(2), SH), op=ALU.subtract))
            kb.op(V, [B["t1%d" % hh], cstb], [B["t1%d" % hh]], lambda e: e.tensor_tensor(
                out=t1[0:64, hs], in0=t1[0:64, hs], in1=bc(NEGT.unsqueeze(1), SH), op=ALU.add))
            kb.op(Aop, [B["t1%d" % hh]], [B["dT%d" % hh]], lambda e: e.activation(out=dT[0:64, hs], in_=t1[0:64, hs], func=AF.Exp))
            kb.op(Aop, [pGb], [B["eg%d" % hh]], lambda e: e.activation(
                out=eg[:, hs].rearrange("p h j -> p (h j)"), in_=pG[:, 0:256], func=AF.Exp))
            kb.op(V, [pGb, B["gcs%d" % hh]], [B["kdsc%d" % hh]], lambda e: e.tensor_tensor(
                out=kdsc[0:64, hs], in0=pG3[:, :, 63], in1=gcs[0:64, hs], op=ALU.subtract))
            kb.op(Aop, [B["kdsc%d" % hh]], [B["kdsc%d" % hh]], lambda e: e.activation(out=kdsc[0:64, hs], in_=kdsc[0:64, hs], func=AF.Exp))
            kb.op(Aop, [B["gcs%d" % hh]], [B["sc%d" % hh]], lambda e: e.activation(out=sc[0:64, hs], in_=gcs[0:64, hs], func=AF.Exp))
            kb.op(V, [B["sc%d" % hh], B["beta"]], [B["sc%d" % hh]], lambda e: e.tensor_tensor(
                out=sc[0:64, hs], in0=sc[0:64, hs], in1=bch, op=ALU.mult))
            yield "PA"
            pB, pBb = nb()
            kb.op("pe", [B["ones64"], B["rhsB%d" % hh]], [pBb], lambda e: e.matmul(
                out=pB[0:64, 0:256], lhsT=ones64[0:64, 0:64], rhs=rhsB[0:64, hs].rearrange("p h j -> p (h j)"), start=True, stop=True))
            pK, pKb = nb()

            def mmk(e, p=pK, other=kT):
                for h in range(h0, h0 + 4):
                    hb, pr = h % 2, h // 2
                    rhs = kTz[:, hb, pr, cs] if other is kT else other[:, pr, cs]
                    ins = e.matmul(out=p[0:64, (h - h0) * 64:(h - h0 + 1) * 64], lhsT=kTz[:, hb, pr, cs], rhs=rhs, start=True, stop=True)
                return ins
            kb.op("pe", [B["kT"]], [pKb], mmk)
            kb.op(V, [B["dT%d" % hh], self.identb], [B["t1%d" % hh]], lambda e: e.tensor_tensor(
                out=t1[0:64, hs], in0=dT[0:64, hs], in1=bc(I64f.unsqueeze(1), SH), op=ALU.subtract))
            kb.op(V, [pKb, B["t1%d" % hh]], [B["t1%d" % hh]], lambda e: e.tensor_tensor(
                out=t1[0:64, hs], in0=v3h(pK[0:64, 0:256]), in1=t1[0:64, hs], op=ALU.mult))
            kb.op(V, [pBb, B["t1%d" % hh]], [B["Lt%d" % hh]], lambda e: e.tensor_tensor(
                out=Lt[0:64, hs], in0=v3h(pB[0:64, 0:256]), in1=t1[0:64, hs], op=ALU.mult))
            yield "PA"
            for bi in range(6):
                first, last = bi == 0, bi == 5
                bsz = 1 << bi
                nk = 64 // (2 * bsz)

                def half(ap3, t, bsz=bsz):
                    return ap3.rearrange("p h (k t r) -> p h k t r", t=2, r=bsz)[:, :, :, t, :]

                def half2(ap2, t, bsz=bsz):
                    return ap2.rearrange("p (k t r) -> p k t r", t=2, r=bsz)[:, :, t, :]

                def cmp3(p, bsz=bsz, nk=nk):
                    return p[0:64, 0:128].rearrange("p (h k r) -> p h k r", h=4, r=bsz)
                pY, pYb = nb()

                def mmy(e, pY=pY, first=first):
                    for h in range(h0, h0 + 4):
                        ins = e.matmul(out=pY[0:64, (h - h0) * 64:(h - h0 + 1) * 64], lhsT=Lt[0:64, h, :],
                                       rhs=(I64b if first else Tm[0:64, h, :]), start=True, stop=True)
                    return ins
                kb.op("pe", [B["Lt%d" % hh], self.identb] + ([] if first else [B["T%d" % hh]]), [pYb], mmy)
                kb.op(V, [pYb, cstb], [B["Y%d" % hh]], lambda e, pY=pY, bi=bi: e.tensor_tensor(
                    out=Ym[0:64, hs], in0=v3h(pY[0:64, 0:256]), in1=bc(NLM(bi).unsqueeze(1), SH), op=ALU.mult))
                yield "PA"
                if not last:
                    pZ, pZb = nb()

                    def mmz1(e, pZ=pZ, first=first, half2=half2, bsz=bsz, nk=nk):
                        for h in range(h0, h0 + 4):
                            o_ = pZ[0:64, (h - h0) * 32:(h - h0 + 1) * 32].rearrange("p (k r) -> p k r", r=bsz)
                            e.matmul(out=o_, lhsT=(I64b if first else Tt[0:64, h, :]), rhs=half2(Ym[0:64, h, :], 0),
                                     start=True, stop=False)
                            ins = e.matmul(out=o_, lhsT=I64b, rhs=half2((I64b if first else Tm[0:64, h, :]), 0),
                                           start=False, stop=True)
                        return ins
                    kb.op("pe", [B["Y%d" % hh], self.identb] + ([] if first else [B["Tt%d" % hh], B["T%d" % hh]]), [pZb], mmz1)
                pZ2, pZ2b = nb()

                def mmz2(e, pZ2=pZ2, first=first, half2=half2, bsz=bsz):
                    for h in range(h0, h0 + 4):
                        o_ = pZ2[0:64, (h - h0) * 32:(h - h0 + 1) * 32].rearrange("p (k r) -> p k r", r=bsz)
                        ins = e.matmul(out=o_, lhsT=Ym[0:64, h, :], rhs=half2((I64b if first else Tt[0:64, h, :]), 1),
                                       start=True, stop=True)
                    return ins
                kb.op("pe", [B["Y%d" % hh], self.identb] + ([] if first else [B["Tt%d" % hh]]), [pZ2b], mmz2)
                if first:
                    kb.op(V, [self.identb], [B["T%d" % hh]], lambda e: e.tensor_copy(out=Tm[0:64, hs], in_=bc(I64b.unsqueeze(1), SH)))
                    kb.op(V, [self.identb], [B["Tt%d" % hh]], lambda e: e.tensor_copy(out=Tt[0:64, hs], in_=bc(I64b.unsqueeze(1), SH)))
                if not last:
                    kb.op(Aop, [pZb], [B["T%d" % hh]], lambda e, pZ=pZ, half=half, cmp3=cmp3: e.activation(
                        out=half(Tm[0:64, hs], 0), in_=cmp3(pZ), func=AF.Copy))
                kb.op(V, [pZ2b, B["Tt%d" % hh]], [B["Tt%d" % hh]], lambda e, pZ2=pZ2, half=half, cmp3=cmp3: e.tensor_tensor(
                    out=half(Tt[0:64, hs], 1), in0=half(Tt[0:64, hs], 1), in1=cmp3(pZ2), op=ALU.add))
                yield "PA"
            yield "PAend"

        def chunk_rest(n, ch, cs):
            nb = nbPh[0]
            gch = gt[0:64, ch, :]
            bch = beta[0:64, ch, :]

            def mmk(e, p, other):
                for h in range(8):
                    hb, pr = h % 2, h // 2
                    ins = e.matmul(out=p[0:64, h * 64:(h + 1) * 64], lhsT=kTz[:, hb, pr, cs], rhs=other[:, pr, cs], start=True, stop=True)
                return ins
            pQ, pQb = nb()
            kb.op("pe", [B["kT"], B["qT"]], [pQb], lambda e: mmk(e, pQ, qT))
            kb.op(V, [pQb, *BB("dT")], [B["attnT"]], lambda e: e.tensor_tensor(
                out=attnT[0:64], in0=v3(pQ[0:64, :]), in1=dT[0:64], op=ALU.mult))
            k8 = v3(k_tm[0:64, ch, :])
            kb.op(PL, [B["v_tm"], B["beta"]], [B["vb"]], lambda e: e.tensor_tensor(
                out=vb[0:64], in0=v3(v_tm[0:64, ch, :]), in1=bc(bch.unsqueeze(2), S8), op=ALU.mult))
            kx5 = kbgx[0:64].rearrange("p (pr hb) (hf d) -> p pr hb hf d", hb=2, hf=2)
            k4 = k_tm[0:64, ch, :].rearrange("p (pr hb d) -> p pr hb d", pr=4, hb=2)
            sc4 = sc[0:64, :].rearrange("p (pr hb) -> p pr hb", hb=2)
            for hb in range(2):
                kb.op(PL, [B["k_tm"], *BB("sc")], [B["kbgx"]], lambda e, hb=hb: e.tensor_tensor(
                    out=kx5[:, :, hb, hb, :], in0=k4[:, :, hb, :], in1=bc(sc4[:, :, hb].unsqueeze(2), [64, 4, 64]), op=ALU.mult))
            kb.op(PL, [B["k_tm"], *BB("kdsc")], [B["kdec"]], lambda e: e.tensor_tensor(
                out=kdec[0:64], in0=k8, in1=bc(kdsc[0:64, :].unsqueeze(2), S8), op=ALU.mult))
            pU, pUb = nb()

            def mmu(e):
                for h in range(8):
                    ins = e.matmul(out=pU[0:64, h * 64:(h + 1) * 64], lhsT=Tt[0:64, h, :], rhs=vb[0:64, h, :], start=True, stop=True)
                return ins
            kb.op("pe", [*BB("Tt"), B["vb"]], [pUb], mmu)
            kb.op(Aop, [pUb], [B["u"]], lambda e: e.activation(
                out=u_sb[0:64].rearrange("p h j -> p (h j)"), in_=pU[0:64, :], func=AF.Copy))
            pW, pWb = nb()

            def mmw(e):
                for h in range(8):
                    pr, hb = h // 2, h % 2
                    ins = e.matmul(out=pW[:, pr * 64:(pr + 1) * 64], lhsT=kbgx[0:64, h, :], rhs=Tt[0:64, h, :],
                                   start=(hb == 0), stop=(hb == 1))
                return ins
            kb.op("pe", [*BB("Tt"), B["kbgx"]], [pWb], mmw)
            kb.op(Aop, [pWb], [B["wT"]], lambda e: e.activation(
                out=wT.rearrange("p c j -> p (c j)"), in_=pW[:, 0:256], func=AF.Copy))
            eg5 = eg.rearrange("p (pr hb) j -> p pr hb j", hb=2)
            for hb in range(2):
                ps_ = slice(hb * 64, (hb + 1) * 64)
                kb.op(PL, [B["qT"], *BB("eg")], [B["qd"]], lambda e, hb=hb, ps_=ps_: e.tensor_tensor(
                    out=qd[ps_, :, :], in0=qT[ps_, :, cs], in1=eg5[ps_, :, hb, :], op=ALU.mult))
                kb.op(PL, [*BB("eg")], [B["gl"]], lambda e, hb=hb, ps_=ps_: e.tensor_copy(
                    out=gl[ps_, :], in_=eg5[ps_, :, hb, 63]))
            yield "Rstart"
            nb = nbR
            pA, pAb = nb()

            def mma(e):
                for pr in range(4):
                    ins = e.matmul(out=pA[0:64, pr * 128:(pr + 1) * 128], lhsT=wT[:, pr, :], rhs=Sbf[:, pr, :], start=True, stop=True)
                return ins
            kb.op("pe", [B["wT"], B["Sbf"]], [pAb], mma)
            kb.op(V, [B["u"], pAb], [B["vnew"]], lambda e: e.tensor_tensor(
                out=vnew[0:64], in0=u_sb[0:64], in1=v3(pA[0:64, :]), op=ALU.subtract))
            yield "R"
            pO, pOb = nb()

            def mmo(e):
                for pr in range(4):
                    for hb in range(2):
                        h = 2 * pr + hb
                        e.matmul(out=pO[0:64, h * 64:(h + 1) * 64], lhsT=qd[:, pr, :], rhs=Sbf[:, pr, hb * 64:(hb + 1) * 64],
                                 start=True, stop=False)
                        ins = e.matmul(out=pO[0:64, h * 64:(h + 1) * 64], lhsT=attnT[0:64, h, :], rhs=vnew[0:64, h, :],
                                       start=False, stop=True)
                return ins
            kb.op("pe", [B["qd"], B["Sbf"], B["attnT"], B["vnew"]], [pOb], mmo)
            kb.op(Aop, [pOb], [B["o"]], lambda e: e.activation(
                out=o_sb[0:64].rearrange("p h j -> p (h j)"), in_=pO[0:64, :], func=AF.Copy))
            yield "R"
            pS, pSb = nb()

            def mms(e):
                for pr in range(4):
                    for hb in range(2):
                        h = 2 * pr + hb
                        ins = e.matmul(out=pS[hb * 64:(hb + 1) * 64, pr * 128 + hb * 64:pr * 128 + (hb + 1) * 64],
                                       lhsT=kdec[0:64, h, :], rhs=vnew[0:64, h, :], start=True, stop=True)
                return ins
            kb.op("pe", [B["kdec"], B["vnew"]], [pSb], mms)
            kb.op(PL, [B["gl"], B["S32"]], [B["S32"]], lambda e: e.tensor_tensor(
                out=S32, in0=S32, in1=bc(gl.unsqueeze(2), [128, 4, 128]), op=ALU.mult))
            pS4 = pS[:, :].rearrange("p (c j) -> p c j", c=4)
            for hb in range(2):
                ps_ = slice(hb * 64, (hb + 1) * 64)
                kb.op(V, [pSb, B["S32"]], [B["S32"]], lambda e, ps_=ps_: e.tensor_tensor(
                    out=S32[ps_, :, ps_], in0=S32[ps_, :, ps_], in1=pS4[ps_, :, ps_], op=ALU.add))
            kb.op(Aop, [B["S32"]], [B["Sbf"]], lambda e: e.activation(
                out=Sbf.rearrange("p c j -> p (c j)"), in_=S32.rearrange("p c j -> p (c j)"), func=AF.Copy))
            yield "R"
            kb.op(Aop, [pOb], [B["t1r"]], lambda e: e.activation(
                out=t1r[0:64].rearrange("p h j -> p (h j)"), in_=pO[0:64, :], func=AF.Square))
            kb.op(V, [B["t1r"]], [B["ss8"]], lambda e: e.tensor_reduce(out=ss8[0:64, :], in_=t1r[0:64], axis=AX, op=ALU.add))
            kb.op(Aop, [B["ss8"], self.epsb], [B["ss8"]], lambda e: e.activation(
                out=ss8[0:64, :], in_=ss8[0:64, :], func=AF.Ln, bias=self.epsc[0:64, 0:1], scale=1.0 / 64))
            kb.op(Aop, [B["ss8"]], [B["ss8"]], lambda e: e.activation(
                out=ss8[0:64, :], in_=ss8[0:64, :], func=AF.Exp, scale=-0.5))
            kb.op(V, [B["o"], B["ss8"]], [B["t1r"]], lambda e: e.tensor_tensor(
                out=t1r[0:64], in0=o_sb[0:64], in1=bc(ss8[0:64, :].unsqueeze(2), S8), op=ALU.mult))
            kb.op(V, [B["t1r"], B["zs"]], [B["ogb"]], lambda e: e.tensor_tensor(
                out=ogb[0:64], in0=t1r[0:64], in1=v3(zs[0:64, ch, :]), op=ALU.mult))

            yield "R"

            def tro(e):
                for pr in range(4):
                    ins = e.transpose(out=tpb[:, pr * 64:(pr + 1) * 64],
                                      in_=ogb[0:64, 2 * pr:2 * pr + 2, :].rearrange("p h j -> p (h j)"),
                                      identity=self.ident[0:64, 0:64])
                return ins
            kb.op("pe", [B["ogb"], self.identb], [tpbb], tro)
            mb = [self.mixb[c][n // 2] for c in range(4, 8)]
            kb.op(Aop, [tpbb, epb], mb, lambda e: e.activation(
                out=self.mix[:, 4:8, n * 64:(n + 1) * 64], in_=tpb[:, 0:256].rearrange("p (c j) -> p c j", c=4), func=AF.Copy,
                scale=self.ep[:, eo + 132:eo + 133]))


        rt2 = [self.rt[:, 0:G], self.rt[:, G:2 * G]]
        rs2 = [self.rr[:, 0:G], self.rr[:, G:2 * G]]
        rt2b = [Buf(), Buf()]
        rs2b = [Buf(), Buf()]

        dgc = ctmp.rearrange("p a t -> p (a t)").bitcast(BF16).rearrange("p (a k j) -> p a k j", a=2, k=4)
        cwv = self.ep[:, eo + 4:eo + 52].rearrange("p (c k) -> p c k", k=4)

        def stage1a(fc):
            w, wb = self.wtile()
            p, pbf = nb()

            def mm(e, w=w, p=p):
                for k in range(NCH):
                    ins = e.matmul(out=p[:, 0:G], lhsT=w[:, k * 128:(k + 1) * 128], rhs=xg[:, k, :],
                                   start=(k == 0), stop=(k == NCH - 1))
                return ins
            kb.op("pe", [wb, B["xg"]], [pbf], mm)
            kb.op(Aop, [pbf], [cinb[fc]],
                  lambda e, p=p, fc=fc: e.activation(out=cin[:, fc, 3:3 + G], in_=p[:, 0:G], func=AF.Copy))

        def stage1b(fc):
            ci = fc % 2
            dg = dgc[:, ci]
            dgb = B["ctmp%d" % ci]
            kb.op(V, [self.identb, epb], [dgb], lambda e, dg=dg, fc=fc: e.tensor_tensor(
                out=dg, in0=bc(self.ident[:, :].unsqueeze(1), [128, 4, 128]),
                in1=bc(cwv[:, fc, :].unsqueeze(2), [128, 4, 128]), op=ALU.mult))
            pc, pcb = nb()

            def mmc(e, pc=pc, ci=ci, fc=fc):
                for k in range(4):
                    ins = e.matmul(out=pc[:, 0:G], lhsT=dgc[:, ci, k, :], rhs=cin[:, fc, k:k + G],
                                   start=(k == 0), stop=(k == 3))
                return ins
            kb.op("pe", [dgb, cinb[fc]], [pcb], mmc)
            kb.op(V, [cinb[fc]], [cinb[fc]],
                  lambda e, fc=fc: e.tensor_copy(out=cin[:, fc, 0:3], in_=cin[:, fc, G:G + 3]))
            if fc >= 8:
                dst, dstb = vT[:, fc - 8, :], B["vT"]
            elif fc >= 4:
                dst, dstb = kT[:, fc - 4, :], B["kT"]
            else:
                dst, dstb = qT[:, fc, :], B["qT"]
            kb.op(Aop, [pcb], [dstb], lambda e, pc=pc, dst=dst: e.activation(out=dst, in_=pc[:, 0:G], func=AF.Silu))

        def stage2a(fc):
            ci = fc % 2
            dst, dstb = (qT[:, fc, :], B["qT"]) if fc < 4 else (kT[:, fc - 4, :], B["kT"])
            sqv = self.sq[:, ci, 0:G]
            kb.op(Aop, [dstb], [self.sqb[ci]], lambda e, dst=dst, sqv=sqv: e.activation(out=sqv, in_=dst, func=AF.Square))
            ps_, psb_ = nb()
            kb.op("pe", [self.sqb[ci], B["bd16"]], [psb_],
                  lambda e, ps_=ps_, sqv=sqv: e.matmul(out=ps_[:, 0:G], lhsT=bd16, rhs=sqv, start=True, stop=True))
            return ps_, psb_

        def stage2b(fc, ps_, psb_):
            ci = fc % 2
            dst, dstb, scl = (qT[:, fc, :], B["qT"], 0.125) if fc < 4 else (kT[:, fc - 4, :], B["kT"], 1.0)
            kb.op(Aop, [psb_, self.epsb], [rt2b[ci]],
                  lambda e, ps_=ps_, ci=ci: e.activation(out=rt2[ci], in_=ps_[:, 0:G], func=AF.Ln, bias=self.epsc[:, 0:1]))
            kb.op(Aop, [rt2b[ci]], [rs2b[ci]],
                  lambda e, ci=ci: e.activation(out=rs2[ci], in_=rt2[ci], func=AF.Exp, scale=-0.5))
            kb.op(V, [dstb, rs2b[ci]], [dstb],
                  lambda e, dst=dst, scl=scl, ci=ci: e.scalar_tensor_tensor(
                      out=dst, in0=dst, scalar=scl, in1=rs2[ci], op0=ALU.mult, op1=ALU.mult))
            if fc >= 4:
                for hb in range(2):
                    kb.op(Aop, [dstb], [dstb], lambda e, hb=hb, fc=fc: e.activation(
                        out=kTz[hb * 64:(hb + 1) * 64, hb, fc - 4, :], in_=kT[hb * 64:(hb + 1) * 64, fc - 4, :], func=AF.Copy))

        for gi in range(T // G):
            t0 = gi * G
            tb = t0 // 512
            gs = slice(t0, t0 + G)
            for c in range(NCH):
                k = self.sqi % 4
                self.sqi += 1
                sqv = self.sq[:, k, 0:G]
                xv = self.x[:, c, gs]
                kb.op(Aop, [self.xb[c][tb]], [self.sqb[k]],
                      lambda e, sqv=sqv, xv=xv: e.activation(out=sqv, in_=xv, func=AF.Square))
                kb.op("pe", [self.sqb[k], self.onesb], [statb],
                      lambda e, sqv=sqv, c=c: e.matmul(out=stat[:, 0:G], lhsT=self.ones[:, :], rhs=sqv,
                                                       start=(c == 0), stop=(c == NCH - 1)))
            kb.op(Aop, [statb, self.epsb], [rt2b[0]],
                  lambda e: e.activation(out=rtd, in_=stat[:, 0:G], func=AF.Ln, bias=self.epsc[:, 0:1]))
            kb.op(Aop, [rt2b[0]], [rs2b[0]], lambda e: e.activation(out=rsd, in_=rtd, func=AF.Exp, scale=-0.5))
            for c in range(NCH):
                xv = self.x[:, c, gs]
                wv = self.nw[:, nidx * NCH + c: nidx * NCH + c + 1]
                kb.op(V, [self.xb[c][tb], rs2b[0], self.nwb], [B["xg"]],
                      lambda e, xv=xv, wv=wv, c=c: e.scalar_tensor_tensor(
                          out=xg[:, c, :], in0=xv, scalar=wv, in1=rsd, op0=ALU.mult, op1=ALU.mult))
            stage1a(0)
            for fc in range(12):
                if fc + 1 < 12:
                    stage1a(fc + 1)
                stage1b(fc)
            wz = [self.wtile() for _ in range(4)]
            for ch in range(NCK):
                p, pbf = nb()

                def mmz(e, p=p, ch=ch, wz=wz):
                    for k in range(NCH):
                        ins = e.matmul(out=p[0:64, :], lhsT=xg[:, k, ch * 64:(ch + 1) * 64],
                                       rhs=wz[k // 2][0][:, (k % 2) * 512:(k % 2 + 1) * 512],
                                       start=(k == 0), stop=(k == NCH - 1))
                    return ins
                kb.op("pe", [B["xg"]] + [w_[1] for w_ in wz], [pbf], mmz)
                kb.op(Aop, [pbf], [B["zs"]],
                      lambda e, p=p, ch=ch: e.activation(out=zs[0:64, ch, :], in_=p[0:64, :], func=AF.Silu))
            pend = stage2a(0)
            for fc in range(8):
                nxt = stage2a(fc + 1) if fc + 1 < 8 else None
                stage2b(fc, *pend)
                pend = nxt
            wba, wbab = self.wtile(128)
            p, pbf = nb()

            def mmb(e, p=p, wba=wba):
                for ch in range(NCK):
                    for k in range(NCH):
                        ins = e.matmul(out=p[0:64, ch * 16:(ch + 1) * 16], lhsT=xg[:, k, ch * 64:(ch + 1) * 64],
                                       rhs=wba[:, k * 16:(k + 1) * 16], start=(k == 0), stop=(k == NCH - 1))
                return ins
            kb.op("pe", [B["xg"], wbab], [pbf], mmb)
            kb.op(V, [pbf], [B["ba"]],
                  lambda e, p=p: e.tensor_copy(out=ba[0:64], in_=p[0:64, 0:NCK * 16].rearrange("p (c f) -> p c f", c=NCK)))
            kb.op(Aop, [B["ba"]], [B["beta"]],
                  lambda e: e.activation(out=beta[0:64], in_=ba[0:64, :, 0:8], func=AF.Exp, scale=-1.0))
            kb.op(V, [B["beta"]], [B["beta"]], lambda e: e.tensor_scalar_add(out=beta[0:64], in0=beta[0:64], scalar1=1.0))
            kb.op(V, [B["beta"]], [B["beta"]], lambda e: e.reciprocal(out=beta[0:64], in_=beta[0:64]))
            kb.op(V, [B["ba"], epb], [B["gtmp"]],
                  lambda e: e.tensor_tensor(out=gtmp[0:64], in0=ba[0:64, :, 8:16],
                                            in1=bc(dtb.unsqueeze(1), [64, NCK, 8]), op=ALU.add))
            kb.op(Aop, [B["gtmp"]], [B["gtmp"]], lambda e: e.activation(out=gtmp[0:64], in_=gtmp[0:64], func=AF.Exp))
            kb.op(Aop, [B["gtmp"], B["onec"]], [B["gtmp"]],
                  lambda e: e.activation(out=gtmp[0:64], in_=gtmp[0:64], func=AF.Ln, bias=onec[0:64, 0:1]))
            kb.op(V, [B["gtmp"], B["nA"]], [B["gt"]],
                  lambda e: e.tensor_tensor(out=gt[0:64], in0=gtmp[0:64], in1=bc(nA[0:64, :].unsqueeze(1), [64, NCK, 8]),
                                            op=ALU.mult))
            for ch in range(NCK):
                for (src, srcb, dst, dstb) in ((kT, B["kT"], k_tm, B["k_tm"]), (vT, B["vT"], v_tm, B["v_tm"])):
                    def tr(e, src=src, ch=ch):
                        for pr in range(4):
                            ins = e.transpose(out=tpb[0:64, pr * 128:(pr + 1) * 128], in_=src[:, pr, ch * 64:(ch + 1) * 64],
                                              identity=self.ident[:, :])
                        return ins
                    kb.op("pe", [srcb, self.identb], [tpbb], tr)
                    kb.op(Aop, [tpbb], [dstb],
                          lambda e, dst=dst, ch=ch: e.activation(out=dst[0:64, ch, :], in_=tpb[0:64, 0:512], func=AF.Copy))
            if os.environ.get("DN_NOCHUNK"):
                if gi == 0:
                    for c in range(4, 8):
                        for nn in range(16):
                            kb.op(V, [], [self.mixb[c][nn]],
                                  lambda e, c=c, nn=nn: e.memset(self.mix[:, c, nn * 128:(nn + 1) * 128], 0.0))
                continue
            def args(ch):
                return (gi * NCK + ch, ch, slice(ch * 64, (ch + 1) * 64))

            def run_pa(ch, rgen):
                ga, gb_ = chunk_pa(*args(ch), 0), chunk_pa(*args(ch), 1)
                da = db = False
                dr = rgen is None
                k = 0
                while not (da and db and dr):
                    if not da and next(ga, "PAend") == "PAend":
                        da = True
                    if not db and next(gb_, "PAend") == "PAend":
                        db = True
                    k += 1
                    if not dr and (k % 2 == 0 or (da and db)):
                        if next(rgen, None) is None:
                            dr = True

            def run_until(g, marks):
                for m in g:
                    if m in marks:
                        return m
                return None
            run_pa(0, None)
            for ch in range(NCK):
                gr = chunk_rest(*args(ch))
                run_until(gr, ("Rstart",))
                if ch + 1 < NCK:
                    run_pa(ch + 1, gr)
                else:
                    for _ in gr:
                        pass

    def build(self):
        kb = self.kb
        self.load()
        for stg in self.stages:
            if stg[0] == "ffn":
                self.ffn(stg[1])
            elif stg[0] == "conv":
                self.convmix(stg[1])
            elif stg[0] == "even":
                self.evenmix(stg[1])
            elif stg[0] == "final":
                self.rmsnorm(12, final=True)
            else:
                raise ValueError(stg)
        self.store()
        nc = self.nc
        with nc.Block() as block:
            @block.tensor
            def _(e):
                kb.replay("pe", e)

            @block.scalar
            def _(e):
                kb.replay("act", e)

            @block.vector
            def _(e):
                kb.replay("dve", e)

            @block.gpsimd
            def _(e):
                kb.replay("pool", e)

            @block.sync
            def _(e):
                kb.replay("sp", e)
        self.st.close()
        return nc


def pack_cols(w, m):
    nk = w.shape[0] // 128
    blk = w[:, m * 128:(m + 1) * 128].reshape(nk, 128, 128)
    return _tile(blk.transpose(1, 0, 2).reshape(128, nk * 128))


def pack_conv(w1, w2):
    tiles = []
    for j in range(8):
        tiles.append(pack_cols(w1, j))
        tiles.append(pack_cols(w1, 8 + j))
    for c in range(8):
        tiles.append(pack_cols(w2, c))
    return tiles


def pack_cp(inp):
    cp = np.zeros((128, 2 * CPW), np.float32)
    for ci in range(2):
        o = ci * CPW
        cp[:, o:o + 16] = inp["conv_b_pw1"][ci].reshape(16, 128).T
        cp[:, o + 16:o + 264] = inp["conv_w_dw"][ci].reshape(31, 8, 128).transpose(2, 1, 0).reshape(128, 248)
        cp[:, o + 264:o + 272] = inp["conv_b_dw"][ci].reshape(8, 128).T
        cp[:, o + 272:o + 280] = inp["conv_ln_w"][ci].reshape(8, 128).T
        cp[:, o + 280:o + 288] = inp["conv_ln_b"][ci].reshape(8, 128).T
        cp[:, o + 288:o + 296] = inp["conv_b_pw2"][ci].reshape(8, 128).T
    return cp


QPERM = np.concatenate([np.concatenate([np.arange(c * 64, (c + 1) * 64), np.arange((4 + c) * 64, (5 + c) * 64)])
                        for c in range(4)])


def pack_dn(w_in):
    tiles = []
    for gi in range(T // GT):
        for fc in range(12):
            tiles.append(pack_cols(w_in[:, 768:2304], fc))
        wz = w_in[:, 2304:2816]
        for kp in range(4):
            tiles.append(_tile(np.concatenate([wz[(2 * kp) * 128:(2 * kp + 1) * 128], wz[(2 * kp + 1) * 128:(2 * kp + 2) * 128]], axis=1)))
        wba = w_in[:, 2816:2832]
        tiles.append(_tile(wba.reshape(8, 128, 16).transpose(1, 0, 2).reshape(128, 128)))
    return tiles


def pack_even(w_in, w_out):
    tiles = []
    if "att" in EVEN_PARTS:
        wq = w_in[:, 0:512][:, QPERM]
        for c in range(4):
            tiles.append(pack_cols(wq, c))
        tiles.append(pack_cols(w_in[:, 512:640], 0))
        tiles.append(pack_cols(w_in[:, 640:768], 0))
    if "dn" in EVEN_PARTS:
        tiles += pack_dn(w_in)
    wo = np.concatenate([w_out[0:512][QPERM], w_out[512:1024]], axis=0)
    for c in range(8):
        tiles.append(pack_cols(wo, c))
    return tiles


def pack_ep(inp):
    ep = np.zeros((128, 2 * EPW), np.float32)
    for e in range(2):
        o = e * EPW
        sk = inp["attn_sinks"][e]
        ep[0:64, o:o + 4] = sk[0:4][None, :]
        ep[64:128, o:o + 4] = sk[4:8][None, :]
        ep[:, o + 4:o + 52] = inp["dn_conv_w"][e].reshape(4, 12, 128).transpose(2, 1, 0).reshape(128, 48)
        ep[:, o + 52:o + 60] = inp["dn_a_log"][e][None, :]
        ep[:, o + 60:o + 68] = inp["dn_dt_bias"][e][None, :]
        ep[:, o + 68:o + 132] = inp["dn_norm_w"][e][None, :]
        ep[:, o + 132] = np.concatenate([inp["dn_norm_w"][e], inp["dn_norm_w"][e]])
    return ep


def make_cst():
    c = np.zeros((128, CSTW), np.float32)
    j = np.arange(128)[:, None]
    i = np.arange(128)[None, :]
    c[:, 0:128] = np.where(j <= i, (j - i).astype(np.float32), -1e9)
    c[:, 128:256] = np.where(j > i, (j - i - 128).astype(np.float32), -1e9)
    t = np.arange(64)[:, None]
    u = np.arange(64)[None, :]
    c[0:64, 256:320] = (t <= u).astype(np.float32)
    c[0:64, 320:384] = np.where(u >= t, 0.0, -30000.0)
    for bi, b in enumerate((1, 2, 4, 8, 16, 32)):
        same = (t // (2 * b)) == (u // (2 * b))
        lm = same & ((t % (2 * b)) >= b) & ((u % (2 * b)) < b)
        c[0:64, 448 + bi * 64:448 + (bi + 1) * 64] = lm.astype(np.float32)
        c[0:64, 960 + bi * 64:960 + (bi + 1) * 64] = -lm.astype(np.float32)
    c[:, 832:960] = ((j // 64) == (i // 64)).astype(np.float32)
    return c


def stage_tiles(stages, inp):
    tiles = []
    for stg in stages:
        if stg[0] == "ffn":
            layer, j = divmod(stg[1], 3)
            fi = 0 if j == 0 else 1
            tiles += pack_ffn(inp["ffn_w_gate"][layer, fi], inp["ffn_w_up"][layer, fi], inp["ffn_w_down"][layer, fi])
        elif stg[0] == "even":
            e = stg[1] // 2
            tiles += pack_even(inp["mix_w_in"][e], inp["mix_w_out"][e])
        elif stg[0] == "conv":
            ci = stg[1] // 2
            tiles += pack_conv(inp["conv_w_pw1"][ci], inp["conv_w_pw2"][ci])
    return tiles


def pack_nw(inp):
    nw = np.concatenate([inp["norm_w"].reshape(12, D), inp["final_norm_w"].reshape(1, D)], axis=0)
    return np.ascontiguousarray(nw.reshape(13, NCH, 128).transpose(2, 0, 1).reshape(128, 13 * NCH))


def run_stages(stages, xT_list, inp, trace=False):
    tiles = stage_tiles(stages, inp)
    wst = np.stack(tiles) if tiles else np.zeros((1, 128, 1024), np.float32)
    nc = Prog(stages, len(tiles)).build()
    nw = pack_nw(inp)
    cp = pack_cp(inp)
    ident = np.eye(128, dtype=np.float32)
    ep = pack_ep(inp)
    cst = make_cst()
    in_maps = [{"xin": np.ascontiguousarray(xT), "wst": wst, "nw": nw, "cp": cp, "ident": ident, "ep": ep, "cst": cst}
               for xT in xT_list]
    res = run_bass_kernel_spmd(nc, in_maps, core_ids=list(range(len(xT_list))), trace=trace)
    return [r["xout"] for r in res.results], res


FULL = []
for _l in range(4):
    FULL += [("ffn", 3 * _l + 0), ("even" if _l % 2 == 0 else "conv", _l), ("ffn", 3 * _l + 2)]
FULL += [("final",)]


def kernel(**inputs):
    inp = {k: np.asarray(v) for k, v in inputs.items()}
    x = inp["x"]
    xT = [np.ascontiguousarray(x[b].T) for b in range(8)]
    outs, _ = run_stages(FULL, xT, inp)
    return np.stack([o.T for o in outs]).astype(np.float32)
```

```python
import os
import numpy as np
from contextlib import ExitStack
import concourse.bass as bass
import concourse.mybir as mybir
from concourse.bass_utils import run_bass_kernel_spmd

F32 = mybir.dt.float32
BF16 = mybir.dt.bfloat16
AF = mybir.ActivationFunctionType
ALU = mybir.AluOpType

D = 1024
T = 2048
DFF = 2816
NCH = D // 128
NTB = T // 512
NMC = DFF // 128
GROUPS = [(0, 8), (8, 8), (16, 6)]
EPS = 1e-6
RING = 8
PAD = 32
CPW = 296
ARENA = NCH * T + 8 * (PAD + T)
EPW = 133
CSTW = 1344
GT = 256
NCK = GT // 64
SLOPES = [2.0 ** (-8.0 * (h + 1) / 8) for h in range(8)]
EVEN_PARTS = ("att", "dn")


class Sem:
    def __init__(self, h):
        self.h = h
        self.count = 0


class Buf:
    __slots__ = ("w", "r")

    def __init__(self):
        self.w = None
        self.r = {}


class Eng:
    def __init__(self, name):
        self.name = name
        self.ops = []
        self.sem = None
        self.seen = {}
        self.fence = None


class KB:
    def __init__(self, nc, st):
        self.nc = nc
        self.st = st
        self.nsem = 0
        self.E = {}
        for n in ("pe", "act", "dve", "pool", "sp"):
            e = Eng(n)
            e.sem = self.newsem(n)
            self.E[n] = e
        self.dsems = []

    def newsem(self, name):
        self.nsem += 1
        return Sem(self.st.enter_context(self.nc.semaphore(f"s_{name}_{self.nsem}")))

    def sb(self, name, shape, dt):
        return self.st.enter_context(self.nc.sbuf_tensor(name, shape, dt))

    def ps(self, name, shape, dt=F32):
        return self.st.enter_context(self.nc.psum_tensor(name, shape, dt))

    def op(self, en, reads, writes, emit, dsem=None):
        eng = self.E[en]
        need = {}

        def add(tok):
            if tok is None:
                return
            s, v = tok
            if need.get(s, 0) < v:
                need[s] = v

        if eng.fence is not None:
            for tok in eng.fence:
                add(tok)
            eng.fence = None
        for b in reads:
            add(b.w)
        for b in writes:
            add(b.w)
            for s, v in b.r.items():
                add((s, v))
        waits = []
        for s, v in need.items():
            if en == "pe" and s is eng.sem:
                continue
            if eng.seen.get(s, 0) < v:
                eng.seen[s] = v
                waits.append((s, v))
        if dsem is None:
            sem = eng.sem
            sem.count += 1
            inc = 1
        else:
            sem = dsem
            sem.count += 16
            inc = 16
        tok = (sem, sem.count)
        eng.ops.append((waits, emit, sem, inc))
        for b in reads:
            if b.r.get(sem, 0) < sem.count:
                b.r[sem] = sem.count
        for b in writes:
            b.w = tok
            b.r = {}
        return tok

    def fence(self, new_epoch=False):
        toks = [(e.sem, e.sem.count) for e in self.E.values() if e.sem.count > 0]
        toks += [(s, s.count) for s in self.dsems if s.count > 0]
        for e in self.E.values():
            e.fence = list(toks)
        if new_epoch:
            for n in ("pe", "act", "dve"):
                self.E[n].sem = self.newsem(n)

    def replay(self, en, e):
        for waits, emit, sem, inc in self.E[en].ops:
            for s, v in waits:
                e.wait_ge(s.h, v)
            ins = emit(e)
            if ins is not None:
                ins.then_inc(sem.h, inc)


def _tile(a):
    t = np.zeros((128, 1024), np.float32)
    t[:, : a.shape[1]] = a
    return t


def pack_ffn(wg, wu, wd):
    tiles = []
    for (m0, nm) in GROUPS:
        for mi in range(nm):
            m = m0 + mi
            for w in (wg, wu):
                blk = w[:, m * 128:(m + 1) * 128].reshape(NCH, 128, 128)
                tiles.append(_tile(blk.transpose(1, 0, 2).reshape(128, NCH * 128)))
        for c in range(NCH):
            blk = wd[m0 * 128:(m0 + nm) * 128, c * 128:(c + 1) * 128].reshape(nm, 128, 128)
            tiles.append(_tile(blk.transpose(1, 0, 2).reshape(128, nm * 128)))
    return tiles


class Prog:
    def __init__(self, stages, n_wtiles):
        self.stages = stages
        nc = bass.Bass("TRN2", target_bir_lowering=False)
        self.nc = nc
        self.st = ExitStack()
        st = self.st
        self.xin = nc.dram_tensor("xin", [D, T], F32, kind="ExternalInput").ap()
        self.wst = nc.dram_tensor("wst", [max(n_wtiles, 1), 128, 1024], F32, kind="ExternalInput").ap()
        self.nw_d = nc.dram_tensor("nw", [128, 13 * NCH], F32, kind="ExternalInput").ap()
        self.cp_d = nc.dram_tensor("cp", [128, 2 * CPW], F32, kind="ExternalInput").ap()
        self.ident_d = nc.dram_tensor("ident", [128, 128], F32, kind="ExternalInput").ap()
        self.xout = nc.dram_tensor("xout", [D, T], F32, kind="ExternalOutput").ap()
        kb = KB(nc, st)
        self.kb = kb
        self.x = kb.sb("x_sb", [128, NCH, T], F32)
        self.xb = [[Buf() for _ in range(NTB)] for _ in range(NCH)]
        self.arena = kb.sb("arena_sb", [128, ARENA], BF16)
        self.xn = self.arena[:, 0:NCH * T].rearrange("p (c t) -> p c t", c=NCH)
        self.xnb = [[Buf() for _ in range(NTB)] for _ in range(NCH)]
        self.nw = kb.sb("nw_sb", [128, 13 * NCH], F32)
        self.nwb = Buf()
        self.ones = kb.sb("ones_sb", [128, 128], BF16)
        self.onesb = Buf()
        self.ring = kb.sb("wring", [128, RING, 1024], BF16)
        self.ringb = [Buf() for _ in range(RING)]
        self.ringsem = [kb.newsem("ring") for _ in range(RING)]
        kb.dsems += self.ringsem
        self.wi = 0
        self.act = self.arena[:, NCH * T:NCH * T + 8 * (PAD + T)].rearrange("p (c t) -> p c t", c=8)
        self.s2 = kb.sb("s2_sb", [128, NCH * T], BF16)
        self.mix = self.s2[:, :].rearrange("p (c t) -> p c t", c=NCH)
        self.mixb = [[Buf() for _ in range(16)] for _ in range(NCH)]
        self.ep_d = nc.dram_tensor("ep", [128, 2 * EPW], F32, kind="ExternalInput").ap()
        self.cst_d = nc.dram_tensor("cst", [128, CSTW], F32, kind="ExternalInput").ap()
        self.ep = kb.sb("ep_sb", [128, 2 * EPW], F32)
        self.cst = kb.sb("cst_sb", [128, CSTW], F32)
        self.epb = Buf()
        self.cstb = Buf()
        self.actb = [[Buf() for _ in range(NTB)] for _ in range(8)]
        self.sq = kb.sb("sq_sb", [128, 4, 512], BF16)
        self.sqb = [Buf() for _ in range(4)]
        self.sqi = 0
        self.rr = kb.sb("r_sb", [128, 512], F32)
        self.rrb = Buf()
        self.rt = kb.sb("rt_sb", [128, 512], F32)
        self.rtb = Buf()
        self.epsc = kb.sb("eps_sb", [128, 1], F32)
        self.epsb = Buf()
        self.sg = kb.sb("sg_sb", [128, 2, 512], F32)
        self.sgb = [Buf() for _ in range(2)]
        self.cp = kb.sb("cp_sb", [128, 2 * CPW], F32)
        self.cpb = Buf()
        self.ident = kb.sb("ident_sb", [128, 128], BF16)
        self.identf = kb.sb("identf_sb", [128, 128], F32)
        self.identb = Buf()
        self.diag = self.s2[:, 0:2 * 31 * 128].rearrange("p (a k j) -> p a k j", a=2, k=31)
        self.diagb = [Buf(), Buf()]
        self.dgi = 0
        self.tmpa = kb.sb("tmpa_sb", [128, 512], F32)
        self.tmpab = Buf()
        self.tmpb = kb.sb("tmpb_sb", [128, 2, 512], F32)
        self.tmpbb = [Buf(), Buf()]
        self.tbi = 0
        self.ones64 = kb.sb("ones64_sb", [128, 128], F32)
        self.sk = kb.sb("sk_sb", [128, 4], F32)
        self.skb = Buf()
        self.den = kb.sb("den_sb", [128, 128], F32)
        self.denb = Buf()
        self.sgi2 = 0
        self.pb = [kb.ps(f"psb{i}", [128, 512], F32) for i in range(8)]
        self.pbb = [Buf() for _ in range(8)]
        self.gi = 0
        self.yi = 0

    def wtile(self, ncols=1024):
        i = self.wi
        self.wi += 1
        slot = i % RING
        buf = self.ringb[slot]
        dst = self.ring[:, slot, 0:ncols]
        src = self.wst[i, :, 0:ncols]
        self.kb.op("pool", [], [buf], lambda e: e.dma_start(out=dst, in_=src), dsem=self.ringsem[slot])
        return self.ring[:, slot, :], buf

    def load(self):
        kb = self.kb
        s1 = kb.newsem("ldx")
        kb.dsems.append(s1)
        for t in range(NTB):
            st_ = kb.newsem("ldx%d" % t)
            kb.dsems.append(st_)
            for c in range(NCH):
                dst = self.x[:, c, t * 512:(t + 1) * 512]
                src = self.xin[c * 128:(c + 1) * 128, t * 512:(t + 1) * 512]
                kb.op("sp", [], [self.xb[c][t]], lambda e, dst=dst, src=src: e.dma_start(out=dst, in_=src), dsem=st_)
            for c in range(NCH):
                self.xb[c][t].w = (st_, st_.count)
        kb.op("sp", [], [self.nwb], lambda e: e.dma_start(out=self.nw[:, :], in_=self.nw_d), dsem=s1)
        kb.op("sp", [], [self.cpb], lambda e: e.dma_start(out=self.cp[:, :], in_=self.cp_d), dsem=s1)
        kb.op("sp", [], [self.epb], lambda e: e.dma_start(out=self.ep[:, :], in_=self.ep_d), dsem=s1)
        kb.op("sp", [], [self.cstb], lambda e: e.dma_start(out=self.cst[:, :], in_=self.cst_d), dsem=s1)
        kb.op("sp", [], [self.identb], lambda e: e.dma_start(out=self.identf[:, :], in_=self.ident_d), dsem=s1)
        fin = (s1, s1.count)
        self.nwb.w = fin
        self.cpb.w = fin
        self.epb.w = fin
        self.cstb.w = fin
        self.identb.w = fin
        kb.op("dve", [self.identb], [self.identb], lambda e: e.tensor_copy(out=self.ident[:, :], in_=self.identf[:, :]))
        kb.op("dve", [], [self.onesb], lambda e: e.memset(self.ones[:, :], 1.0 / D))
        kb.op("dve", [], [self.epsb], lambda e: e.memset(self.epsc[:, :], EPS))

    def store(self):
        kb = self.kb
        s1 = kb.newsem("stx")
        kb.dsems.append(s1)
        for t in range(NTB):
            for c in range(NCH):
                src = self.x[:, c, t * 512:(t + 1) * 512]
                dst = self.xout[c * 128:(c + 1) * 128, t * 512:(t + 1) * 512]
                kb.op("sp", [self.xb[c][t]], [], lambda e, dst=dst, src=src: e.dma_start(out=dst, in_=src), dsem=s1)
        fin = Buf()
        fin.w = (s1, s1.count)
        kb.op("sp", [fin], [], lambda e: None)

    def rmsnorm(self, nidx, final=False):
        kb = self.kb
        stat = self.pb[6]
        statb = self.pbb[6]
        for t in range(NTB):
            ts = slice(t * 512, (t + 1) * 512)
            for c in range(NCH):
                k = self.sqi % 4
                self.sqi += 1
                sqv = self.sq[:, k, :]
                xv = self.x[:, c, ts]
                kb.op("act", [self.xb[c][t]], [self.sqb[k]],
                      lambda e, sqv=sqv, xv=xv: e.activation(out=sqv, in_=xv, func=AF.Square))
                kb.op("pe", [self.sqb[k], self.onesb], [statb],
                      lambda e, sqv=sqv, c=c: e.matmul(out=stat[:, :], lhsT=self.ones[:, :], rhs=sqv,
                                                       start=(c == 0), stop=(c == NCH - 1)))
            kb.op("act", [statb, self.epsb], [self.rtb],
                  lambda e: e.activation(out=self.rt[:, :], in_=stat[:, :], func=AF.Ln, bias=self.epsc[:, 0:1]))
            kb.op("act", [self.rtb], [self.rrb],
                  lambda e: e.activation(out=self.rr[:, :], in_=self.rt[:, :], func=AF.Exp, scale=-0.5))
            for c in range(NCH):
                xv = self.x[:, c, ts]
                wv = self.nw[:, nidx * NCH + c: nidx * NCH + c + 1]
                eng_x = "dve"
                if final:
                    kb.op("dve", [self.rrb, self.nwb], [self.xb[c][t]],
                          lambda e, xv=xv, wv=wv: e.scalar_tensor_tensor(
                              out=xv, in0=xv, scalar=wv, in1=self.rr[:, :], op0=ALU.mult, op1=ALU.mult))
                else:
                    ov = self.xn[:, c, ts]
                    kb.op(eng_x, [self.xb[c][t], self.rrb, self.nwb], [self.xnb[c][t]],
                          lambda e, xv=xv, wv=wv, ov=ov: e.scalar_tensor_tensor(
                              out=ov, in0=xv, scalar=wv, in1=self.rr[:, :], op0=ALU.mult, op1=ALU.mult))

    def ffn(self, nidx):
        kb = self.kb
        self.rmsnorm(nidx)
        for (m0, nm) in GROUPS:
            for mi in range(nm):
                wg, wgb = self.wtile()
                wu, wub = self.wtile()
                for t in range(NTB):
                    ts = slice(t * 512, (t + 1) * 512)
                    j = self.gi % 2
                    self.gi += 1
                    pg, pgb = self.pb[2 * j], self.pbb[2 * j]
                    pu, pub = self.pb[2 * j + 1], self.pbb[2 * j + 1]
                    xr = [self.xnb[c][t] for c in range(NCH)]

                    def mm(e, w=wg, p=pg, ts=ts):
                        for k in range(NCH):
                            ins = e.matmul(out=p[:, :], lhsT=w[:, k * 128:(k + 1) * 128], rhs=self.xn[:, k, ts],
                                           start=(k == 0), stop=(k == NCH - 1))
                        return ins
                    kb.op("pe", [wgb] + xr, [pgb], mm)
                    kb.op("pe", [wub] + xr, [pub], lambda e, mm=mm, wu=wu, pu=pu, ts=ts: mm(e, wu, pu, ts))
                    sgv = self.sg[:, j, :]
                    kb.op("act", [pgb], [self.sgb[j]],
                          lambda e, sgv=sgv, pg=pg: e.activation(out=sgv, in_=pg[:, :], func=AF.Silu))
                    av = self.act[:, mi, PAD + t * 512: PAD + (t + 1) * 512]
                    kb.op("dve", [self.sgb[j], pub], [self.actb[mi][t]],
                          lambda e, av=av, sgv=sgv, pu=pu: e.tensor_tensor(out=av, in0=pu[:, :], in1=sgv, op=ALU.mult))
            for c in range(NCH):
                wd, wdb = self.wtile(nm * 128)
                for t in range(NTB):
                    ts = slice(t * 512, (t + 1) * 512)
                    j = self.yi % 2
                    self.yi += 1
                    py, pyb = self.pb[4 + j], self.pbb[4 + j]

                    def mmd(e, wd=wd, py=py, ts=ts, nm=nm):
                        for mi in range(nm):
                            ins = e.matmul(out=py[:, :], lhsT=wd[:, mi * 128:(mi + 1) * 128], rhs=self.act[:, mi, PAD + ts.start: PAD + ts.stop],
                                           start=(mi == 0), stop=(mi == nm - 1))
                        return ins
                    kb.op("pe", [wdb] + [self.actb[mi][t] for mi in range(nm)], [pyb], mmd)
                    xv = self.x[:, c, ts]
                    kb.op("dve", [pyb], [self.xb[c][t]],
                          lambda e, xv=xv, py=py: e.scalar_tensor_tensor(
                              out=xv, in0=py[:, :], scalar=0.5, in1=xv, op0=ALU.mult, op1=ALU.add))

    def convmix(self, layer):
        kb = self.kb
        ci = layer // 2
        self.rmsnorm(layer * 3 + 1)
        o = ci * CPW
        b1a = lambda j: self.cp[:, o + j: o + j + 1]
        b1g = lambda j: self.cp[:, o + 8 + j: o + 8 + j + 1]
        wdw = self.cp[:, o + 16: o + 16 + 248]
        bdw = lambda c: self.cp[:, o + 264 + c: o + 264 + c + 1]
        lnw = lambda c: self.cp[:, o + 272 + c: o + 272 + c + 1]
        lnb = lambda c: self.cp[:, o + 280 + c: o + 280 + c + 1]
        b2 = lambda c: self.cp[:, o + 288 + c: o + 288 + c + 1]
        padb = [self.actb[c][0] for c in range(8)]
        kb.op("dve", [], padb, lambda e: e.memset(self.act[:, :, 0:PAD], 0.0))
        for j in range(8):
            wa, wab = self.wtile()
            wg, wgb = self.wtile()
            for t in range(NTB):
                ts = slice(t * 512, (t + 1) * 512)
                q = self.gi % 2
                self.gi += 1
                pa, pab = self.pb[2 * q], self.pbb[2 * q]
                pg, pgb = self.pb[2 * q + 1], self.pbb[2 * q + 1]
                xr = [self.xnb[c][t] for c in range(NCH)]

                def mm(e, w=wa, p=pa, ts=ts):
                    for k in range(NCH):
                        ins = e.matmul(out=p[:, :], lhsT=w[:, k * 128:(k + 1) * 128], rhs=self.xn[:, k, ts],
                                       start=(k == 0), stop=(k == NCH - 1))
                    return ins
                kb.op("pe", [wab] + xr, [pab], mm)
                kb.op("pe", [wgb] + xr, [pgb], lambda e, mm=mm, wg=wg, pg=pg, ts=ts: mm(e, wg, pg, ts))
                sgv = self.sg[:, q, :]
                kb.op("act", [pgb, self.cpb], [self.sgb[q]],
                      lambda e, sgv=sgv, pg=pg, j=j: e.activation(out=sgv, in_=pg[:, :], func=AF.Sigmoid, bias=b1g(j)))
                av = self.act[:, j, PAD + t * 512: PAD + (t + 1) * 512]
                kb.op("dve", [self.sgb[q], pab, self.cpb], [self.actb[j][t]],
                      lambda e, av=av, sgv=sgv, pa=pa, j=j: e.scalar_tensor_tensor(
                          out=av, in0=pa[:, :], scalar=b1a(j), in1=sgv, op0=ALU.add, op1=ALU.mult))
        for c in range(NCH):
            dsl = self.dgi % 2
            self.dgi += 1
            dg = self.diag[:, dsl, :, :]
            idb = self.ident[:, :].unsqueeze(1).broadcast_to([128, 31, 128])
            wb = wdw[:, c * 31:(c + 1) * 31].unsqueeze(2).broadcast_to([128, 31, 128])
            kb.op("dve", [self.identb, self.cpb], [self.diagb[dsl]],
                  lambda e, dg=dg, idb=idb, wb=wb: e.tensor_tensor(out=dg, in0=idb, in1=wb, op=ALU.mult))
            for t in range(NTB):
                ts = slice(t * 512, (t + 1) * 512)
                q = self.yi % 2
                self.yi += 1
                py, pyb = self.pb[4 + q], self.pbb[4 + q]

                def mmc(e, py=py, c=c, t=t, dsl=dsl):
                    for k in range(31):
                        st0 = PAD + t * 512 - 30 + k
                        ins = e.matmul(out=py[:, :], lhsT=self.diag[:, dsl, k, :], rhs=self.act[:, c, st0: st0 + 512],
                                       start=(k == 0), stop=(k == 30))
                    return ins
                rd = [self.diagb[dsl], self.actb[c][t]] + ([self.actb[c][t - 1]] if t > 0 else [])
                kb.op("pe", rd, [pyb], mmc)
                ov = self.xn[:, c, ts]
                kb.op("act", [pyb, self.cpb], [self.xnb[c][t]],
                      lambda e, ov=ov, py=py, c=c: e.activation(out=ov, in_=py[:, :], func=AF.Identity, bias=bdw(c)))
        msq, msqb = self.pb[6], self.pbb[6]
        mean, meanb = self.pb[7], self.pbb[7]
        for t in range(NTB):
            ts = slice(t * 512, (t + 1) * 512)
            for c in range(NCH):
                k = self.sqi % 4
                self.sqi += 1
                sqv = self.sq[:, k, :]
                cv = self.xn[:, c, ts]
                kb.op("act", [self.xnb[c][t]], [self.sqb[k]],
                      lambda e, sqv=sqv, cv=cv: e.activation(out=sqv, in_=cv, func=AF.Square))
                kb.op("pe", [self.sqb[k], self.onesb], [msqb],
                      lambda e, sqv=sqv, c=c: e.matmul(out=msq[:, :], lhsT=self.ones[:, :], rhs=sqv,
                                                       start=(c == 0), stop=(c == NCH - 1)))
                kb.op("pe", [self.xnb[c][t], self.onesb], [meanb],
                      lambda e, cv=cv, c=c: e.matmul(out=mean[:, :], lhsT=self.ones[:, :], rhs=cv,
                                                     start=(c == 0), stop=(c == NCH - 1)))
            kb.op("act", [meanb], [self.tmpab],
                  lambda e: e.activation(out=self.tmpa[:, :], in_=mean[:, :], func=AF.Square))
            kb.op("dve", [msqb, self.tmpab], [self.tmpab],
                  lambda e: e.tensor_tensor(out=self.tmpa[:, :], in0=msq[:, :], in1=self.tmpa[:, :], op=ALU.subtract))
            kb.op("act", [self.tmpab, self.epsb], [self.rtb],
                  lambda e: e.activation(out=self.rt[:, :], in_=self.tmpa[:, :], func=AF.Ln, bias=self.epsc[:, 0:1]))
            kb.op("act", [self.rtb], [self.rrb],
                  lambda e: e.activation(out=self.rr[:, :], in_=self.rt[:, :], func=AF.Exp, scale=-0.5))
            for c in range(NCH):
                k = self.tbi % 2
                self.tbi += 1
                tv = self.tmpb[:, k, :]
                cv = self.xn[:, c, ts]
                kb.op("dve", [self.xnb[c][t], meanb], [self.tmpbb[k]],
                      lambda e, tv=tv, cv=cv: e.tensor_tensor(out=tv, in0=cv, in1=mean[:, :], op=ALU.subtract))
                kb.op("dve", [self.tmpbb[k], self.rrb], [self.tmpbb[k]],
                      lambda e, tv=tv: e.tensor_tensor(out=tv, in0=tv, in1=self.rr[:, :], op=ALU.mult))
                av = self.act[:, c, PAD + t * 512: PAD + (t + 1) * 512]
                kb.op("act", [self.tmpbb[k], self.cpb], [self.actb[c][t]],
                      lambda e, av=av, tv=tv, c=c: e.activation(out=av, in_=tv, func=AF.Silu, bias=lnb(c), scale=lnw(c)))
        for c in range(NCH):
            w2, w2b = self.wtile()
            for t in range(NTB):
                ts = slice(t * 512, (t + 1) * 512)
                q = self.yi % 2
                self.yi += 1
                py, pyb = self.pb[4 + q], self.pbb[4 + q]

                def mm2(e, w2=w2, py=py, t=t):
                    for k in range(NCH):
                        ins = e.matmul(out=py[:, :], lhsT=w2[:, k * 128:(k + 1) * 128],
                                       rhs=self.act[:, k, PAD + t * 512: PAD + (t + 1) * 512],
                                       start=(k == 0), stop=(k == NCH - 1))
                    return ins
                kb.op("pe", [w2b] + [self.actb[k][t] for k in range(NCH)], [pyb], mm2)
                xv = self.x[:, c, ts]
                kb.op("dve", [pyb, self.cpb], [self.xb[c][t]],
                      lambda e, xv=xv, py=py, c=c: e.scalar_tensor_tensor(
                          out=xv, in0=py[:, :], scalar=b2(c), in1=xv, op0=ALU.add, op1=ALU.add))
        kb.fence()

    def evenmix(self, layer):
        kb = self.kb
        e_i = layer // 2
        self.rmsnorm(layer * 3 + 1)
        kb.fence()
        A = self.arena
        o0 = NCH * T
        qa = A[:, o0:o0 + 4 * T].rearrange("p (c t) -> p c t", c=4)
        ka = A[:, o0 + 4 * T:o0 + 6 * T].rearrange("p (h t) -> p h t", h=2)
        vax = A[:, o0 + 6 * T:o0 + 6 * T + 16 * 256].rearrange("p (n k j) -> p n k j", n=16, k=2)
        onx = A[:, o0 + 6 * T + 4096:o0 + 6 * T + 4096 + 256].rearrange("p (k j) -> p k j", k=2)
        qab = [[Buf() for _ in range(NTB)] for _ in range(4)]
        kab = [Buf() for _ in range(NTB)]
        vab = [Buf() for _ in range(4)]
        onxb = Buf()
        eo = e_i * EPW
        if "att" in EVEN_PARTS:
            self.attention(qa, ka, vax, onx, qab, kab, vab, onxb, eo)
        else:
            for c in range(4):
                for n in range(16):
                    kb.op("dve", [], [self.mixb[c][n]],
                          lambda e, c=c, n=n: e.memset(self.mix[:, c, n * 128:(n + 1) * 128], 0.0))
        kb.fence()
        if "dn" in EVEN_PARTS:
            self.deltanet(layer, eo)
        else:
            for c in range(4, 8):
                for n in range(16):
                    kb.op("dve", [], [self.mixb[c][n]],
                          lambda e, c=c, n=n: e.memset(self.mix[:, c, n * 128:(n + 1) * 128], 0.0))
        kb.fence()
        for c in range(NCH):
            w2, w2b = self.wtile()
            for t in range(NTB):
                ts = slice(t * 512, (t + 1) * 512)
                q = self.yi % 2
                self.yi += 1
                py, pyb = self.pb[4 + q], self.pbb[4 + q]

                def mm2(e, w2=w2, py=py, ts=ts):
                    for k in range(NCH):
                        ins = e.matmul(out=py[:, :], lhsT=w2[:, k * 128:(k + 1) * 128], rhs=self.mix[:, k, ts],
                                       start=(k == 0), stop=(k == NCH - 1))
                    return ins
                rd = [w2b] + [self.mixb[k][4 * t + i] for k in range(NCH) for i in range(4)]
                kb.op("pe", rd, [pyb], mm2)
                xv = self.x[:, c, ts]
                kb.op("dve", [pyb], [self.xb[c][t]],
                      lambda e, xv=xv, py=py: e.tensor_tensor(out=xv, in0=py[:, :], in1=xv, op=ALU.add))
        kb.fence()

    def attention(self, qa, ka, vax, onx, qab, kab, vab, onxb, eo):
        kb = self.kb
        kb.op("dve", [], vab, lambda e: e.memset(vax, 0.0))
        kb.op("dve", [], kab, lambda e: e.memset(ka, 0.0))
        kb.op("dve", [], [onxb], lambda e: e.memset(onx, 0.0))
        kb.op("dve", [onxb], [onxb], lambda e: e.memset(onx[:, 0, 0:64], 1.0))
        kb.op("dve", [onxb], [onxb], lambda e: e.memset(onx[:, 1, 64:128], 1.0))
        sk = self.sk
        kb.op("act", [self.epb], [self.skb],
              lambda e: e.activation(out=sk[:, :], in_=self.ep[:, eo:eo + 4], func=AF.Exp))
        for c in range(5):
            w, wb = self.wtile()
            for t in range(NTB):
                ts = slice(t * 512, (t + 1) * 512)
                q = self.gi % 2
                self.gi += 1
                p, pbf = self.pb[2 * q], self.pbb[2 * q]

                def mm(e, w=w, p=p, ts=ts):
                    for k in range(NCH):
                        ins = e.matmul(out=p[:, :], lhsT=w[:, k * 128:(k + 1) * 128], rhs=self.xn[:, k, ts],
                                       start=(k == 0), stop=(k == NCH - 1))
                    return ins
                kb.op("pe", [wb] + [self.xnb[k][t] for k in range(NCH)], [pbf], mm)
                if c < 4:
                    ov, ob = qa[:, c, ts], qab[c][t]
                    kb.op("act", [pbf], [ob], lambda e, ov=ov, p=p: e.activation(out=ov, in_=p[:, :], func=AF.Copy))
                else:
                    for h in range(2):
                        kb.op("act", [pbf], [kab[t]], lambda e, h=h, p=p, ts=ts: e.activation(
                            out=ka[h * 64:(h + 1) * 64, h, ts], in_=p[h * 64:(h + 1) * 64, :], func=AF.Copy))
        wv, wvb = self.wtile()
        for t in range(NTB):
            q = self.gi % 2
            self.gi += 1
            p, pbf = self.pb[2 * q], self.pbb[2 * q]

            def mmv(e, p=p, t=t):
                for i in range(4):
                    n = 4 * t + i
                    for k in range(NCH):
                        ins = e.matmul(out=p[:, i * 128:(i + 1) * 128], lhsT=self.xn[:, k, n * 128:(n + 1) * 128],
                                       rhs=wv[:, k * 128:(k + 1) * 128], start=(k == 0), stop=(k == NCH - 1))
                return ins
            kb.op("pe", [wvb] + [self.xnb[k][t] for k in range(NCH)], [pbf], mmv)
            pv = p[:, :].rearrange("p (n j) -> p n j", n=4)
            kb.op("act", [pbf], [vab[t]],
                  lambda e, pv=pv, t=t: e.activation(out=vax[:, 4 * t:4 * t + 4, 0, 0:64], in_=pv[:, :, 0:64], func=AF.Copy))
            kb.op("act", [pbf], [vab[t]],
                  lambda e, pv=pv, t=t: e.activation(out=vax[:, 4 * t:4 * t + 4, 1, 64:128], in_=pv[:, :, 64:128], func=AF.Copy))
        import os
        if os.environ.get("ATT_STOP") == "3":
            for c in range(4):
                for n in range(16):
                    kb.op("dve", [], [self.mixb[c][n]],
                          lambda e, c=c, n=n: e.memset(self.mix[:, c, n * 128:(n + 1) * 128], 0.0))
            return
        d0 = self.cst[:, 0:256]
        hb = (lambda h: 0) if os.environ.get("ATT_NOB64") else (lambda h: h)
        def att_iter(n, c, t, qs, ks, nk):
            q = self.gi % 2
            self.gi += 1
            p, pbf = self.pb[2 * q], self.pbb[2 * q]
            po, pob = self.pb[2 * q + 1], self.pbb[2 * q + 1]

            def mms(e, p=p, c=c, qs=qs, ks=ks, nk=nk):
                for h in range(2):
                    for kk in range(nk):
                        ins = e.matmul(out=p[:, h * 256 + kk * 128:h * 256 + (kk + 1) * 128],
                                       lhsT=ka[:, h, ks[kk]], rhs=qa[:, c, qs],
                                       start=True, stop=True)
                return ins
            rd = [qab[c][t], kab[t]] + ([kab[(n - 1) // 4]] if n > 0 else [])
            kb.op("pe", rd, [pbf], mms)
            w = 128 * nk
            si = self.sgi2 % 2
            self.sgi2 += 1
            tm = self.sg[:, si, :]
            tmb = self.sgb[si]
            k = self.sqi % 4
            self.sqi += 1
            pt = self.sq[:, k, :]
            ptb = self.sqb[k]
            for h in range(2):
                sl = 8.0 * SLOPES[c + 4 * h]
                kb.op("dve", [pbf, self.cstb], [tmb],
                      lambda e, h=h, sl=sl, p=p, tm=tm, w=w: e.scalar_tensor_tensor(
                          out=tm[:, h * 256:h * 256 + w], in0=p[:, h * 256:h * 256 + w], scalar=1.0 / sl, in1=d0[:, 0:w],
                          op0=ALU.mult, op1=ALU.add))
                kb.op("act", [tmb], [ptb], lambda e, h=h, sl=sl, pt=pt, tm=tm, w=w: e.activation(
                    out=pt[:, h * 256:h * 256 + w], in_=tm[:, h * 256:h * 256 + w], func=AF.Exp, scale=0.125 * sl))

            yield "A"
            if os.environ.get("ATT_STOP") == "4":
                kb.op("dve", [ptb], [self.mixb[c][n]],
                      lambda e, c=c, qs=qs: e.memset(self.mix[:, c, qs], 0.0))
                return

            def mmo(e, po=po, pt=pt, n=n, nk=nk):
                for part, (lh, off) in enumerate(((vax, 0), (onx, 128))):
                    cnt = 0
                    for h in range(2):
                        for kk in range(nk):
                            lhsT = vax[:, n - kk, h, :] if part == 0 else onx[:, h, :]
                            ins = e.matmul(out=po[:, off:off + 128], lhsT=lhsT,
                                           rhs=pt[:, h * 256 + kk * 128:h * 256 + (kk + 1) * 128],
                                           start=(cnt == 0), stop=(cnt == 2 * nk - 1))
                            cnt += 1
                return ins
            rd = [ptb, onxb, vab[t]] + ([vab[(n - 1) // 4]] if n > 0 else [])
            kb.op("pe", rd, [pob], mmo)
            if os.environ.get("ATT_STOP") == "5":
                kb.op("dve", [pob], [self.mixb[c][n]],
                      lambda e, c=c, qs=qs: e.memset(self.mix[:, c, qs], 0.0))
                return
            dn = self.den
            kb.op("dve", [pob, self.skb], [self.denb],
                  lambda e, po=po, c=c: e.tensor_scalar_add(out=dn[:, :], in0=po[:, 128:256], scalar1=sk[:, c:c + 1]))
            kb.op("dve", [self.denb], [self.denb], lambda e: e.reciprocal(out=dn[:, :], in_=dn[:, :]))
            kb.op("dve", [pob, self.denb], [self.mixb[c][n]],
                  lambda e, po=po, c=c, qs=qs: e.tensor_tensor(out=self.mix[:, c, qs], in0=po[:, 0:128], in1=dn[:, :],
                                                               op=ALU.mult))

        prev = None
        for n in range(16):
            t = n // 4
            qs = slice(n * 128, (n + 1) * 128)
            ks = [slice(n * 128, (n + 1) * 128), slice((n - 1) * 128, n * 128)]
            nk = 1 if n == 0 else 2
            for c in range(4):
                g = att_iter(n, c, t, qs, ks, nk)
                next(g, None)
                if prev is not None:
                    for _ in prev:
                        pass
                prev = g
        for _ in prev:
            pass

    def deltanet(self, layer, eo):
        kb = self.kb
        self.dbg_list = []
        A = self.arena
        pos = [0]

        def alloc(n_bf16, dt=BF16):
            a0 = pos[0]
            self.dbg_list.append((a0, n_bf16, dt == F32))
            pos[0] += (n_bf16 + 15) // 16 * 16
            assert pos[0] <= ARENA, ("arena overflow", pos[0], ARENA)
            self.arena_used = pos[0]
            v = A[:, a0:a0 + n_bf16]
            return v.bitcast(F32) if dt == F32 else v

        G = GT
        xg = alloc(8 * G).rearrange("p (c t) -> p c t", c=8)
        cin = alloc(12 * (G + 4)).rearrange("p (c t) -> p c t", c=12)
        ctmp = alloc(2 * 2 * G, F32).rearrange("p (a t) -> p a t", a=2)
        qT = alloc(4 * G).rearrange("p (c t) -> p c t", c=4)
        kT = alloc(4 * G).rearrange("p (c t) -> p c t", c=4)
        vT = alloc(4 * G).rearrange("p (c t) -> p c t", c=4)
        kTz = alloc(2 * 4 * G).rearrange("p (b c t) -> p b c t", b=2, c=4)
        k_tm = alloc(NCK * 512).rearrange("p (c f) -> p c f", c=NCK)
        v_tm = alloc(NCK * 512).rearrange("p (c f) -> p c f", c=NCK)
        zs = alloc(NCK * 512).rearrange("p (c f) -> p c f", c=NCK)
        ba = alloc(2 * NCK * 16, F32).rearrange("p (c f) -> p c f", c=NCK)
        beta = alloc(2 * NCK * 8, F32).rearrange("p (c f) -> p c f", c=NCK)
        gt = alloc(2 * NCK * 8, F32).rearrange("p (c f) -> p c f", c=NCK)
        gtmp = alloc(2 * NCK * 8, F32).rearrange("p (c f) -> p c f", c=NCK)
        nA = alloc(2 * 8, F32)
        bd16 = alloc(128)
        ones64 = self.ones64[:, :]
        onec = alloc(2 * 1, F32)
        rhsG = alloc(2 * 512, F32).rearrange("p (h j) -> p h j", h=8)
        rhsB = alloc(2 * 512, F32).rearrange("p (h j) -> p h j", h=8)
        gcs = alloc(2 * 8, F32)
        sc = alloc(2 * 8, F32)
        kdsc = alloc(2 * 8, F32)
        dT = alloc(2 * 512, F32).rearrange("p (h j) -> p h j", h=8)
        t1 = alloc(2 * 512, F32).rearrange("p (h j) -> p h j", h=8)
        eg = alloc(2 * 512, F32).rearrange("p (h j) -> p h j", h=8)
        u_sb = alloc(2 * 512, F32).rearrange("p (h j) -> p h j", h=8)
        o_sb = alloc(2 * 512, F32).rearrange("p (h j) -> p h j", h=8)
        S32 = alloc(2 * 512, F32).rearrange("p (c j) -> p c j", c=4)
        tmpS = t1.rearrange("p h j -> p (h j)").rearrange("p (c j) -> p c j", c=4)
        gl = alloc(2 * 4, F32)
        ss8 = alloc(2 * 8, F32)
        Lt = alloc(512).rearrange("p (h j) -> p h j", h=8)
        Tm = alloc(512).rearrange("p (h j) -> p h j", h=8)
        Tt = alloc(512).rearrange("p (h j) -> p h j", h=8)
        Ym = alloc(512).rearrange("p (h j) -> p h j", h=8)
        attnT = alloc(512).rearrange("p (h j) -> p h j", h=8)
        vb = alloc(512).rearrange("p (h j) -> p h j", h=8)
        kdec = alloc(512).rearrange("p (h j) -> p h j", h=8)
        vnew = alloc(512).rearrange("p (h j) -> p h j", h=8)
        ogb = alloc(512).rearrange("p (h j) -> p h j", h=8)
        kbgx = alloc(1024).rearrange("p (h j) -> p h j", h=8)
        wT = alloc(256).rearrange("p (c j) -> p c j", c=4)
        qd = alloc(256).rearrange("p (c j) -> p c j", c=4)
        Sbf = alloc(512).rearrange("p (c j) -> p c j", c=4)
        B = {n: Buf() for n in ("xg", "ctmp0", "ctmp1", "qT", "kT", "vT", "k_tm", "v_tm", "zs", "ba", "beta", "gt", "gtmp",
                                "nA", "bd16", "ones64", "onec", "rhsG", "rhsB", "gcs", "sc", "kdsc", "dT", "t1", "eg", "u",
                                "o", "S32", "tmpS", "gl", "ss8", "Lt", "T", "Tt", "Y", "attnT", "vb", "kdec", "vnew", "ogb",
                                "kbgx", "wT", "qd", "Sbf", "sqd", "rtd", "rsd")}
        cinb = [Buf() for _ in range(12)]
        bank = [0]

        def nb():
            i = bank[0] % 6
            bank[0] += 1
            return self.pb[i], self.pbb[i]

        cstb, epb = self.cstb, self.epb
        U64 = self.cst[0:64, 256:320]
        NEGT = self.cst[0:64, 320:384]
        LM = lambda bi: self.cst[0:64, 448 + bi * 64:448 + (bi + 1) * 64]
        NLM = lambda bi: self.cst[0:64, 960 + bi * 64:960 + (bi + 1) * 64]
        BDM = self.cst[:, 832:960]
        I64f = self.identf[0:64, 0:64]
        I64b = self.ident[0:64, 0:64]
        cw = lambda fc, k: self.ep[:, eo + 4 + fc * 4 + k:eo + 4 + fc * 4 + k + 1]
        alog = self.ep[0:64, eo + 52:eo + 60]
        dtb = self.ep[0:64, eo + 60:eo + 68]
        nwd = self.ep[0:64, eo + 68:eo + 132]
        V, Aop = "dve", "act"
        PL = "dve" if os.environ.get("DN_NOPOOL") else "pool"

        def bc(ap, shape):
            return ap.broadcast_to(shape)

        kb.op(V, [], cinb, lambda e: e.memset(cin, 0.0))
        kb.op(V, [], [B["kT"]], lambda e: e.memset(kTz, 0.0))
        kb.op(V, [], [B["S32"]], lambda e: e.memset(S32, 0.0))
        kb.op(V, [], [B["Sbf"]], lambda e: e.memset(Sbf, 0.0))
        kb.op(V, [], [B["kbgx"]], lambda e: e.memset(kbgx, 0.0))
        kb.op(V, [], [B["ones64"]], lambda e: e.memset(ones64, 1.0))
        kb.op(V, [], [B["onec"]], lambda e: e.memset(onec, 1.0))
        kb.op(V, [cstb], [B["bd16"]], lambda e: e.tensor_copy(out=bd16, in_=BDM))
        kb.op(Aop, [epb], [B["nA"]], lambda e: e.activation(out=nA[0:64, :], in_=alog, func=AF.Exp))
        kb.op(V, [B["nA"]], [B["nA"]], lambda e: e.tensor_scalar_mul(out=nA[0:64, :], in0=nA[0:64, :], scalar1=-1.0))
        sqd = self.sq[:, 0, 0:G]
        rtd = self.rt[:, 0:G]
        rsd = self.rr[:, 0:G]
        nidx = layer * 3 + 1
        tpb = self.pb[7][:, :].bitcast(BF16)
        tpbb = self.pbb[7]
        stat, statb = self.pb[6], self.pbb[6]


        for _nm in ("rhsG", "rhsB", "gcs", "sc", "kdsc", "dT", "t1", "eg", "Lt", "T", "Tt", "Y"):
            B[_nm + "0"] = Buf()
            B[_nm + "1"] = Buf()

        def BB(nm):
            return [B[nm + "0"], B[nm + "1"]]
        SH = [64, 4, 64]

        def v3h(ap):
            return ap.rearrange("p (h j) -> p h j", h=4)
        bkH = [[0], [0]]

        def mk_nb(hh):
            def f():
                i = 2 * hh + bkH[hh][0] % 2
                bkH[hh][0] += 1
                return self.pb[i], self.pbb[i]
            return f
        nbPh = [mk_nb(0), mk_nb(1)]
        B["t1r"] = Buf()
        t1r = self.sg[:, 0, :].rearrange("p (h j) -> p h j", h=8)
        tmpSr = self.sg[:, 0, :].rearrange("p (c j) -> p c j", c=4)
        bkP, bkR = [0], [0]

        def nbP():
            i = bkP[0] % 4
            bkP[0] += 1
            return self.pb[i], self.pbb[i]

        def nbR():
            i = 4 + bkR[0] % 2
            bkR[0] += 1
            return self.pb[i], self.pbb[i]
        B["rtd"] = self.rtb
        B["rsd"] = self.rrb
        B["sqd"] = self.sqb[0]
        AX = mybir.AxisListType.X
        S8 = [64, 8, 64]

        def v3(ap):
            return ap.rearrange("p (h j) -> p h j", h=8)

        def chunk_pa(n, ch, cs, hh):
            nb = nbPh[hh]
            h0 = 4 * hh
            hs = slice(h0, h0 + 4)
            gch = gt[0:64, ch, hs]
            bch = beta[0:64, ch, hs]
            kb.op(PL, [cstb, B["gt"]], [B["rhsG%d" % hh]], lambda e: e.tensor_tensor(
                out=rhsG[0:64, hs], in0=bc(U64.unsqueeze(1), SH), in1=bc(gch.unsqueeze(2), SH), op=ALU.mult))
            kb.op(PL, [self.identb, B["beta"]], [B["rhsB%d" % hh]], lambda e: e.tensor_tensor(
                out=rhsB[0:64, hs], in0=bc(I64f.unsqueeze(1), SH), in1=bc(bch.unsqueeze(2), SH), op=ALU.mult))
            pG, pGb = nb()
            kb.op("pe", [B["ones64"], B["rhsG%d" % hh]], [pGb], lambda e: e.matmul(
                out=pG[:, 0:256], lhsT=ones64[0:64, :], rhs=rhsG[0:64, hs].rearrange("p h j -> p (h j)"), start=True, stop=True))
            pC, pCb = nb()
            kb.op("pe", [cstb, B["gt"]], [pCb], lambda e: e.matmul(out=pC[0:64, 0:4], lhsT=U64, rhs=gch, start=True, stop=True))
            kb.op(V, [pCb], [B["gcs%d" % hh]], lambda e: e.tensor_copy(out=gcs[0:64, hs], in_=pC[0:64, 0:4]))
            pG3 = v3h(pG[0:64, 0:256])
            kb.op(V, [pGb, B["gcs%d" % hh]], [B["t1%d" % hh]], lambda e: e.tensor_tensor(
                out=t1[0:64, hs], in0=pG3, in1=bc(gcs[0:64, hs].unsqueeze(2), SH), op=ALU.subtract))
            kb.op(V, [B["t1%d" % hh], cstb], [B["t1%d" % hh]], lambda e: e.tensor_tensor(
                out=t1[0:64, hs], in0=t1[0:64, hs], in1=bc(NEGT.unsqueeze(1), SH), op=ALU.add))
            kb.op(Aop, [B["t1%d" % hh]], [B["dT%d" % hh]], lambda e: e.activation(out=dT[0:64, hs], in_=t1[0:64, hs], func=AF.Exp))
            kb.op(Aop, [pGb], [B["eg%d" % hh]], lambda e: e.activation(
                out=eg[:, hs].rearrange("p h j -> p (h j)"), in_=pG[:, 0:256], func=AF.Exp))
            kb.op(V, [pGb, B["gcs%d" % hh]], [B["kdsc%d" % hh]], lambda e: e.tensor_tensor(
                out=kdsc[0:64, hs], in0=pG3[:, :, 63], in1=gcs[0:64, hs], op=ALU.subtract))
            kb.op(Aop, [B["kdsc%d" % hh]], [B["kdsc%d" % hh]], lambda e: e.activation(out=kdsc[0:64, hs], in_=kdsc[0:64, hs], func=AF.Exp))
            kb.op(Aop, [B["gcs%d" % hh]], [B["sc%d" % hh]], lambda e: e.activation(out=sc[0:64, hs], in_=gcs[0:64, hs], func=AF.Exp))
            kb.op(V, [B["sc%d" % hh], B["beta"]], [B["sc%d" % hh]], lambda e: e.tensor_tensor(
                out=sc[0:64, hs], in0=sc[0:64, hs], in1=bch, op=ALU.mult))
            yield "PA"
            pB, pBb = nb()
            kb.op("pe", [B["ones64"], B["rhsB%d" % hh]], [pBb], lambda e: e.matmul(
                out=pB[0:64, 0:256], lhsT=ones64[0:64, 0:64], rhs=rhsB[0:64, hs].rearrange("p h j -> p (h j)"), start=True, stop=True))
            pK, pKb = nb()

            def mmk(e, p=pK, other=kT):
                for h in range(h0, h0 + 4):
                    hb, pr = h % 2, h // 2
                    rhs = kTz[:, hb, pr, cs] if other is kT else other[:, pr, cs]
                    ins = e.matmul(out=p[0:64, (h - h0) * 64:(h - h0 + 1) * 64], lhsT=kTz[:, hb, pr, cs], rhs=rhs, start=True, stop=True)
                return ins
            kb.op("pe", [B["kT"]], [pKb], mmk)
            kb.op(V, [B["dT%d" % hh], self.identb], [B["t1%d" % hh]], lambda e: e.tensor_tensor(
                out=t1[0:64, hs], in0=dT[0:64, hs], in1=bc(I64f.unsqueeze(1), SH), op=ALU.subtract))
            kb.op(V, [pKb, B["t1%d" % hh]], [B["t1%d" % hh]], lambda e: e.tensor_tensor(
                out=t1[0:64, hs], in0=v3h(pK[0:64, 0:256]), in1=t1[0:64, hs], op=ALU.mult))
            kb.op(V, [pBb, B["t1%d" % hh]], [B["Lt%d" % hh]], lambda e: e.tensor_tensor(
                out=Lt[0:64, hs], in0=v3h(pB[0:64, 0:256]), in1=t1[0:64, hs], op=ALU.mult))
            yield "PA"
            for bi in range(6):
                first, last = bi == 0, bi == 5
                bsz = 1 << bi
                nk = 64 // (2 * bsz)

                def half(ap3, t, bsz=bsz):
                    return ap3.rearrange("p h (k t r) -> p h k t r", t=2, r=bsz)[:, :, :, t, :]

                def half2(ap2, t, bsz=bsz):
                    return ap2.rearrange("p (k t r) -> p k t r", t=2, r=bsz)[:, :, t, :]

                def cmp3(p, bsz=bsz, nk=nk):
                    return p[0:64, 0:128].rearrange("p (h k r) -> p h k r", h=4, r=bsz)
                pY, pYb = nb()

                def mmy(e, pY=pY, first=first):
                    for h in range(h0, h0 + 4):
                        ins = e.matmul(out=pY[0:64, (h - h0) * 64:(h - h0 + 1) * 64], lhsT=Lt[0:64, h, :],
                                       rhs=(I64b if first else Tm[0:64, h, :]), start=True, stop=True)
                    return ins
                kb.op("pe", [B["Lt%d" % hh], self.identb] + ([] if first else [B["T%d" % hh]]), [pYb], mmy)
                kb.op(V, [pYb, cstb], [B["Y%d" % hh]], lambda e, pY=pY, bi=bi: e.tensor_tensor(
                    out=Ym[0:64, hs], in0=v3h(pY[0:64, 0:256]), in1=bc(NLM(bi).unsqueeze(1), SH), op=ALU.mult))
                yield "PA"
                if not last:
                    pZ, pZb = nb()

                    def mmz1(e, pZ=pZ, first=first, half2=half2, bsz=bsz, nk=nk):
                        for h in range(h0, h0 + 4):
                            o_ = pZ[0:64, (h - h0) * 32:(h - h0 + 1) * 32].rearrange("p (k r) -> p k r", r=bsz)
                            e.matmul(out=o_, lhsT=(I64b if first else Tt[0:64, h, :]), rhs=half2(Ym[0:64, h, :], 0),
                                     start=True, stop=False)
                            ins = e.matmul(out=o_, lhsT=I64b, rhs=half2((I64b if first else Tm[0:64, h, :]), 0),
                                           start=False, stop=True)
                        return ins
                    kb.op("pe", [B["Y%d" % hh], self.identb] + ([] if first else [B["Tt%d" % hh], B["T%d" % hh]]), [pZb], mmz1)
                pZ2, pZ2b = nb()

                def mmz2(e, pZ2=pZ2, first=first, half2=half2, bsz=bsz):
                    for h in range(h0, h0 + 4):
                        o_ = pZ2[0:64, (h - h0) * 32:(h - h0 + 1) * 32].rearrange("p (k r) -> p k r", r=bsz)
                        ins = e.matmul(out=o_, lhsT=Ym[0:64, h, :], rhs=half2((I64b if first else Tt[0:64, h, :]), 1),
                                       start=True, stop=True)
                    return ins
                kb.op("pe", [B["Y%d" % hh], self.identb] + ([] if first else [B["Tt%d" % hh]]), [pZ2b], mmz2)
                if first:
                    kb.op(V, [self.identb], [B["T%d" % hh]], lambda e: e.tensor_copy(out=Tm[0:64, hs], in_=bc(I64b.unsqueeze(1), SH)))
                    kb.op(V, [self.identb], [B["Tt%d" % hh]], lambda e: e.tensor_copy(out=Tt[0:64, hs], in_=bc(I64b.unsqueeze(1), SH)))
                if not last:
                    kb.op(Aop, [pZb], [B["T%d" % hh]], lambda e, pZ=pZ, half=half, cmp3=cmp3: e.activation(
                        out=half(Tm[0:64, hs], 0), in_=cmp3(pZ), func=AF.Copy))
                kb.op(V, [pZ2b, B["Tt%d" % hh]], [B["Tt%d" % hh]], lambda e, pZ2=pZ2, half=half, cmp3=cmp3: e.tensor_tensor(
                    out=half(Tt[0:64, hs], 1), in0=half(Tt[0:64, hs], 1), in1=cmp3(pZ2), op=ALU.add))
                yield "PA"
            yield "PAend"

        def chunk_rest(n, ch, cs):
            nb = nbPh[0]
            gch = gt[0:64, ch, :]
            bch = beta[0:64, ch, :]

            def mmk(e, p, other):
                for h in range(8):
                    hb, pr = h % 2, h // 2
                    ins = e.matmul(out=p[0:64, h * 64:(h + 1) * 64], lhsT=kTz[:, hb, pr, cs], rhs=other[:, pr, cs], start=True, stop=True)
                return ins
            pQ, pQb = nb()
            kb.op("pe", [B["kT"], B["qT"]], [pQb], lambda e: mmk(e, pQ, qT))
            kb.op(V, [pQb, *BB("dT")], [B["attnT"]], lambda e: e.tensor_tensor(
                out=attnT[0:64], in0=v3(pQ[0:64, :]), in1=dT[0:64], op=ALU.mult))
            k8 = v3(k_tm[0:64, ch, :])
            kb.op(PL, [B["v_tm"], B["beta"]], [B["vb"]], lambda e: e.tensor_tensor(
                out=vb[0:64], in0=v3(v_tm[0:64, ch, :]), in1=bc(bch.unsqueeze(2), S8), op=ALU.mult))
            kx5 = kbgx[0:64].rearrange("p (pr hb) (hf d) -> p pr hb hf d", hb=2, hf=2)
            k4 = k_tm[0:64, ch, :].rearrange("p (pr hb d) -> p pr hb d", pr=4, hb=2)
            sc4 = sc[0:64, :].rearrange("p (pr hb) -> p pr hb", hb=2)
            for hb in range(2):
                kb.op(PL, [B["k_tm"], *BB("sc")], [B["kbgx"]], lambda e, hb=hb: e.tensor_tensor(
                    out=kx5[:, :, hb, hb, :], in0=k4[:, :, hb, :], in1=bc(sc4[:, :, hb].unsqueeze(2), [64, 4, 64]), op=ALU.mult))
            kb.op(PL, [B["k_tm"], *BB("kdsc")], [B["kdec"]], lambda e: e.tensor_tensor(
                out=kdec[0:64], in0=k8, in1=bc(kdsc[0:64, :].unsqueeze(2), S8), op=ALU.mult))
            pU, pUb = nb()

            def mmu(e):
                for h in range(8):
                    ins = e.matmul(out=pU[0:64, h * 64:(h + 1) * 64], lhsT=Tt[0:64, h, :], rhs=vb[0:64, h, :], start=True, stop=True)
                return ins
            kb.op("pe", [*BB("Tt"), B["vb"]], [pUb], mmu)
            kb.op(Aop, [pUb], [B["u"]], lambda e: e.activation(
                out=u_sb[0:64].rearrange("p h j -> p (h j)"), in_=pU[0:64, :], func=AF.Copy))
            pW, pWb = nb()

            def mmw(e):
                for h in range(8):
                    pr, hb = h // 2, h % 2
                    ins = e.matmul(out=pW[:, pr * 64:(pr + 1) * 64], lhsT=kbgx[0:64, h, :], rhs=Tt[0:64, h, :],
                                   start=(hb == 0), stop=(hb == 1))
                return ins
            kb.op("pe", [*BB("Tt"), B["kbgx"]], [pWb], mmw)
            kb.op(Aop, [pWb], [B["wT"]], lambda e: e.activation(
                out=wT.rearrange("p c j -> p (c j)"), in_=pW[:, 0:256], func=AF.Copy))
            eg5 = eg.rearrange("p (pr hb) j -> p pr hb j", hb=2)
            for hb in range(2):
                ps_ = slice(hb * 64, (hb + 1) * 64)
                kb.op(PL, [B["qT"], *BB("eg")], [B["qd"]], lambda e, hb=hb, ps_=ps_: e.tensor_tensor(
                    out=qd[ps_, :, :], in0=qT[ps_, :, cs], in1=eg5[ps_, :, hb, :], op=ALU.mult))
                kb.op(PL, [*BB("eg")], [B["gl"]], lambda e, hb=hb, ps_=ps_: e.tensor_copy(
                    out=gl[ps_, :], in_=eg5[ps_, :, hb, 63]))
            yield "Rstart"
            nb = nbR
            pA, pAb = nb()

            def mma(e):
                for pr in range(4):
                    ins = e.matmul(out=pA[0:64, pr * 128:(pr + 1) * 128], lhsT=wT[:, pr, :], rhs=Sbf[:, pr, :], start=True, stop=True)
                return ins
            kb.op("pe", [B["wT"], B["Sbf"]], [pAb], mma)
            kb.op(V, [B["u"], pAb], [B["vnew"]], lambda e: e.tensor_tensor(
                out=vnew[0:64], in0=u_sb[0:64], in1=v3(pA[0:64, :]), op=ALU.subtract))
            yield "R"
            pO, pOb = nb()

            def mmo(e):
                for pr in range(4):
                    for hb in range(2):
                        h = 2 * pr + hb
                        e.matmul(out=pO[0:64, h * 64:(h + 1) * 64], lhsT=qd[:, pr, :], rhs=Sbf[:, pr, hb * 64:(hb + 1) * 64],
                                 start=True, stop=False)
                        ins = e.matmul(out=pO[0:64, h * 64:(h + 1) * 64], lhsT=attnT[0:64, h, :], rhs=vnew[0:64, h, :],
                                       start=False, stop=True)
                return ins
            kb.op("pe", [B["qd"], B["Sbf"], B["attnT"], B["vnew"]], [pOb], mmo)
            kb.op(Aop, [pOb], [B["o"]], lambda e: e.activation(
                out=o_sb[0:64].rearrange("p h j -> p (h j)"), in_=pO[0:64, :], func=AF.Copy))
            yield "R"
            pS, pSb = nb()

            def mms(e):
                for pr in range(4):
                    for hb in range(2):
                        h = 2 * pr + hb
                        ins = e.matmul(out=pS[hb * 64:(hb + 1) * 64, pr * 128 + hb * 64:pr * 128 + (hb + 1) * 64],
                                       lhsT=kdec[0:64, h, :], rhs=vnew[0:64, h, :], start=True, stop=True)
                return ins
            kb.op("pe", [B["kdec"], B["vnew"]], [pSb], mms)
            kb.op(PL, [B["gl"], B["S32"]], [B["S32"]], lambda e: e.tensor_tensor(
                out=S32, in0=S32, in1=bc(gl.unsqueeze(2), [128, 4, 128]), op=ALU.mult))
            pS4 = pS[:, :].rearrange("p (c j) -> p c j", c=4)
            for hb in range(2):
                ps_ = slice(hb * 64, (hb + 1) * 64)
                kb.op(V, [pSb, B["S32"]], [B["S32"]], lambda e, ps_=ps_: e.tensor_tensor(
                    out=S32[ps_, :, ps_], in0=S32[ps_, :, ps_], in1=pS4[ps_, :, ps_], op=ALU.add))
            kb.op(Aop, [B["S32"]], [B["Sbf"]], lambda e: e.activation(
                out=Sbf.rearrange("p c j -> p (c j)"), in_=S32.rearrange("p c j -> p (c j)"), func=AF.Copy))
            yield "R"
            kb.op(Aop, [pOb], [B["t1r"]], lambda e: e.activation(
                out=t1r[0:64].rearrange("p h j -> p (h j)"), in_=pO[0:64, :], func=AF.Square))
            kb.op(V, [B["t1r"]], [B["ss8"]], lambda e: e.tensor_reduce(out=ss8[0:64, :], in_=t1r[0:64], axis=AX, op=ALU.add))
            kb.op(Aop, [B["ss8"], self.epsb], [B["ss8"]], lambda e: e.activation(
                out=ss8[0:64, :], in_=ss8[0:64, :], func=AF.Ln, bias=self.epsc[0:64, 0:1], scale=1.0 / 64))
            kb.op(Aop, [B["ss8"]], [B["ss8"]], lambda e: e.activation(
                out=ss8[0:64, :], in_=ss8[0:64, :], func=AF.Exp, scale=-0.5))
            kb.op(V, [B["o"], B["ss8"]], [B["t1r"]], lambda e: e.tensor_tensor(
                out=t1r[0:64], in0=o_sb[0:64], in1=bc(ss8[0:64, :].unsqueeze(2), S8), op=ALU.mult))
            kb.op(V, [B["t1r"], B["zs"]], [B["ogb"]], lambda e: e.tensor_tensor(
                out=ogb[0:64], in0=t1r[0:64], in1=v3(zs[0:64, ch, :]), op=ALU.mult))

            yield "R"

            def tro(e):
                for pr in range(4):
                    ins = e.transpose(out=tpb[:, pr * 64:(pr + 1) * 64],
                                      in_=ogb[0:64, 2 * pr:2 * pr + 2, :].rearrange("p h j -> p (h j)"),
                                      identity=self.ident[0:64, 0:64])
                return ins
            kb.op("pe", [B["ogb"], self.identb], [tpbb], tro)
            mb = [self.mixb[c][n // 2] for c in range(4, 8)]
            kb.op(Aop, [tpbb, epb], mb, lambda e: e.activation(
                out=self.mix[:, 4:8, n * 64:(n + 1) * 64], in_=tpb[:, 0:256].rearrange("p (c j) -> p c j", c=4), func=AF.Copy,
                scale=self.ep[:, eo + 132:eo + 133]))


        rt2 = [self.rt[:, 0:G], self.rt[:, G:2 * G]]
        rs2 = [self.rr[:, 0:G], self.rr[:, G:2 * G]]
        rt2b = [Buf(), Buf()]
        rs2b = [Buf(), Buf()]

        dgc = ctmp.rearrange("p a t -> p (a t)").bitcast(BF16).rearrange("p (a k j) -> p a k j", a=2, k=4)
        cwv = self.ep[:, eo + 4:eo + 52].rearrange("p (c k) -> p c k", k=4)

        def stage1a(fc):
            w, wb = self.wtile()
            p, pbf = nb()

            def mm(e, w=w, p=p):
                for k in range(NCH):
                    ins = e.matmul(out=p[:, 0:G], lhsT=w[:, k * 128:(k + 1) * 128], rhs=xg[:, k, :],
                                   start=(k == 0), stop=(k == NCH - 1))
                return ins
            kb.op("pe", [wb, B["xg"]], [pbf], mm)
            kb.op(Aop, [pbf], [cinb[fc]],
                  lambda e, p=p, fc=fc: e.activation(out=cin[:, fc, 3:3 + G], in_=p[:, 0:G], func=AF.Copy))

        def stage1b(fc):
            ci = fc % 2
            dg = dgc[:, ci]
            dgb = B["ctmp%d" % ci]
            kb.op(V, [self.identb, epb], [dgb], lambda e, dg=dg, fc=fc: e.tensor_tensor(
                out=dg, in0=bc(self.ident[:, :].unsqueeze(1), [128, 4, 128]),
                in1=bc(cwv[:, fc, :].unsqueeze(2), [128, 4, 128]), op=ALU.mult))
            pc, pcb = nb()

            def mmc(e, pc=pc, ci=ci, fc=fc):
                for k in range(4):
                    ins = e.matmul(out=pc[:, 0:G], lhsT=dgc[:, ci, k, :], rhs=cin[:, fc, k:k + G],
                                   start=(k == 0), stop=(k == 3))
                return ins
            kb.op("pe", [dgb, cinb[fc]], [pcb], mmc)
            kb.op(V, [cinb[fc]], [cinb[fc]],
                  lambda e, fc=fc: e.tensor_copy(out=cin[:, fc, 0:3], in_=cin[:, fc, G:G + 3]))
            if fc >= 8:
                dst, dstb = vT[:, fc - 8, :], B["vT"]
            elif fc >= 4:
                dst, dstb = kT[:, fc - 4, :], B["kT"]
            else:
                dst, dstb = qT[:, fc, :], B["qT"]
            kb.op(Aop, [pcb], [dstb], lambda e, pc=pc, dst=dst: e.activation(out=dst, in_=pc[:, 0:G], func=AF.Silu))

        def stage2a(fc):
            ci = fc % 2
            dst, dstb = (qT[:, fc, :], B["qT"]) if fc < 4 else (kT[:, fc - 4, :], B["kT"])
            sqv = self.sq[:, ci, 0:G]
            kb.op(Aop, [dstb], [self.sqb[ci]], lambda e, dst=dst, sqv=sqv: e.activation(out=sqv, in_=dst, func=AF.Square))
            ps_, psb_ = nb()
            kb.op("pe", [self.sqb[ci], B["bd16"]], [psb_],
                  lambda e, ps_=ps_, sqv=sqv: e.matmul(out=ps_[:, 0:G], lhsT=bd16, rhs=sqv, start=True, stop=True))
            return ps_, psb_

        def stage2b(fc, ps_, psb_):
            ci = fc % 2
            dst, dstb, scl = (qT[:, fc, :], B["qT"], 0.125) if fc < 4 else (kT[:, fc - 4, :], B["kT"], 1.0)
            kb.op(Aop, [psb_, self.epsb], [rt2b[ci]],
                  lambda e, ps_=ps_, ci=ci: e.activation(out=rt2[ci], in_=ps_[:, 0:G], func=AF.Ln, bias=self.epsc[:, 0:1]))
            kb.op(Aop, [rt2b[ci]], [rs2b[ci]],
                  lambda e, ci=ci: e.activation(out=rs2[ci], in_=rt2[ci], func=AF.Exp, scale=-0.5))
            kb.op(V, [dstb, rs2b[ci]], [dstb],
                  lambda e, dst=dst, scl=scl, ci=ci: e.scalar_tensor_tensor(
                      out=dst, in0=dst, scalar=scl, in1=rs2[ci], op0=ALU.mult, op1=ALU.mult))
            if fc >= 4:
                for hb in range(2):
                    kb.op(Aop, [dstb], [dstb], lambda e, hb=hb, fc=fc: e.activation(
                        out=kTz[hb * 64:(hb + 1) * 64, hb, fc - 4, :], in_=kT[hb * 64:(hb + 1) * 64, fc - 4, :], func=AF.Copy))

        for gi in range(T // G):
            t0 = gi * G
            tb = t0 // 512
            gs = slice(t0, t0 + G)
            for c in range(NCH):
                k = self.sqi % 4
                self.sqi += 1
                sqv = self.sq[:, k, 0:G]
                xv = self.x[:, c, gs]
                kb.op(Aop, [self.xb[c][tb]], [self.sqb[k]],
                      lambda e, sqv=sqv, xv=xv: e.activation(out=sqv, in_=xv, func=AF.Square))
                kb.op("pe", [self.sqb[k], self.onesb], [statb],
                      lambda e, sqv=sqv, c=c: e.matmul(out=stat[:, 0:G], lhsT=self.ones[:, :], rhs=sqv,
                                                       start=(c == 0), stop=(c == NCH - 1)))
            kb.op(Aop, [statb, self.epsb], [rt2b[0]],
                  lambda e: e.activation(out=rtd, in_=stat[:, 0:G], func=AF.Ln, bias=self.epsc[:, 0:1]))
            kb.op(Aop, [rt2b[0]], [rs2b[0]], lambda e: e.activation(out=rsd, in_=rtd, func=AF.Exp, scale=-0.5))
            for c in range(NCH):
                xv = self.x[:, c, gs]
                wv = self.nw[:, nidx * NCH + c: nidx * NCH + c + 1]
                kb.op(V, [self.xb[c][tb], rs2b[0], self.nwb], [B["xg"]],
                      lambda e, xv=xv, wv=wv, c=c: e.scalar_tensor_tensor(
                          out=xg[:, c, :], in0=xv, scalar=wv, in1=rsd, op0=ALU.mult, op1=ALU.mult))
            stage1a(0)
            for fc in range(12):
                if fc + 1 < 12:
                    stage1a(fc + 1)
                stage1b(fc)
            wz = [self.wtile() for _ in range(4)]
            for ch in range(NCK):
                p, pbf = nb()

                def mmz(e, p=p, ch=ch, wz=wz):
                    for k in range(NCH):
                        ins = e.matmul(out=p[0:64, :], lhsT=xg[:, k, ch * 64:(ch + 1) * 64],
                                       rhs=wz[k // 2][0][:, (k % 2) * 512:(k % 2 + 1) * 512],
                                       start=(k == 0), stop=(k == NCH - 1))
                    return ins
                kb.op("pe", [B["xg"]] + [w_[1] for w_ in wz], [pbf], mmz)
                kb.op(Aop, [pbf], [B["zs"]],
                      lambda e, p=p, ch=ch: e.activation(out=zs[0:64, ch, :], in_=p[0:64, :], func=AF.Silu))
            pend = stage2a(0)
            for fc in range(8):
                nxt = stage2a(fc + 1) if fc + 1 < 8 else None
                stage2b(fc, *pend)
                pend = nxt
            wba, wbab = self.wtile(128)
            p, pbf = nb()

            def mmb(e, p=p, wba=wba):
                for ch in range(NCK):
                    for k in range(NCH):
                        ins = e.matmul(out=p[0:64, ch * 16:(ch + 1) * 16], lhsT=xg[:, k, ch * 64:(ch + 1) * 64],
                                       rhs=wba[:, k * 16:(k + 1) * 16], start=(k == 0), stop=(k == NCH - 1))
                return ins
            kb.op("pe", [B["xg"], wbab], [pbf], mmb)
            kb.op(V, [pbf], [B["ba"]],
                  lambda e, p=p: e.tensor_copy(out=ba[0:64], in_=p[0:64, 0:NCK * 16].rearrange("p (c f) -> p c f", c=NCK)))
            kb.op(Aop, [B["ba"]], [B["beta"]],
                  lambda e: e.activation(out=beta[0:64], in_=ba[0:64, :, 0:8], func=AF.Exp, scale=-1.0))
            kb.op(V, [B["beta"]], [B["beta"]], lambda e: e.tensor_scalar_add(out=beta[0:64], in0=beta[0:64], scalar1=1.0))
            kb.op(V, [B["beta"]], [B["beta"]], lambda e: e.reciprocal(out=beta[0:64], in_=beta[0:64]))
            kb.op(V, [B["ba"], epb], [B["gtmp"]],
                  lambda e: e.tensor_tensor(out=gtmp[0:64], in0=ba[0:64, :, 8:16],
                                            in1=bc(dtb.unsqueeze(1), [64, NCK, 8]), op=ALU.add))
            kb.op(Aop, [B["gtmp"]], [B["gtmp"]], lambda e: e.activation(out=gtmp[0:64], in_=gtmp[0:64], func=AF.Exp))
            kb.op(Aop, [B["gtmp"], B["onec"]], [B["gtmp"]],
                  lambda e: e.activation(out=gtmp[0:64], in_=gtmp[0:64], func=AF.Ln, bias=onec[0:64, 0:1]))
            kb.op(V, [B["gtmp"], B["nA"]], [B["gt"]],
                  lambda e: e.tensor_tensor(out=gt[0:64], in0=gtmp[0:64], in1=bc(nA[0:64, :].unsqueeze(1), [64, NCK, 8]),
                                            op=ALU.mult))
            for ch in range(NCK):
                for (src, srcb, dst, dstb) in ((kT, B["kT"], k_tm, B["k_tm"]), (vT, B["vT"], v_tm, B["v_tm"])):
                    def tr(e, src=src, ch=ch):
                        for pr in range(4):
                            ins = e.transpose(out=tpb[0:64, pr * 128:(pr + 1) * 128], in_=src[:, pr, ch * 64:(ch + 1) * 64],
                                              identity=self.ident[:, :])
                        return ins
                    kb.op("pe", [srcb, self.identb], [tpbb], tr)
                    kb.op(Aop, [tpbb], [dstb],
                          lambda e, dst=dst, ch=ch: e.activation(out=dst[0:64, ch, :], in_=tpb[0:64, 0:512], func=AF.Copy))
            if os.environ.get("DN_NOCHUNK"):
                if gi == 0:
                    for c in range(4, 8):
                        for nn in range(16):
                            kb.op(V, [], [self.mixb[c][nn]],
                                  lambda e, c=c, nn=nn: e.memset(self.mix[:, c, nn * 128:(nn + 1) * 128], 0.0))
                continue
            def args(ch):
                return (gi * NCK + ch, ch, slice(ch * 64, (ch + 1) * 64))

            def run_pa(ch, rgen):
                ga, gb_ = chunk_pa(*args(ch), 0), chunk_pa(*args(ch), 1)
                da = db = False
                dr = rgen is None
                k = 0
                while not (da and db and dr):
                    if not da and next(ga, "PAend") == "PAend":
                        da = True
                    if not db and next(gb_, "PAend") == "PAend":
                        db = True
                    k += 1
                    if not dr:
                        if next(rgen, None) is None:
                            dr = True

            def run_until(g, marks):
                for m in g:
                    if m in marks:
                        return m
                return None
            run_pa(0, None)
            for ch in range(NCK):
                gr = chunk_rest(*args(ch))
                run_until(gr, ("Rstart",))
                if ch + 1 < NCK:
                    run_pa(ch + 1, gr)
                else:
                    for _ in gr:
                        pass

    def build(self):
        kb = self.kb
        self.load()
        for stg in self.stages:
            if stg[0] == "ffn":
                self.ffn(stg[1])
            elif stg[0] == "conv":
                self.convmix(stg[1])
            elif stg[0] == "even":
                self.evenmix(stg[1])
            elif stg[0] == "final":
                self.rmsnorm(12, final=True)
            else:
                raise ValueError(stg)
        self.store()
        nc = self.nc
        with nc.Block() as block:
            @block.tensor
            def _(e):
                kb.replay("pe", e)

            @block.scalar
            def _(e):
                kb.replay("act", e)

            @block.vector
            def _(e):
                kb.replay("dve", e)

            @block.gpsimd
            def _(e):
                kb.replay("pool", e)

            @block.sync
            def _(e):
                kb.replay("sp", e)
        self.st.close()
        return nc


def pack_cols(w, m):
    nk = w.shape[0] // 128
    blk = w[:, m * 128:(m + 1) * 128].reshape(nk, 128, 128)
    return _tile(blk.transpose(1, 0, 2).reshape(128, nk * 128))


def pack_conv(w1, w2):
    tiles = []
    for j in range(8):
        tiles.append(pack_cols(w1, j))
        tiles.append(pack_cols(w1, 8 + j))
    for c in range(8):
        tiles.append(pack_cols(w2, c))
    return tiles


def pack_cp(inp):
    cp = np.zeros((128, 2 * CPW), np.float32)
    for ci in range(2):
        o = ci * CPW
        cp[:, o:o + 16] = inp["conv_b_pw1"][ci].reshape(16, 128).T
        cp[:, o + 16:o + 264] = inp["conv_w_dw"][ci].reshape(31, 8, 128).transpose(2, 1, 0).reshape(128, 248)
        cp[:, o + 264:o + 272] = inp["conv_b_dw"][ci].reshape(8, 128).T
        cp[:, o + 272:o + 280] = inp["conv_ln_w"][ci].reshape(8, 128).T
        cp[:, o + 280:o + 288] = inp["conv_ln_b"][ci].reshape(8, 128).T
        cp[:, o + 288:o + 296] = inp["conv_b_pw2"][ci].reshape(8, 128).T
    return cp


QPERM = np.concatenate([np.concatenate([np.arange(c * 64, (c + 1) * 64), np.arange((4 + c) * 64, (5 + c) * 64)])
                        for c in range(4)])


def pack_dn(w_in):
    tiles = []
    for gi in range(T // GT):
        for fc in range(12):
            tiles.append(pack_cols(w_in[:, 768:2304], fc))
        wz = w_in[:, 2304:2816]
        for kp in range(4):
            tiles.append(_tile(np.concatenate([wz[(2 * kp) * 128:(2 * kp + 1) * 128], wz[(2 * kp + 1) * 128:(2 * kp + 2) * 128]], axis=1)))
        wba = w_in[:, 2816:2832]
        tiles.append(_tile(wba.reshape(8, 128, 16).transpose(1, 0, 2).reshape(128, 128)))
    return tiles


def pack_even(w_in, w_out):
    tiles = []
    if "att" in EVEN_PARTS:
        wq = w_in[:, 0:512][:, QPERM]
        for c in range(4):
            tiles.append(pack_cols(wq, c))
        tiles.append(pack_cols(w_in[:, 512:640], 0))
        tiles.append(pack_cols(w_in[:, 640:768], 0))
    if "dn" in EVEN_PARTS:
        tiles += pack_dn(w_in)
    wo = np.concatenate([w_out[0:512][QPERM], w_out[512:1024]], axis=0)
    for c in range(8):
        tiles.append(pack_cols(wo, c))
    return tiles


def pack_ep(inp):
    ep = np.zeros((128, 2 * EPW), np.float32)
    for e in range(2):
        o = e * EPW
        sk = inp["attn_sinks"][e]
        ep[0:64, o:o + 4] = sk[0:4][None, :]
        ep[64:128, o:o + 4] = sk[4:8][None, :]
        ep[:, o + 4:o + 52] = inp["dn_conv_w"][e].reshape(4, 12, 128).transpose(2, 1, 0).reshape(128, 48)
        ep[:, o + 52:o + 60] = inp["dn_a_log"][e][None, :]
        ep[:, o + 60:o + 68] = inp["dn_dt_bias"][e][None, :]
        ep[:, o + 68:o + 132] = inp["dn_norm_w"][e][None, :]
        ep[:, o + 132] = np.concatenate([inp["dn_norm_w"][e], inp["dn_norm_w"][e]])
    return ep


def make_cst():
    c = np.zeros((128, CSTW), np.float32)
    j = np.arange(128)[:, None]
    i = np.arange(128)[None, :]
    c[:, 0:128] = np.where(j <= i, (j - i).astype(np.float32), -1e9)
    c[:, 128:256] = np.where(j > i, (j - i - 128).astype(np.float32), -1e9)
    t = np.arange(64)[:, None]
    u = np.arange(64)[None, :]
    c[0:64, 256:320] = (t <= u).astype(np.float32)
    c[0:64, 320:384] = np.where(u >= t, 0.0, -30000.0)
    for bi, b in enumerate((1, 2, 4, 8, 16, 32)):
        same = (t // (2 * b)) == (u // (2 * b))
        lm = same & ((t % (2 * b)) >= b) & ((u % (2 * b)) < b)
        c[0:64, 448 + bi * 64:448 + (bi + 1) * 64] = lm.astype(np.float32)
        c[0:64, 960 + bi * 64:960 + (bi + 1) * 64] = -lm.astype(np.float32)
    c[:, 832:960] = ((j // 64) == (i // 64)).astype(np.float32)
    return c


def stage_tiles(stages, inp):
    tiles = []
    for stg in stages:
        if stg[0] == "ffn":
            layer, j = divmod(stg[1], 3)
            fi = 0 if j == 0 else 1
            tiles += pack_ffn(inp["ffn_w_gate"][layer, fi], inp["ffn_w_up"][layer, fi], inp["ffn_w_down"][layer, fi])
        elif stg[0] == "even":
            e = stg[1] // 2
            tiles += pack_even(inp["mix_w_in"][e], inp["mix_w_out"][e])
        elif stg[0] == "conv":
            ci = stg[1] // 2
            tiles += pack_conv(inp["conv_w_pw1"][ci], inp["conv_w_pw2"][ci])
    return tiles


def pack_nw(inp):
    nw = np.concatenate([inp["norm_w"].reshape(12, D), inp["final_norm_w"].reshape(1, D)], axis=0)
    return np.ascontiguousarray(nw.reshape(13, NCH, 128).transpose(2, 0, 1).reshape(128, 13 * NCH))


def run_stages(stages, xT_list, inp, trace=False):
    tiles = stage_tiles(stages, inp)
    wst = np.stack(tiles) if tiles else np.zeros((1, 128, 1024), np.float32)
    nc = Prog(stages, len(tiles)).build()
    nw = pack_nw(inp)
    cp = pack_cp(inp)
    ident = np.eye(128, dtype=np.float32)
    ep = pack_ep(inp)
    cst = make_cst()
    in_maps = [{"xin": np.ascontiguousarray(xT), "wst": wst, "nw": nw, "cp": cp, "ident": ident, "ep": ep, "cst": cst}
               for xT in xT_list]
    res = run_bass_kernel_spmd(nc, in_maps, core_ids=list(range(len(xT_list))), trace=trace)
    return [r["xout"] for r in res.results], res


FULL = []
for _l in range(4):
    FULL += [("ffn", 3 * _l + 0), ("even" if _l % 2 == 0 else "conv", _l), ("ffn", 3 * _l + 2)]
FULL += [("final",)]


def kernel(**inputs):
    inp = {k: np.asarray(v) for k, v in inputs.items()}
    x = inp["x"]
    xT = [np.ascontiguousarray(x[b].T) for b in range(8)]
    outs, _ = run_stages(FULL, xT, inp)
    return np.stack([o.T for o in outs]).astype(np.float32)
```
